# Optimizing a Trainium2 kernel written in Bass

```python
import math
import jax, jax.numpy as jnp
from jax import lax
import numpy as np

D_MODEL = 2048
BATCH = 8
SEQ = 2048
DEPTH = 2

N_META = 16
GRID_W = 64
BLOCK = 128
HEAD_DIM = 128
D_FF = 4 * D_MODEL
NORM_EPS = 1e-6
ROPE_THETA = 10000.0
MIX_WIDTH = D_MODEL
ATT_WIDTH = 3 * MIX_WIDTH // 4
ATT_HEADS = ATT_WIDTH // HEAD_DIM
ATT_KV_HEADS = ATT_HEADS // 3
ATT_KV_WIDTH = ATT_KV_HEADS * HEAD_DIM
S5_WIDTH = MIX_WIDTH - ATT_WIDTH
S5_GROUP = 16
S5_GROUPS = S5_WIDTH // S5_GROUP
S5_STATE = 64
EVEN_IN = ATT_WIDTH + 2 * ATT_KV_WIDTH + S5_WIDTH
RET_WIDTH = MIX_WIDTH // 2
RET_HEADS = RET_WIDTH // HEAD_DIM
ML_WIDTH = MIX_WIDTH - RET_WIDTH
ML_HEADS = ML_WIDTH // HEAD_DIM
CONV_W = 5
NEG_GATE = -1e4
ODD_IN = 4 * RET_WIDTH + 2 * ML_WIDTH + 4 * ML_HEADS
N_EVEN = (DEPTH + 1) // 2
N_ODD = DEPTH // 2

kernel_name = 'hybrid_bidir_attn_s5_retnet_mlstm'


def rms_norm(x, g):
    xf = x.astype(jnp.float32)
    y = xf * lax.rsqrt(jnp.mean(xf * xf, axis=-1, keepdims=True) + NORM_EPS)
    return (y * g.astype(jnp.float32)).astype(x.dtype)


def head_layer_norm(x, g):
    b, l, h, d = x.shape
    xf = x.astype(jnp.float32)
    xc = xf - jnp.mean(xf, axis=-1, keepdims=True)
    y = xc * lax.rsqrt(jnp.mean(xc * xc, axis=-1, keepdims=True) + NORM_EPS)
    return y.reshape(b, l, h * d) * g.astype(jnp.float32)


def rope_freqs(dim):
    return ROPE_THETA ** (-jnp.arange(dim // 2, dtype=jnp.float32) / (dim // 2))


def rope(x, ang):
    c = jnp.cos(ang)[None, :, None, :]
    s = jnp.sin(ang)[None, :, None, :]
    x1, x2 = jnp.split(x.astype(jnp.float32), 2, axis=-1)
    return jnp.concatenate([x1 * c - x2 * s, x1 * s + x2 * c], axis=-1).astype(x.dtype)


def axial_rope(x, ang_row, ang_col):
    half = x.shape[-1] // 2
    return jnp.concatenate([rope(x[..., :half], ang_row), rope(x[..., half:], ang_col)], axis=-1)


def grid_positions(n_tok):
    rows = n_tok // GRID_W
    row = jnp.concatenate([-jnp.ones((N_META,), jnp.float32),
                           jnp.repeat(jnp.arange(rows, dtype=jnp.float32), GRID_W)])
    col = jnp.concatenate([jnp.arange(N_META, dtype=jnp.float32),
                           jnp.tile(jnp.arange(GRID_W, dtype=jnp.float32), rows)])
    return row, col


def pad_front(a, n):
    return jnp.pad(a, [(0, 0), (n, 0)] + [(0, 0)] * (a.ndim - 2))


def flip_t(a):
    return jnp.flip(a, axis=1)


def sq_relu_mlp(x, w1, w2):
    return jnp.square(jax.nn.relu(x @ w1)) @ w2


def grid_attention(q, k, v):
    b, l, h, d = q.shape
    kvh = k.shape[2]
    grp = h // kvh
    pad = (-l) % BLOCK
    nb = (l + pad) // BLOCK
    qb = pad_front(q, pad).reshape(b, nb, BLOCK, kvh, grp, d).transpose(1, 0, 2, 3, 4, 5)
    scale = d ** -0.5

    def attend(q_blk):
        s = jnp.einsum('bqkgd,bskd->bkgqs', q_blk, k).astype(jnp.float32) * scale
        p = jax.nn.softmax(s, axis=-1).astype(v.dtype)
        return jnp.einsum('bkgqs,bskd->bqkgd', p, v)

    out = lax.map(attend, qb)
    return out.transpose(1, 0, 2, 3, 4, 5).reshape(b, nb * BLOCK, h * d)[:, pad:]


def _complex_scan_combine(left, right):
    a1r, a1i, b1r, b1i = left
    a2r, a2i, b2r, b2i = right
    return (a1r * a2r - a1i * a2i,
            a1r * a2i + a1i * a2r,
            a2r * b1r - a2i * b1i + b2r,
            a2r * b1i + a2i * b1r + b2i)


def s5_direction(u, lam_re, lam_im, log_dt, b_re, b_im, c_re, c_im):
    lr = jnp.minimum(lam_re, -1e-4)
    li = lam_im
    dt = jnp.exp(log_dt)[:, None]
    er = jnp.exp(lr * dt)
    abar_re = er * jnp.cos(li * dt)
    abar_im = er * jnp.sin(li * dt)
    nr = abar_re - 1.0
    den = lr * lr + li * li
    coef_re = (nr * lr + abar_im * li) / den
    coef_im = (abar_im * lr - nr * li) / den
    bbar_re = coef_re[..., None] * b_re - coef_im[..., None] * b_im
    bbar_im = coef_re[..., None] * b_im + coef_im[..., None] * b_re
    bu_re = jnp.einsum('blgh,gph->blgp', u, bbar_re)
    bu_im = jnp.einsum('blgh,gph->blgp', u, bbar_im)
    n_pos = u.shape[1]
    a_re = jnp.broadcast_to(abar_re, (1, n_pos) + abar_re.shape)
    a_im = jnp.broadcast_to(abar_im, (1, n_pos) + abar_im.shape)
    _, _, x_re, x_im = lax.associative_scan(_complex_scan_combine, (a_re, a_im, bu_re, bu_im), axis=1)
    return jnp.einsum('blgp,ghp->blgh', x_re, c_re) - jnp.einsum('blgp,ghp->blgh', x_im, c_im)


def s5_mixer(u, lam_re, lam_im, log_dt, b_re, b_im, c_re, c_im, d_skip, glu_w, glu_b):
    b, l, _ = u.shape
    f32 = jnp.float32
    uf = u.astype(f32).reshape(b, l, S5_GROUPS, S5_GROUP)
    lam_re, lam_im, log_dt = lam_re.astype(f32), lam_im.astype(f32), log_dt.astype(f32)
    b_re, b_im, c_re, c_im = b_re.astype(f32), b_im.astype(f32), c_re.astype(f32), c_im.astype(f32)
    y_fw = s5_direction(uf, lam_re[0], lam_im[0], log_dt[0], b_re[0], b_im[0], c_re[0], c_im[0])
    y_bw = flip_t(s5_direction(flip_t(uf), lam_re[1], lam_im[1], log_dt[1], b_re[1], b_im[1], c_re[1], c_im[1]))
    y = y_fw + y_bw + d_skip.astype(f32) * uf
    y = jax.nn.gelu(y.reshape(b, l, S5_WIDTH))
    return y * jax.nn.sigmoid(y @ glu_w.astype(f32) + glu_b.astype(f32))


def retention_direction(q, k, v, log_gamma, strict):
    b, lp, h, dk = q.shape
    dv = v.shape[-1]
    nc = lp // BLOCK
    qc = q.reshape(b, nc, BLOCK, h, dk)
    kc = k.reshape(b, nc, BLOCK, h, dk)
    vc = v.reshape(b, nc, BLOCK, h, dv)
    idx = jnp.arange(BLOCK, dtype=jnp.float32)
    diff = idx[:, None] - idx[None, :]
    mask = (diff > 0) if strict else (diff >= 0)
    decay = jnp.where(mask, jnp.exp(jnp.where(mask, diff, 0.0)[None] * log_gamma[:, None, None]), 0.0)
    scores = jnp.einsum('bnqhd,bnshd->bnhqs', qc, kc) * decay
    intra = jnp.einsum('bnhqs,bnshe->bnqhe', scores, vc)
    zeta = jnp.exp((BLOCK - 1 - idx)[:, None] * log_gamma[None, :])
    kv = jnp.einsum('bnshd,bnshe->bnhde', kc * zeta[:, :, None], vc)
    g_chunk = jnp.exp(BLOCK * log_gamma)[None, :, None, None]

    def step(r, kv_c):
        return g_chunk * r + kv_c, r

    _, r_prev = lax.scan(step, jnp.zeros((b, h, dk, dv), kv.dtype), kv.transpose(1, 0, 2, 3, 4))
    xi = jnp.exp((idx + 1.0)[:, None] * log_gamma[None, :])
    inter = jnp.einsum('bnqhd,nbhde->bnqhe', qc, r_prev) * xi[None, None, :, :, None]
    return (intra + inter).reshape(b, lp, h, dv)


def mlstm_direction(q, k, v, log_i, log_f):
    b, lp, h, d = q.shape
    nc = lp // BLOCK
    f32 = jnp.float32
    qc = q.reshape(b, nc, BLOCK, h, d)
    kc = k.reshape(b, nc, BLOCK, h, d)
    vc = v.reshape(b, nc, BLOCK, h, d)
    li = log_i.reshape(b, nc, BLOCK, h).transpose(0, 1, 3, 2)
    bt = jnp.cumsum(log_f.reshape(b, nc, BLOCK, h).transpose(0, 1, 3, 2), axis=-1)
    lower = jnp.tril(jnp.ones((BLOCK, BLOCK), bool))
    dlog = jnp.where(lower, bt[..., :, None] - bt[..., None, :] + li[..., None, :], -jnp.inf)
    a = bt[..., -1:] - bt + li
    m_loc = jnp.max(a, axis=-1)
    w = jnp.exp(a - m_loc[..., None])
    c_loc = jnp.einsum('bnhs,bnshd,bnshe->bnhde', w, kc, vc)
    n_loc = jnp.einsum('bnhs,bnshd->bnhd', w, kc)

    def step(carry, inp):
        c_s, n_s, m_s = carry
        b_last, m_c, c_c, n_c = inp
        m_new = jnp.maximum(b_last + m_s, m_c)
        f_prev = jnp.exp(b_last + m_s - m_new)
        f_loc = jnp.exp(m_c - m_new)
        c_new = f_prev[..., None, None] * c_s + f_loc[..., None, None] * c_c
        n_new = f_prev[..., None] * n_s + f_loc[..., None] * n_c
        return (c_new, n_new, m_new), (c_s, n_s, m_s)

    init = (jnp.zeros((b, h, d, d), c_loc.dtype), jnp.zeros((b, h, d), n_loc.dtype), jnp.zeros((b, h), f32))
    xs = (bt[..., -1].transpose(1, 0, 2), m_loc.transpose(1, 0, 2),
          c_loc.transpose(1, 0, 2, 3, 4), n_loc.transpose(1, 0, 2, 3))
    _, (c_prev, n_prev, m_prev) = lax.scan(step, init, xs)
    c_prev = c_prev.transpose(1, 0, 2, 3, 4)
    n_prev = n_prev.transpose(1, 0, 2, 3)
    m_prev = m_prev.transpose(1, 0, 2)
    g = bt + m_prev[..., None]
    m_t = jnp.maximum(g, jnp.max(dlog, axis=-1))
    s = jnp.einsum('bnqhd,bnshd->bnhqs', qc, kc) * jnp.exp(dlog - m_t[..., None])
    w_inter = jnp.exp(g - m_t)
    num = (jnp.einsum('bnhqs,bnshe->bnhqe', s, vc)
           + w_inter[..., None] * jnp.einsum('bnqhd,bnhde->bnhqe', qc, c_prev))
    den = jnp.sum(s, axis=-1) + w_inter * jnp.einsum('bnqhd,bnhd->bnhq', qc, n_prev)
    out = num / jnp.maximum(jnp.abs(den), jnp.exp(-m_t))[..., None]
    return out.transpose(0, 1, 3, 2, 4).reshape(b, lp, h, d)


def centred_dwconv(x, w, bias):
    out = lax.conv_general_dilated(x, w[:, None, :].astype(x.dtype), window_strides=(1,),
                                   padding=[(CONV_W // 2, CONV_W // 2)],
                                   dimension_numbers=('NWC', 'WIO', 'NWC'),
                                   feature_group_count=x.shape[-1])
    return out + bias.astype(x.dtype)


def even_mixer(h, w_in, w_out, q_norm, k_norm, lam_re, lam_im, log_dt, b_re, b_im, c_re, c_im,
               d_skip, glu_w, glu_b, ang_row, ang_col):
    b, l, _ = h.shape
    q, k, v, u = jnp.split(h @ w_in, [ATT_WIDTH, ATT_WIDTH + ATT_KV_WIDTH, ATT_WIDTH + 2 * ATT_KV_WIDTH], axis=-1)
    q = axial_rope(rms_norm(q.reshape(b, l, ATT_HEADS, HEAD_DIM), q_norm), ang_row, ang_col)
    k = axial_rope(rms_norm(k.reshape(b, l, ATT_KV_HEADS, HEAD_DIM), k_norm), ang_row, ang_col)
    att = grid_attention(q, k, v.reshape(b, l, ATT_KV_HEADS, HEAD_DIM))
    ssm = s5_mixer(u, lam_re, lam_im, log_dt, b_re, b_im, c_re, c_im, d_skip, glu_w, glu_b)
    return jnp.concatenate([att.astype(jnp.float32), ssm], axis=-1).astype(h.dtype) @ w_out


def odd_mixer(h, w_in, w_out, ret_log_decay, ret_norm, conv_w, conv_b, wq, wk, wv, gate_b, ml_norm, ang_lin):
    b, l, _ = h.shape
    f32 = jnp.float32
    pad = (-l) % BLOCK
    splits = [RET_WIDTH, 2 * RET_WIDTH, 3 * RET_WIDTH, 4 * RET_WIDTH,
              4 * RET_WIDTH + ML_WIDTH, 4 * RET_WIDTH + 2 * ML_WIDTH]
    rq, rk, rv, rg, mu, mo, gates = jnp.split(h @ w_in, splits, axis=-1)

    rq = pad_front(rope(rq.reshape(b, l, RET_HEADS, HEAD_DIM), ang_lin) * HEAD_DIM ** -0.5, pad)
    rk = pad_front(rope(rk.reshape(b, l, RET_HEADS, HEAD_DIM), ang_lin), pad)
    rv = pad_front(rv.reshape(b, l, RET_HEADS, HEAD_DIM), pad)
    log_gamma = -jnp.abs(ret_log_decay.astype(f32))
    ret = (retention_direction(rq, rk, rv, log_gamma[0], False)
           + flip_t(retention_direction(flip_t(rq), flip_t(rk), flip_t(rv), log_gamma[1], True)))
    ret = head_layer_norm(ret[:, pad:], ret_norm) * jax.nn.silu(rg.astype(f32))

    uc = jax.nn.silu(centred_dwconv(mu, conv_w, conv_b))
    mq = pad_front(jnp.einsum('blhd,hde->blhe', uc.reshape(b, l, ML_HEADS, HEAD_DIM), wq), pad)
    mk = pad_front(jnp.einsum('blhd,hde->blhe', uc.reshape(b, l, ML_HEADS, HEAD_DIM), wk) * HEAD_DIM ** -0.5, pad)
    mv = pad_front(jnp.einsum('blhd,hde->blhe', mu.reshape(b, l, ML_HEADS, HEAD_DIM), wv), pad)
    g = gates.astype(f32).reshape(b, l, 4, ML_HEADS) + gate_b.astype(f32)
    valid = (jnp.arange(l + pad) >= pad)[None, :, None, None]
    log_i = jnp.where(valid, pad_front(g[:, :, 0::2], pad), NEG_GATE)
    log_f = jnp.where(valid, pad_front(jax.nn.log_sigmoid(g[:, :, 1::2]), pad), 0.0)
    hm = (mlstm_direction(mq, mk, mv, log_i[:, :, 0], log_f[:, :, 0])
          + flip_t(mlstm_direction(flip_t(mq), flip_t(mk), flip_t(mv),
                                   flip_t(log_i[:, :, 1]), flip_t(log_f[:, :, 1]))))
    hm = head_layer_norm(hm[:, pad:], ml_norm) * jax.nn.sigmoid(mo.astype(f32))
    return jnp.concatenate([ret, hm], axis=-1).astype(h.dtype) @ w_out


def setup_inputs(seed: int = 0) -> dict:
    key = jax.random.key(seed)
    ks = iter(jax.random.split(key, 32))
    f32 = jnp.float32

    def nrm(shape, scale):
        return scale * jax.random.normal(next(ks), shape, f32)

    lam_im_base = jnp.pi * jnp.arange(S5_STATE, dtype=f32)
    ret_base = jnp.log(1.0 - 2.0 ** (-5.0 - jnp.arange(RET_HEADS, dtype=f32)))
    forget_base = jnp.linspace(3.0, 6.0, ML_HEADS, dtype=f32)
    gate_rows = jnp.array([0.0, 1.0, 0.0, 1.0], f32)
    return {
        'x': nrm((BATCH, SEQ, D_MODEL), 1.0),
        'meta_tokens': nrm((N_META, D_MODEL), 1.0),
        'norm_gains': 1.0 + nrm((DEPTH, 4, D_MODEL), 0.02),
        'mlp_w1': nrm((DEPTH, D_MODEL, D_FF), D_MODEL ** -0.5),
        'mlp_w2': nrm((DEPTH, D_FF, D_MODEL), D_FF ** -0.5),
        'even_w_in': nrm((N_EVEN, D_MODEL, EVEN_IN), D_MODEL ** -0.5),
        'even_w_out': nrm((N_EVEN, MIX_WIDTH, D_MODEL), MIX_WIDTH ** -0.5),
        'att_q_norm': 1.0 + nrm((N_EVEN, HEAD_DIM), 0.02),
        'att_k_norm': 1.0 + nrm((N_EVEN, HEAD_DIM), 0.02),
        's5_lam_re': -0.5 + nrm((N_EVEN, 2, S5_GROUPS, S5_STATE), 0.01),
        's5_lam_im': lam_im_base + nrm((N_EVEN, 2, S5_GROUPS, S5_STATE), 0.01),
        's5_log_dt': jax.random.uniform(next(ks), (N_EVEN, 2, S5_GROUPS), f32,
                                        minval=math.log(1e-3), maxval=math.log(1e-1)),
        's5_b_re': nrm((N_EVEN, 2, S5_GROUPS, S5_STATE, S5_GROUP), (2 * S5_GROUP) ** -0.5),
        's5_b_im': nrm((N_EVEN, 2, S5_GROUPS, S5_STATE, S5_GROUP), (2 * S5_GROUP) ** -0.5),
        's5_c_re': nrm((N_EVEN, 2, S5_GROUPS, S5_GROUP, S5_STATE), S5_STATE ** -0.5),
        's5_c_im': nrm((N_EVEN, 2, S5_GROUPS, S5_GROUP, S5_STATE), S5_STATE ** -0.5),
        's5_d': nrm((N_EVEN, S5_GROUPS, S5_GROUP), 1.0),
        's5_glu_w': nrm((N_EVEN, S5_WIDTH, S5_WIDTH), S5_WIDTH ** -0.5),
        's5_glu_b': nrm((N_EVEN, S5_WIDTH), 0.02),
        'odd_w_in': nrm((N_ODD, D_MODEL, ODD_IN), D_MODEL ** -0.5),
        'odd_w_out': nrm((N_ODD, MIX_WIDTH, D_MODEL), MIX_WIDTH ** -0.5),
        'ret_log_decay': ret_base * (1.0 + nrm((N_ODD, 2, RET_HEADS), 0.05)),
        'ret_norm': 1.0 + nrm((N_ODD, RET_WIDTH), 0.02),
        'ml_conv_w': nrm((N_ODD, CONV_W, ML_WIDTH), CONV_W ** -0.5),
        'ml_conv_b': nrm((N_ODD, ML_WIDTH), 0.02),
        'ml_wq': nrm((N_ODD, ML_HEADS, HEAD_DIM, HEAD_DIM), HEAD_DIM ** -0.5),
        'ml_wk': nrm((N_ODD, ML_HEADS, HEAD_DIM, HEAD_DIM), HEAD_DIM ** -0.5),
        'ml_wv': nrm((N_ODD, ML_HEADS, HEAD_DIM, HEAD_DIM), HEAD_DIM ** -0.5),
        'ml_gate_b': nrm((N_ODD, 4, ML_HEADS), 0.1) + gate_rows[None, :, None] * forget_base[None, None, :],
        'ml_norm': 1.0 + nrm((N_ODD, ML_WIDTH), 0.02),
    }


def reference(x, meta_tokens, norm_gains, mlp_w1, mlp_w2, even_w_in, even_w_out, att_q_norm, att_k_norm,
              s5_lam_re, s5_lam_im, s5_log_dt, s5_b_re, s5_b_im, s5_c_re, s5_c_im, s5_d, s5_glu_w, s5_glu_b,
              odd_w_in, odd_w_out, ret_log_decay, ret_norm, ml_conv_w, ml_conv_b, ml_wq, ml_wk, ml_wv,
              ml_gate_b, ml_norm):
    b, n_tok, d_model = x.shape
    h = jnp.concatenate([jnp.broadcast_to(meta_tokens.astype(x.dtype)[None], (b, N_META, d_model)), x], axis=1)
    l = h.shape[1]
    row, col = grid_positions(n_tok)
    f_axis = rope_freqs(HEAD_DIM // 2)
    ang_row = row[:, None] * f_axis[None, :]
    ang_col = col[:, None] * f_axis[None, :]
    ang_lin = jnp.arange(l, dtype=jnp.float32)[:, None] * rope_freqs(HEAD_DIM)[None, :]

    for i in range(DEPTH):
        j = i // 2
        hn = rms_norm(h, norm_gains[i, 0])
        if i % 2 == 0:
            mix = even_mixer(hn, even_w_in[j], even_w_out[j], att_q_norm[j], att_k_norm[j],
                             s5_lam_re[j], s5_lam_im[j], s5_log_dt[j], s5_b_re[j], s5_b_im[j],
                             s5_c_re[j], s5_c_im[j], s5_d[j], s5_glu_w[j], s5_glu_b[j], ang_row, ang_col)
        else:
            mix = odd_mixer(hn, odd_w_in[j], odd_w_out[j], ret_log_decay[j], ret_norm[j], ml_conv_w[j],
                            ml_conv_b[j], ml_wq[j], ml_wk[j], ml_wv[j], ml_gate_b[j], ml_norm[j], ang_lin)
        h = h + rms_norm(mix, norm_gains[i, 1])
        h = h + rms_norm(sq_relu_mlp(rms_norm(h, norm_gains[i, 2]), mlp_w1[i], mlp_w2[i]), norm_gains[i, 3])
    return h[:, N_META:]
```

```python
import numpy as np
import concourse.bass as bass
import concourse.mybir as mybir

F32 = mybir.dt.float32
BF16 = mybir.dt.bfloat16
ALU = mybir.AluOpType
AF = mybir.ActivationFunctionType
AX = mybir.AxisListType

EPOCH = 30000
NSLOT = 8


class _Op(object):
    __slots__ = ("fn", "waits", "signal", "stream", "idx", "is_dma")


class Prog(object):
    ENGS = ("pe", "act", "dve", "pool", "sp")

    def __init__(self, nc):
        self.nc = nc
        self.ops = {e: [] for e in self.ENGS}
        self.known = {e: {} for e in self.ENGS}
        self.state = {}
        self.tkeys = {}
        self.inherit = {}
        self.cnt = {}
        self.streamops = {}
        self.dma_rr = {"sp": 0, "pool": 0, "act": 0}
        self.regions = []
        self.top = 17408
        self.n_sb = 0
        self.dram_out_streams = set()

    def sb(self, name, shape, dtype, keep=False):
        esz = 2 if dtype == BF16 else 4
        nbytes = int(np.prod(shape[1:])) * esz
        nbytes = (nbytes + 63) // 64 * 64
        lo = self.top
        hi = lo + nbytes
        assert hi <= 229312, ("SBUF overflow", name, hi)
        self.top = hi
        self.n_sb += 1
        uname = "%s_%d" % (name, self.n_sb)
        t = self.nc.alloc_sbuf_tensor_at(uname, list(shape), dtype, offset=lo)
        inh = {}
        for (l2, h2, n2) in self.regions:
            if l2 < hi and lo < h2:
                for k in self.tkeys.get(n2, ()):
                    w, r = self.state[k]
                    if w is not None:
                        if inh.get(w[0], -1) < w[1]:
                            inh[w[0]] = w[1]
                    for s, i in r.items():
                        if inh.get(s, -1) < i:
                            inh[s] = i
                for s, i in self.inherit.get(n2, {}).items():
                    if inh.get(s, -1) < i:
                        inh[s] = i
        self.regions = [(l2, h2, n2) for (l2, h2, n2) in self.regions
                        if not (l2 >= lo and h2 <= hi)]
        self.regions.append((lo, hi, uname))
        self.inherit[uname] = inh
        return T(t, uname)

    def mark(self):
        return self.top

    def release(self, m):
        self.top = m

    def _deps(self, stream, reads, writes, same_raw):
        deps = {}

        def add(s, i):
            if deps.get(s, -1) < i:
                deps[s] = i

        for k in reads:
            st = self._st(k)
            w = st[0]
            if w is not None:
                if w[0] != stream or same_raw:
                    add(w[0], w[1])
            else:
                for s, i in st[1].items():
                    pass
        for k in writes:
            st = self._st(k)
            w = st[0]
            if w is not None and w[0] != stream:
                add(w[0], w[1])
            for s, i in st[1].items():
                if s != stream:
                    add(s, i)
        return deps

    def _st(self, k):
        st = self.state.get(k)
        if st is None:
            tn = k[0]
            st = [None, dict(self.inherit.get(tn, {}))]
            self.state[k] = st
            self.tkeys.setdefault(tn, set()).add(k)
        return st

    def _commit(self, stream, idx, reads, writes):
        for k in reads:
            st = self._st(k)
            if st[1].get(stream, -1) < idx:
                st[1][stream] = idx
        for k in writes:
            st = self._st(k)
            st[0] = (stream, idx)
            st[1] = {}

    def op(self, eng, fn, reads=(), writes=()):
        stream = eng
        idx = self.cnt.get(stream, 0)
        self.cnt[stream] = idx + 1
        deps = self._deps(stream, reads, writes, same_raw=(eng != "pe"))
        o = _Op()
        o.fn = fn
        o.stream = stream
        o.idx = idx
        o.signal = False
        o.is_dma = False
        o.waits = self._prune(eng, deps)
        self.ops[eng].append(o)
        self.streamops.setdefault(stream, []).append(o)
        self._commit(stream, idx, reads, writes)
        return o

    def dma(self, q, fn, reads=(), writes=(), is_out=False, grp="m"):
        nsl = NSLOT if grp == "m" else 4
        rk = q + grp
        slot = self.dma_rr.get(rk, 0) % nsl
        self.dma_rr[rk] = self.dma_rr.get(rk, 0) + 1
        stream = "dq:%s:%s%d" % (q, grp, slot)
        idx = self.cnt.get(stream, 0)
        self.cnt[stream] = idx + 1
        deps = self._deps(stream, reads, writes, same_raw=True)
        if idx > 0:
            if deps.get(stream, -1) < idx - 1:
                deps[stream] = idx - 1
        o = _Op()
        o.fn = fn
        o.stream = stream
        o.idx = idx
        o.signal = True
        o.is_dma = True
        o.waits = self._prune(q, deps)
        self.ops[q].append(o)
        self.streamops.setdefault(stream, []).append(o)
        self._commit(stream, idx, reads, writes)
        if is_out:
            self.dram_out_streams.add(stream)
        return o

    def _prune(self, issuer, deps):
        kn = self.known[issuer]
        out = []
        for s, i in deps.items():
            if kn.get(s, -1) < i:
                kn[s] = i
                out.append((s, i))
        return out

    def finish(self):
        deps = {}
        for s, n in self.cnt.items():
            if s.startswith("dq:"):
                deps[s] = n - 1
        o = _Op()
        o.fn = None
        o.stream = None
        o.idx = -1
        o.signal = False
        o.is_dma = False
        o.waits = self._prune("sp", deps)
        self.ops["sp"].append(o)

    def emit(self, stack):
        nc = self.nc
        for e in self.ENGS:
            for o in self.ops[e]:
                for (s, i) in o.waits:
                    self.streamops[s][i].signal = True
        semval = {}
        sems = {}
        for s, lst in self.streamops.items():
            if s.startswith("dq:"):
                sem = stack.enter_context(nc.semaphore("s_" + s.replace(":", "_")))
                sems[s] = [sem]
                for o in lst:
                    semval[(s, o.idx)] = (sem, 16 * (o.idx + 1))
            else:
                c = 0
                sems[s] = []
                for o in lst:
                    if o.signal:
                        ep = c // EPOCH
                        if ep >= len(sems[s]):
                            sems[s].append(stack.enter_context(
                                nc.semaphore("s_%s_%d" % (s, ep))))
                        semval[(s, o.idx)] = (sems[s][ep], c % EPOCH + 1)
                        c += 1
        block = stack.enter_context(nc.Block())
        regs = {"pe": block.tensor, "act": block.scalar, "dve": block.vector,
                "pool": block.gpsimd, "sp": block.sync}
        for e in self.ENGS:
            lst = self.ops[e]
            if not lst:
                continue

            def body(eng, lst=lst):
                for o in lst:
                    for (s, i) in o.waits:
                        sem, val = semval[(s, i)]
                        eng.wait_ge(sem, val)
                    if o.fn is None:
                        continue
                    inst = o.fn(eng)
                    if o.signal:
                        sem, val = semval[(o.stream, o.idx)]
                        inst.then_inc(sem, 16 if o.is_dma else 1)
            regs[e](body)

    def stats(self):
        return {e: len(self.ops[e]) for e in self.ENGS}


class T(object):
    def __init__(self, h, name):
        self.h = h
        self.name = name

    def k(self, *sub):
        return (self.name,) + tuple(sub)

    def __getitem__(self, idx):
        return self.h[idx]

import math
import numpy as np
from contextlib import ExitStack
import concourse.bass as bass
import concourse.mybir as mybir
from concourse.bass_utils import run_bass_kernel_spmd

D = 2048
L = 2064
NM = 16
DFF = 8192
EPS = 1e-6
TT = [(0, 16)] + [(16 + 128 * i, 128) for i in range(16)]
TB = [(0, 16, [0])] + [(16 + 512 * j, 512, [1 + 4 * j + m for m in range(4)]) for j in range(4)]
EVEN_IN = 3072
ODD_IN = 6176
TWO_PI = 2.0 * math.pi


class K(object):
    pass


def build(stop_after=99, dbg=False):
    nc = bass.Bass("TRN2", target_bir_lowering=False)
    st = ExitStack()
    P = Prog(nc)
    din = {}

    def inp(name, shape, dt=F32):
        h = nc.dram_tensor(name, list(shape), dt, kind="ExternalInput").ap()
        din[name] = T(h, name)
        return din[name]

    def scratch(name, shape, dt):
        h = nc.dram_tensor(name, list(shape), dt).ap()
        return T(h, name)

    x = inp("x", [2048, D])
    meta = inp("meta", [NM, D])
    gains = inp("gains", [8, D])
    w1 = [inp("w1_%d" % i, [D, DFF]) for i in range(2)]
    w2 = [inp("w2_%d" % i, [DFF, D]) for i in range(2)]
    ewin = inp("ewin", [D, EVEN_IN])
    ewout = inp("ewout", [D, D])
    owin = inp("owin", [D, ODD_IN])
    owout = inp("owout", [D, D])
    c_ident = inp("c_ident", [128, 128])
    c_rope_e = inp("c_rope_e", [128, 17, 2, 64])
    qkg = inp("qkg", [2, 128])
    out = nc.dram_tensor("out", [2048, D], F32, kind="ExternalOutput").ap()
    OUT = T(out, "out")
    s5p = inp("s5p", [128, 3, 32])
    s5b = inp("s5b", [128, 2, 32, 16])
    s5c = inp("s5c", [128, 2, 32, 16])
    s5d = inp("s5d", [128, 4])
    s5gw = inp("s5gw", [512, 512])
    s5gb = inp("s5gb", [128, 4])
    c_tau = inp("c_tau", [128, 2, L])

    c_rope_o = inp("c_rope_o", [128, 17, 4, 64])
    c_ret = inp("c_ret", [128, 8, 128])
    c_col = inp("c_col", [128, 4])
    ret_ld = inp("ret_ld", [1, 16])
    ret_norm = inp("ret_norm", [1, 1024])
    ml_gb = inp("ml_gb", [8, 4])
    c_rows = inp("c_rows", [8, 4, L])
    ml_cw = inp("ml_cw", [128, 8, 5])
    ml_cb = inp("ml_cb", [128, 8])
    ml_wq = inp("ml_wq", [8, 128, 128])
    ml_wk = inp("ml_wk", [8, 128, 128])
    ml_wv = inp("ml_wv", [8, 128, 128])
    ml_norm = inp("ml_norm", [1, 1024])
    c_mmask = inp("c_mmask", [128, 2, 128])
    pd = scratch("pd", [48, 128, 17, 128], BF16)
    wtd = {}
    wtd["w1_0"] = scratch("wt_w1_0", [16, 128, 16 * 512], BF16)
    wtd["w2_0"] = scratch("wt_w2_0", [16, 128, 16 * 512], BF16)
    wtd["w1_1"] = scratch("wt_w1_1", [16, 128, 16 * 512], BF16)
    wtd["w2_1"] = scratch("wt_w2_1", [16, 128, 16 * 512], BF16)
    wtd["owin"] = scratch("wt_owin", [12, 128, 16 * 512], BF16)
    wtd["ewin"] = scratch("wt_ewin", [6, 128, 16 * 512], BF16)
    bgq = []
    hres = scratch("hres", [L, D], F32)
    mixd = scratch("mixd", [D, L], BF16)
    wb = {}
    for nm, shp in (("ewin", [D, EVEN_IN]), ("ewout", [D, D]), ("w1_0", [D, DFF]), ("w2_0", [DFF, D]),
                    ("owin", [D, ODD_IN]), ("owout", [D, D]), ("w1_1", [D, DFF]), ("w2_1", [DFF, D])):
        wb[nm] = scratch("wb_" + nm, shp, BF16)

    B = []
    for b in range(8):
        h = st.enter_context(nc.psum_tensor("bank%d" % b, [128, 512], F32))
        B.append(T(h, "bank%d" % b))

    def bf(bank):
        return bank.h[:, :].bitcast(BF16)

    castq = []
    cast_pending = {}

    def convert(nm):
        src = din[nm]
        dst = wb[nm]
        rows = src.h.shape[0]
        cols = src.h.shape[1]
        rstep = max(128, (1 << 20) // cols // 128 * 128)
        cast_pending[nm] = 0
        for r0 in range(0, rows, rstep):
            r1 = min(rows, r0 + rstep)

            def f(gate, r0=r0, r1=r1, nm=nm):
                P.dma("pool", lambda e: e.dma_start(out=dst.h[r0:r1, :], in_=src.h[r0:r1, :]),
                      reads=[src.k()] + list(gate), writes=[dst.k(r0)])
                cast_pending[nm] -= 1
            castq.append((nm, f))
            cast_pending[nm] += 1
        dst.rkeys = [dst.k(r0) for r0 in range(0, rows, rstep)]
        dst.rstep = rstep

    def cast_pump(n=1, gate=()):
        for _ in range(n):
            if castq:
                castq.pop(0)[1](gate)

    def need(nm):
        while cast_pending.get(nm, 0) > 0:
            cast_pump(1)

    def retile(nm):
        src = wb[nm]
        dst = wtd[nm]
        if nm.startswith("w2"):
            lst = [(cb * 4 + fq, fq * 2048, 2048, cb * 512) for cb in range(4) for fq in range(4)]
        elif nm.startswith("w1"):
            lst = [(fc, 0, 2048, fc * 512) for fc in range(16)]
        else:
            lst = [(cbk, 0, 2048, cbk * 512) for cbk in range(6 if nm == "ewin" else 12)]
        for (ci, r0, nr, c0) in lst:
            def f(ci=ci, r0=r0, nr=nr, c0=c0):
                need(nm)
                rk_ = [src.k(rr) for rr in range(r0 // src.rstep * src.rstep, r0 + nr, src.rstep)]
                P.dma("sp", lambda e: e.dma_start(out=dst.h[ci].rearrange("p (kt c) -> p kt c", c=512),
                                                  in_=src.h[r0:r0 + nr, c0:c0 + 512].rearrange("(kt p) c -> p kt c", p=128)),
                      reads=rk_, writes=[dst.k(ci)], grp="bg")
            bgq.append(f)

    bgc = {"n": 0}

    def pump(n=1):
        for _ in range(n):
            if bgq:
                bgq.pop(0)()
                bgc["n"] += 1

    def pump_until(kk):
        while bgc["n"] < kk and bgq:
            pump(1)

    ident = P.sb("ident", [128, 128], BF16)
    identf = P.sb("identf", [128, 128], F32)
    ones = P.sb("ones", [128, 128], BF16)
    cst = P.sb("cst", [128, 8], F32)
    P.dma("sp", lambda e: e.dma_start(out=identf[:, :], in_=c_ident[:, :]), reads=[c_ident.k()], writes=[identf.k()])
    P.op("dve", lambda e: e.tensor_copy(out=ident[:, :], in_=identf[:, :]), reads=[identf.k()], writes=[ident.k()])
    P.op("dve", lambda e: e.memset(ones[:, :], 1.0), writes=[ones.k()])
    P.op("dve", lambda e: e.memset(cst[:, 0:1], EPS), writes=[cst.k()])
    P.op("dve", lambda e: e.memset(cst[:, 1:2], -math.pi), writes=[cst.k()])
    P.op("dve", lambda e: e.memset(cst[:, 2:3], 0.0), writes=[cst.k()])
    P.op("dve", lambda e: e.memset(cst[:, 3:4], -2.0 * math.pi), writes=[cst.k()])
    P.op("dve", lambda e: e.memset(cst[:, 4:5], math.pi / 2), writes=[cst.k()])
    base_mark = P.mark()

    state = K()
    state.layer0 = True

    def h_src(tt):
        s0, p = TT[tt]
        if state.layer0:
            if tt == 0:
                return meta.h[0:16, :], meta.k()
            return x.h[s0 - 16:s0 - 16 + p, :], x.k()
        return hres.h[s0:s0 + p, :], hres.k(tt)

    rr = {"bank": 0}

    def rstd_from_ss(ss_t, ss_ap, rs_t, rs_ap, n):
        P.op("act", lambda e: e.activation(out=rs_ap, in_=ss_ap, func=AF.Sqrt, bias=cst[:ss_ap.shape[0], 0:1], scale=1.0 / n),
             reads=[ss_t.k(), cst.k()], writes=[rs_t.k()])
        P.op("dve", lambda e: e.reciprocal(out=rs_ap, in_=rs_ap), reads=[rs_t.k()], writes=[rs_t.k()])

    def phase_norm_T(gi, dstT, tiles, tok_off=0, banks=(0, 1), bufs=None, koff=0):
        m = P.mark()
        if bufs is None:
            bufs = K()
            bufs.gb = P.sb("gb", [128, D], F32)
            bufs.ht = [P.sb("ht", [128, D], F32) for _ in range(2)]
            bufs.hb = [P.sb("hb", [128, D], BF16) for _ in range(2)]
            bufs.junk = P.sb("junk", [128, D], BF16)
            bufs.ss = [P.sb("ss", [128, 2], F32) for _ in range(2)]
        gb, ht, hb, junk, ss = bufs.gb, bufs.ht, bufs.hb, bufs.junk, bufs.ss
        P.dma("sp", lambda e: e.dma_start(out=gb[:, :], in_=gains.h[gi:gi + 1, :].partition_broadcast(128)),
              reads=[gains.k()], writes=[gb.k()])
        for n, tt in enumerate(tiles):
            s0, p = TT[tt]
            a = n % len(ht)
            src, skey = h_src(tt)
            P.dma("sp", lambda e, a=a, p=p, src=src: e.dma_start(out=ht[a][:p, :], in_=src), reads=[skey], writes=[ht[a].k()])
            P.op("act", lambda e, a=a, p=p: e.activation(out=junk[:p, :], in_=ht[a][:p, :], func=AF.Square, accum_out=ss[a][:p, 0:1]),
                 reads=[ht[a].k()], writes=[junk.k(), ss[a].k()])
            P.op("act", lambda e, a=a, p=p: e.activation(out=ss[a][:p, 1:2], in_=ss[a][:p, 0:1], func=AF.Sqrt, bias=cst[:p, 0:1], scale=1.0 / D),
                 reads=[ss[a].k(), cst.k()], writes=[ss[a].k("r")])
            P.op("dve", lambda e, a=a, p=p: e.reciprocal(out=ss[a][:p, 1:2], in_=ss[a][:p, 1:2]), reads=[ss[a].k("r")], writes=[ss[a].k("r")])
            P.op("dve", lambda e, a=a, p=p: e.scalar_tensor_tensor(out=hb[a][:p, :], in0=ht[a][:p, :], scalar=ss[a][:p, 1:2], in1=gb[:p, :],
                                                                  op0=ALU.mult, op1=ALU.mult),
                 reads=[ht[a].k(), ss[a].k("r"), gb.k()], writes=[hb[a].k()])
            for half in range(2):
                bk = B[banks[half]]
                for j in range(8):
                    kt = half * 8 + j
                    P.op("pe", lambda e, a=a, p=p, j=j, kt=kt, bk=bk: e.transpose(out=bf(bk)[:, j * 128:j * 128 + p], in_=hb[a][:p, kt * 128:(kt + 1) * 128],
                                                                                   identity=ident[:p, :p]),
                         reads=[hb[a].k(), ident.k()], writes=[bk.k()])
                P.op("act", lambda e, p=p, half=half, bk=bk, s0=s0: e.copy(
                    out=dstT[:, half * 8:(half + 1) * 8, s0 - tok_off:s0 - tok_off + p],
                    in_=bf(bk).rearrange("q (j c) -> q j c", c=128)[:, 0:8, 0:p]),
                    reads=[bk.k()], writes=[dstT.k(tt - koff)])
        P.release(m)

    def proj_tok(wt_dram, col_blocks, srcT, tiles, consumer, tok_off=0, banks=(2, 3, 4, 5), wbufs=None, tiled=None, koff=0):
        nb = 0
        fifo = []
        for cb, (c0, cw) in enumerate(col_blocks):
            wtb = wbufs[cb % 2]
            if tiled is not None and cw == 512:
                P.dma("sp", lambda e, wtb=wtb, cb=cb: e.dma_start(out=wtb[:, :, :], in_=tiled.h[cb].rearrange("p (kt c) -> p kt c", c=512)),
                      reads=[tiled.k(cb)], writes=[wtb.k()])
            else:
                P.dma("sp", lambda e, wtb=wtb, c0=c0, cw=cw: e.dma_start(
                    out=wtb[:, :, 0:cw], in_=wt_dram.h[:, c0:c0 + cw].rearrange("(kt p) c -> p kt c", p=128)),
                    reads=wt_dram.rkeys, writes=[wtb.k()])
            for tt in tiles:
                s0, p = TT[tt]
                bk = B[banks[nb % len(banks)]]
                nb += 1
                for kt in range(16):
                    P.op("pe", lambda e, bk=bk, p=p, kt=kt, s0=s0, wtb=wtb, cw=cw: e.matmul(
                        bk.h[:p, 0:cw], lhsT=srcT[:, kt, s0 - tok_off:s0 - tok_off + p], rhs=wtb[:, kt, 0:cw], start=(kt == 0), stop=(kt == 15)),
                        reads=[srcT.k(tt - koff), wtb.k()], writes=[bk.k()])
                dfr = consumer(cb, tt, bk, cw)
                if dfr is not None:
                    fifo.append(dfr)
                    if len(fifo) > 2:
                        fifo.pop(0)()
        while fifo:
            fifo.pop(0)()

    for nm in ("ewin", "ewout", "w1_0", "w2_0", "owin", "owout", "w1_1", "w2_1"):
        convert(nm)
    cast_pump(12)
    for nm in ("ewin", "w1_0", "w2_0", "owin", "w1_1", "w2_1"):
        retile(nm)

    def mlp_and_finish(li, final):
        m0 = P.mark()
        groups = [[0, 1, 2, 3, 4], [5, 6, 7, 8], [9, 10, 11, 12], [13, 14, 15, 16]]
        hTs = [P.sb("hn2T", [128, 16, 528], BF16) for _ in range(2)]
        hid = P.sb("hid", [128, 64, 528], BF16)
        wA = [P.sb("wA", [128, 16, 512], BF16) for _ in range(2)]
        wBf = wA
        ot = [P.sb("ot", [128, D], F32) for _ in range(5)]
        rl = [P.sb("rl", [128, 512], BF16) for _ in range(2)]
        nb_ = K()
        nb_.gb = P.sb("gb", [128, D], F32)
        nb_.ht = [P.sb("ht", [128, D], F32)]
        nb_.hb = [P.sb("hb", [128, D], BF16)]
        nb_.junk = P.sb("junk", [128, D], BF16)
        nb_.ss = [P.sb("ss", [128, 2], F32)]
        g4 = nb_.gb
        hres_t = nb_.ht
        junk2 = nb_.junk
        ssm = P.sb("ssm", [128, 8], F32)
        pump_until(38 if li == 0 else 82)
        mstep = {"n": 0}
        W1 = wtd["w1_%d" % li]
        W2 = wtd["w2_%d" % li]
        phase_norm_T(li * 4 + 2, hTs[0], groups[0], tok_off=0, banks=(0, 1), bufs=nb_)
        for gi, gtiles in enumerate(groups):
            tok0 = TT[gtiles[0]][0]
            ntok = sum(TT[t][1] for t in gtiles)
            hT = hTs[gi % 2]
            if gtiles[0] == 0:
                sub = [(0, 16, [0]), (16, 512, [1, 2, 3, 4])]
            else:
                sub = [(0, 512, gtiles)]
            nb = 0
            for fc in range(16):
                wt = wA[fc % 2]
                P.dma("sp", lambda e, wt=wt, fc=fc: e.dma_start(
                    out=wt[:, :, :], in_=W1.h[fc].rearrange("p (kt c) -> p kt c", c=512)),
                    reads=[W1.k(fc)], writes=[wt.k()])
                if li == 0:
                    mstep["n"] += 1
                    if castq:
                        if mstep["n"] % 2 == 0:
                            cast_pump(1, gate=[wt.k()])
                    else:
                        pump(1)
                for fl in range(4):
                    ft = fc * 4 + fl
                    for (o0, n, stiles) in sub:
                        bk = B[nb % 3]
                        nb += 1
                        for kt in range(16):
                            P.op("pe", lambda e, bk=bk, n=n, kt=kt, o0=o0, wt=wt, fl=fl, hT=hT: e.matmul(
                                bk.h[:, 0:n], lhsT=wt[:, kt, fl * 128:(fl + 1) * 128], rhs=hT[:, kt, o0:o0 + n], start=(kt == 0), stop=(kt == 15)),
                                reads=[hT.k(t) for t in stiles] + [wt.k()], writes=[bk.k()])
                        r = rl[nb % 2]
                        P.op("act", lambda e, bk=bk, n=n, r=r: e.activation(out=r[:, 0:n], in_=bk.h[:, 0:n], func=AF.Relu),
                             reads=[bk.k()], writes=[r.k()])
                        P.op("dve", lambda e, n=n, r=r, ft=ft, o0=o0: e.tensor_tensor(out=hid[:, ft, o0:o0 + n], in0=r[:, 0:n], in1=r[:, 0:n], op=ALU.mult),
                             reads=[r.k()], writes=[hid.k(ft)])
            for cb in range(4):
                if cb == 2 and gi + 1 < len(groups):
                    phase_norm_T(li * 4 + 2, hTs[(gi + 1) % 2], groups[gi + 1], tok_off=TT[groups[gi + 1][0]][0], banks=(0, 1), bufs=nb_)
                for fq in range(4):
                    wt = wBf[(cb * 4 + fq) % 2]
                    P.dma("sp", lambda e, wt=wt, cb=cb, fq=fq: e.dma_start(
                        out=wt[:, :, :], in_=W2.h[cb * 4 + fq].rearrange("p (kt c) -> p kt c", c=512)),
                        reads=[W2.k(cb * 4 + fq)], writes=[wt.k()])
                    if li == 0:
                        mstep["n"] += 1
                        if castq:
                            if mstep["n"] % 2 == 0:
                                cast_pump(1, gate=[wt.k()])
                        else:
                            pump(1)
                    for fl in range(16):
                        ft = fq * 16 + fl
                        for n_, tt in enumerate(gtiles):
                            s0, p = TT[tt]
                            bk = B[3 + n_]
                            P.op("pe", lambda e, bk=bk, p=p, ft=ft, fl=fl, s0=s0, wt=wt, tok0=tok0: e.matmul(
                                bk.h[:p, :], lhsT=hid[:, ft, s0 - tok0:s0 - tok0 + p], rhs=wt[:, fl, :], start=(ft == 0), stop=(ft == 63)),
                                reads=[hid.k(ft), wt.k()], writes=[bk.k()])
                for n_, tt in enumerate(gtiles):
                    s0, p = TT[tt]
                    bk = B[3 + n_]
                    P.op("act", lambda e, bk=bk, p=p, n_=n_, cb=cb: e.copy(out=ot[n_][:p, cb * 512:(cb + 1) * 512], in_=bk.h[:p, :]),
                         reads=[bk.k()], writes=[ot[n_].k(cb)])
            P.dma("sp", lambda e: e.dma_start(out=g4[:, :], in_=gains.h[li * 4 + 3:li * 4 + 4, :].partition_broadcast(128)),
                  reads=[gains.k()], writes=[g4.k()])
            for n_, tt in enumerate(gtiles):
                s0, p = TT[tt]
                a = 0
                src, skey = h_src(tt)
                P.dma("sp", lambda e, a=a, p=p, src=src: e.dma_start(out=hres_t[a][:p, :], in_=src), reads=[skey], writes=[hres_t[a].k()])
                okeys = [ot[n_].k(c) for c in range(4)]
                P.op("act", lambda e, p=p, n_=n_: e.activation(out=junk2[:p, :], in_=ot[n_][:p, :], func=AF.Square, accum_out=ssm[:p, 0:1]),
                     reads=okeys, writes=[junk2.k(), ssm.k()])
                P.op("act", lambda e, p=p: e.activation(out=ssm[:p, 1:2], in_=ssm[:p, 0:1], func=AF.Sqrt, bias=cst[:p, 0:1], scale=1.0 / D),
                     reads=[ssm.k(), cst.k()], writes=[ssm.k("r")])
                P.op("dve", lambda e, p=p: e.reciprocal(out=ssm[:p, 1:2], in_=ssm[:p, 1:2]), reads=[ssm.k("r")], writes=[ssm.k("r")])
                P.op("dve", lambda e, p=p, n_=n_: e.scalar_tensor_tensor(out=ot[n_][:p, :], in0=ot[n_][:p, :], scalar=ssm[:p, 1:2], in1=g4[:p, :],
                                                                         op0=ALU.mult, op1=ALU.mult),
                     reads=okeys + [ssm.k("r"), g4.k()], writes=okeys)
                P.op("dve", lambda e, p=p, n_=n_, a=a: e.tensor_tensor(out=ot[n_][:p, :], in0=ot[n_][:p, :], in1=hres_t[a][:p, :], op=ALU.add),
                     reads=okeys + [hres_t[a].k()], writes=okeys)
                if final:
                    if tt > 0:
                        P.dma("sp", lambda e, p=p, n_=n_, s0=s0: e.dma_start(out=out[s0 - 16:s0 - 16 + p, :], in_=ot[n_][:p, :]),
                              reads=okeys, writes=[OUT.k(tt)], is_out=True)
                else:
                    P.dma("sp", lambda e, p=p, n_=n_, s0=s0: e.dma_start(out=hres.h[s0:s0 + p, :], in_=ot[n_][:p, :]),
                          reads=okeys, writes=[hres.k(tt)])
        P.release(m0)

    def wout_phase(li, wname):
        m0 = P.mark()
        mixT = P.sb("mixT", [128, 16, L], BF16)
        for kt in range(16):
            P.dma("sp", lambda e, kt=kt: e.dma_start(out=mixT[:, kt, :], in_=mixd.h[kt * 128:(kt + 1) * 128, :]),
                  reads=[mixd.k(kt)], writes=[mixT.k(kt)])
        need(wname)
        W = wb[wname]
        wo = P.sb("wo", [128, 16, D], BF16)
        for kt in range(16):
            P.dma("sp", lambda e, kt=kt: e.dma_start(out=wo[:, kt, :], in_=W.h[kt * 128:(kt + 1) * 128, :]),
                  reads=W.rkeys, writes=[wo.k(kt)])
        g1 = P.sb("g1", [128, D], F32)
        P.dma("sp", lambda e: e.dma_start(out=g1[:, :], in_=gains.h[li * 4 + 1:li * 4 + 2, :].partition_broadcast(128)),
              reads=[gains.k()], writes=[g1.k()])
        ot = [P.sb("wot", [128, D], F32) for _ in range(2)]
        hrt = [P.sb("whr", [128, D], F32) for _ in range(2)]
        ssm = P.sb("wss", [128, 8], F32)
        junk2 = P.sb("wjunk", [128, D], BF16)
        for tt in range(17):
            s0, p = TT[tt]
            a = tt % 2
            src, skey = h_src(tt)
            P.dma("sp", lambda e, a=a, p=p, src=src: e.dma_start(out=hrt[a][:p, :], in_=src), reads=[skey], writes=[hrt[a].k()])
            for cb in range(4):
                bk = B[(tt * 4 + cb) % 8]
                for kt in range(16):
                    P.op("pe", lambda e, bk=bk, p=p, kt=kt, s0=s0, cb=cb: e.matmul(
                        bk.h[:p, :], lhsT=mixT[:, kt, s0:s0 + p], rhs=wo[:, kt, cb * 512:(cb + 1) * 512], start=(kt == 0), stop=(kt == 15)),
                        reads=[mixT.k(kt), wo.k(kt)], writes=[bk.k()])
                P.op("act", lambda e, bk=bk, p=p, a=a, cb=cb: e.copy(out=ot[a][:p, cb * 512:(cb + 1) * 512], in_=bk.h[:p, :]),
                     reads=[bk.k()], writes=[ot[a].k(cb)])
            okeys = [ot[a].k(c) for c in range(4)]
            P.op("act", lambda e, p=p, a=a: e.activation(out=junk2[:p, :], in_=ot[a][:p, :], func=AF.Square, accum_out=ssm[:p, 0:1]),
                 reads=okeys, writes=[junk2.k(), ssm.k()])
            P.op("act", lambda e, p=p: e.activation(out=ssm[:p, 1:2], in_=ssm[:p, 0:1], func=AF.Sqrt, bias=cst[:p, 0:1], scale=1.0 / D),
                 reads=[ssm.k(), cst.k()], writes=[ssm.k("r")])
            P.op("dve", lambda e, p=p: e.reciprocal(out=ssm[:p, 1:2], in_=ssm[:p, 1:2]), reads=[ssm.k("r")], writes=[ssm.k("r")])
            P.op("dve", lambda e, p=p, a=a: e.scalar_tensor_tensor(out=ot[a][:p, :], in0=ot[a][:p, :], scalar=ssm[:p, 1:2], in1=g1[:p, :],
                                                                   op0=ALU.mult, op1=ALU.mult),
                 reads=okeys + [ssm.k("r"), g1.k()], writes=okeys)
            P.op("dve", lambda e, p=p, a=a: e.tensor_tensor(out=ot[a][:p, :], in0=ot[a][:p, :], in1=hrt[a][:p, :], op=ALU.add),
                 reads=okeys + [hrt[a].k()], writes=okeys)
            P.dma("sp", lambda e, p=p, a=a, s0=s0: e.dma_start(out=hres.h[s0:s0 + p, :], in_=ot[a][:p, :]),
                  reads=okeys, writes=[hres.k(tt)])
        P.release(m0)
        state.layer0 = False

    ctx = K()
    ctx.nc = nc; ctx.P = P; ctx.B = B; ctx.bf = bf; ctx.din = din; ctx.wb = wb; ctx.ident = ident; ctx.identf = identf
    ctx.ones = ones; ctx.cst = cst; ctx.mixd = mixd; ctx.hres = hres; ctx.state = state; ctx.h_src = h_src
    ctx.pd = pd; ctx.wt = wtd; ctx.retile = retile; ctx.pump = pump; ctx.cast_pump = cast_pump; ctx.need = need; ctx.castq = castq; ctx.pump_until = pump_until; ctx.bgq = bgq; ctx.phase_norm_T = phase_norm_T; ctx.proj_tok = proj_tok; ctx.convert = convert; ctx.gains = gains

    if stop_after == 104:
        need("owin")
        pump_until(50)
        odd_mixer(ctx)
        stop_after = 4
    else:
        if stop_after >= 1:
            even_mixer(ctx)
        if stop_after >= 2:
            wout_phase(0, "ewout")
        if stop_after >= 3:
            mlp_and_finish(0, final=False)
        if stop_after >= 4:
            odd_mixer(ctx)
        if stop_after >= 5:
            wout_phase(1, "owout")
        if stop_after >= 6:
            mlp_and_finish(1, final=True)
    if stop_after < 6:
        m0 = P.mark()
        if stop_after in (1, 4):
            tmpb = P.sb("dbg_b", [128, L], BF16)
            tmpf = P.sb("dbg_f", [128, L], F32)
            for kt in range(16):
                P.dma("sp", lambda e, kt=kt: e.dma_start(out=tmpb[:, :], in_=mixd.h[kt * 128:(kt + 1) * 128, :]), reads=[mixd.k(kt)], writes=[tmpb.k()])
                P.op("dve", lambda e: e.tensor_copy(out=tmpf[:, :], in_=tmpb[:, :]), reads=[tmpb.k()], writes=[tmpf.k()])
                P.dma("sp", lambda e, kt=kt: e.dma_start(out=out[kt * 128:(kt + 1) * 128, :], in_=tmpf[:, 16:L]), reads=[tmpf.k()], writes=[OUT.k(kt)], is_out=True)
        else:
            tmpf = P.sb("dbg_f", [128, D], F32)
            for tt in range(1, 17):
                s0, p = TT[tt]
                P.dma("sp", lambda e, s0=s0: e.dma_start(out=tmpf[:, :], in_=hres.h[s0:s0 + 128, :]), reads=[hres.k(tt)], writes=[tmpf.k()])
                P.dma("sp", lambda e, s0=s0: e.dma_start(out=out[s0 - 16:s0 + 112, :], in_=tmpf[:, :]), reads=[tmpf.k()], writes=[OUT.k(tt)], is_out=True)
    P.finish()
    P.emit(st)
    print("ops:", P.stats())
    return nc, st


def even_mixer(c):
    P, B, bf, din, wb = c.P, c.B, c.bf, c.din, c.wb
    ident, identf, ones, cst, mixd = c.ident, c.identf, c.ones, c.cst, c.mixd
    m_all = P.mark()
    uT = P.sb("uT", [128, 4, L], BF16)
    m_q = P.mark()
    qT = P.sb("qT", [128, 12, L], BF16)
    kT = P.sb("kT", [128, 4, L], BF16)
    V = P.sb("V", [128, 17, 512], BF16)
    m_proj = P.mark()
    hnT = P.sb("hnT", [128, 16, 1040], BF16)
    nbf = K()
    nbf.gb = P.sb("gb", [128, D], F32)
    nbf.ht = [P.sb("ht", [128, D], F32)]
    nbf.hb = [P.sb("hb", [128, D], BF16)]
    nbf.junk = P.sb("junk", [128, D], BF16)
    nbf.ss = [P.sb("ss", [128, 2], F32)]
    wbufs = [P.sb("wp", [128, 16, 512], BF16) for _ in range(2)]
    gqk = P.sb("gqk", [128, 2, 128], F32)
    P.dma("sp", lambda e: e.dma_start(out=gqk[:, 0, :], in_=din["qkg"].h[0:1, :].partition_broadcast(128)), reads=[din["qkg"].k()], writes=[gqk.k()])
    P.dma("sp", lambda e: e.dma_start(out=gqk[:, 1, :], in_=din["qkg"].h[1:2, :].partition_broadcast(128)), reads=[din["qkg"].k()], writes=[gqk.k()])
    P.op("dve", lambda e: e.tensor_scalar(out=gqk[:, 0, :], in0=gqk[:, 0, :], scalar1=128.0 ** -0.5, scalar2=None, op0=ALU.mult),
         reads=[gqk.k()], writes=[gqk.k()])
    rope = [P.sb("rope", [128, 2, 64], F32) for _ in range(2)]
    qf = [P.sb("qf", [128, 512], F32) for _ in range(2)]
    sq = P.sb("sq", [128, 512], F32)
    ss4 = P.sb("ss4", [128, 8], F32)
    tA = P.sb("tA", [128, 256], F32)
    tB_ = P.sb("tB", [128, 256], F32)
    qr = [P.sb("qr", [128, 512], BF16) for _ in range(4)]
    cnt = {"n": 0}
    rope_loaded = {}

    def consumer(cb, tt, bk, cw):
        s0, p = TT[tt]
        n = cnt["n"]
        cnt["n"] += 1
        a = n % 2
        if cb < 4:
            isq = cb < 3
            gi = 0 if isq else 1
            rp = rope[tt % 2]
            if rope_loaded.get(tt % 2) != tt:
                P.dma("sp", lambda e, rp=rp, tt=tt, p=p: e.dma_start(out=rp[:p, :, :], in_=din["c_rope_e"].h[0:p, tt, :, :]),
                      reads=[din["c_rope_e"].k()], writes=[rp.k()])
                rope_loaded[tt % 2] = tt
            f = qf[a]
            P.op("act", lambda e, f=f, p=p, bk=bk: e.copy(out=f[:p, :], in_=bk.h[:p, 0:512]), reads=[bk.k()], writes=[f.k()])
            P.op("act", lambda e, f=f, p=p: e.activation(out=sq[:p, :], in_=f[:p, :], func=AF.Square), reads=[f.k()], writes=[sq.k()])
            P.op("dve", lambda e, p=p: e.reduce_sum(out=ss4[:p, 0:4], in_=sq[:p, :].rearrange("q (h d) -> q h d", d=128), axis=AX.X),
                 reads=[sq.k()], writes=[ss4.k()])
            P.op("act", lambda e, p=p: e.activation(out=ss4[:p, 4:8], in_=ss4[:p, 0:4], func=AF.Sqrt, bias=cst[:p, 0:1], scale=1.0 / 128),
                 reads=[ss4.k(), cst.k()], writes=[ss4.k("r")])
            P.op("dve", lambda e, p=p: e.reciprocal(out=ss4[:p, 4:8], in_=ss4[:p, 4:8]), reads=[ss4.k("r")], writes=[ss4.k("r")])
            f3 = f[:p, :].rearrange("q (h d) -> q h d", d=128)
            P.op("dve", lambda e, f3=f3, p=p: e.tensor_tensor(out=f3, in0=f3, in1=ss4[:p, 4:8].unsqueeze(2).broadcast_to([p, 4, 128]), op=ALU.mult),
                 reads=[f.k(), ss4.k("r")], writes=[f.k()])
            P.op("dve", lambda e, f3=f3, p=p, gi=gi: e.tensor_tensor(out=f3, in0=f3, in1=gqk[:p, gi, :].unsqueeze(1).broadcast_to([p, 4, 128]), op=ALU.mult),
                 reads=[f.k(), gqk.k()], writes=[f.k()])
            f5 = f[:p, :].rearrange("q (g t f) -> q g t f", g=8, t=2, f=32)
            r = qr[n % 4]
            r5 = r[:p, :].rearrange("q (g t f) -> q g t f", g=8, t=2, f=32)
            x1 = f5[:, :, 0, :]
            x2 = f5[:, :, 1, :]
            cs_ = rp[:p, 0, :].rearrange("q (a f) -> q a f", a=2).unsqueeze(1).broadcast_to([p, 4, 2, 32])
            sn_ = rp[:p, 1, :].rearrange("q (a f) -> q a f", a=2).unsqueeze(1).broadcast_to([p, 4, 2, 32])
            x1 = f[:p, :].rearrange("q (h a t f) -> q h a t f", h=4, a=2, t=2, f=32)[:, :, :, 0, :]
            x2 = f[:p, :].rearrange("q (h a t f) -> q h a t f", h=4, a=2, t=2, f=32)[:, :, :, 1, :]
            r5 = r[:p, :].rearrange("q (h a t f) -> q h a t f", h=4, a=2, t=2, f=32)
            ta4 = tA[:p, :].rearrange("q (h a f) -> q h a f", h=4, a=2)
            tb4 = tB_[:p, :].rearrange("q (h a f) -> q h a f", h=4, a=2)
            rk = [f.k(), rp.k()]
            P.op("dve", lambda e: e.tensor_tensor(out=ta4, in0=x1, in1=cs_, op=ALU.mult), reads=rk, writes=[tA.k()])
            P.op("dve", lambda e: e.tensor_tensor(out=tb4, in0=x2, in1=sn_, op=ALU.mult), reads=rk, writes=[tB_.k()])
            P.op("dve", lambda e: e.tensor_tensor(out=r5[:, :, :, 0, :], in0=ta4, in1=tb4, op=ALU.subtract), reads=[tA.k(), tB_.k()], writes=[r.k(0)])
            P.op("dve", lambda e: e.tensor_tensor(out=ta4, in0=x1, in1=sn_, op=ALU.mult), reads=rk + [r.k(0)], writes=[tA.k()])
            P.op("dve", lambda e: e.tensor_tensor(out=tb4, in0=x2, in1=cs_, op=ALU.mult), reads=rk + [r.k(0)], writes=[tB_.k()])
            P.op("dve", lambda e: e.tensor_tensor(out=r5[:, :, :, 1, :], in0=ta4, in1=tb4, op=ALU.add), reads=[tA.k(), tB_.k()], writes=[r.k(1)])
            tb = B[6 + n % 2]
            dst = qT if isq else kT
            h0 = cb * 4 if isq else 0

            def part2():
                for j in range(4):
                    P.op("pe", lambda e, j=j: e.transpose(out=bf(tb)[:, j * 128:j * 128 + p], in_=r[:p, j * 128:(j + 1) * 128], identity=ident[:p, :p]),
                         reads=[r.k(0), r.k(1), ident.k()], writes=[tb.k()])
                P.op("act", lambda e: e.copy(out=dst[:, h0:h0 + 4, s0:s0 + p], in_=bf(tb).rearrange("q (j c) -> q j c", c=128)[:, 0:4, 0:p]),
                     reads=[tb.k()], writes=[dst.k(tt)])
            return part2
        elif cb == 4:
            P.op("act", lambda e, bk=bk, p=p, tt=tt: e.copy(out=V[:p, tt, :], in_=bk.h[:p, 0:512]), reads=[bk.k()], writes=[V.k(tt)])
        else:
            r = qr[n % 4]
            P.op("act", lambda e, bk=bk, p=p, r=r: e.copy(out=r[:p, :], in_=bk.h[:p, 0:512]), reads=[bk.k()], writes=[r.k(0), r.k(1)])
            tb = B[6 + n % 2]

            def part2u():
                for j in range(4):
                    P.op("pe", lambda e, j=j: e.transpose(out=bf(tb)[:, j * 128:j * 128 + p], in_=r[:p, j * 128:(j + 1) * 128], identity=ident[:p, :p]),
                         reads=[r.k(0), r.k(1), ident.k()], writes=[tb.k()])
                P.op("act", lambda e: e.copy(out=uT[:, 0:4, s0:s0 + p], in_=bf(tb).rearrange("q (j c) -> q j c", c=128)[:, 0:4, 0:p]),
                     reads=[tb.k()], writes=[uT.k(tt)])
            return part2u

    blocks = [(i * 512, 512) for i in range(6)]
    for (tiles, toff, koff) in ((list(range(0, 9)), 0, 0), (list(range(9, 17)), 1024, 8)):
        c.phase_norm_T(0, hnT, tiles, tok_off=toff, bufs=nbf, koff=koff)
        if koff == 0:
            c.pump(6)
        c.proj_tok(wb["ewin"], blocks, hnT, tiles, consumer, tok_off=toff, banks=(2, 3, 4, 5), wbufs=wbufs, tiled=c.wt["ewin"], koff=koff)
    P.release(m_proj)

    m_att = P.mark()
    PT = [P.sb("PT", [128, 512], BF16) for _ in range(4)]
    rden = [P.sb("rden", [128, 512], F32) for _ in range(2)]
    ob = [P.sb("ob", [128, 512], BF16) for _ in range(4)]
    it = 0
    for g in range(4):
        for hq in range(3):
            head = g * 3 + hq
            for (q0, nq, qtiles) in TB:
                Ob = B[4 + it % 2]
                Db = B[6 + it % 2]
                qk = [qT.k(t) for t in qtiles]

                def S(s, q0=q0, nq=nq, head=head, g=g, qk=qk):
                    ss0, ps = TT[s]
                    bk = B[s % 4]
                    P.op("pe", lambda e: e.matmul(bk.h[:ps, 0:nq], lhsT=kT[:, g, ss0:ss0 + ps], rhs=qT[:, head, q0:q0 + nq], start=True, stop=True),
                         reads=[kT.k(s)] + qk, writes=[bk.k()])

                S(0)
                S(1)
                S(2)
                for s in range(17):
                    if s + 3 < 17:
                        S(s + 3)
                    ss0, ps = TT[s]
                    bk = B[s % 4]
                    pt = PT[s % 4]
                    P.op("act", lambda e, bk=bk, pt=pt, ps=ps, nq=nq: e.activation(out=pt[:ps, 0:nq], in_=bk.h[:ps, 0:nq], func=AF.Exp),
                         reads=[bk.k()], writes=[pt.k()])
                    P.op("pe", lambda e, pt=pt, ps=ps, nq=nq, s=s, g=g, Ob=Ob: e.matmul(Ob.h[:, 0:nq], lhsT=V[:ps, s, g * 128:(g + 1) * 128], rhs=pt[:ps, 0:nq],
                                                                                        start=(s == 0), stop=(s == 16)),
                         reads=[V.k(s), pt.k()], writes=[Ob.k()])
                    P.op("pe", lambda e, pt=pt, ps=ps, nq=nq, s=s, Db=Db: e.matmul(Db.h[:, 0:nq], lhsT=ones[:ps, :], rhs=pt[:ps, 0:nq],
                                                                                  start=(s == 0), stop=(s == 16)),
                         reads=[ones.k(), pt.k()], writes=[Db.k()])
                o = ob[it % 4]
                rd = rden[it % 2]
                P.op("dve", lambda e, Db=Db, nq=nq, rd=rd: e.reciprocal(out=rd[:, 0:nq], in_=Db.h[:, 0:nq]), reads=[Db.k()], writes=[rd.k()])
                P.op("dve", lambda e, Ob=Ob, nq=nq, o=o, rd=rd: e.tensor_tensor(out=o[:, 0:nq], in0=Ob.h[:, 0:nq], in1=rd[:, 0:nq], op=ALU.mult),
                     reads=[Ob.k(), rd.k()], writes=[o.k()])
                P.dma("sp", lambda e, o=o, head=head, q0=q0, nq=nq: e.dma_start(out=mixd.h[head * 128:(head + 1) * 128, q0:q0 + nq], in_=o[:, 0:nq]),
                      reads=[o.k()], writes=[mixd.k(head)])
                it += 1
    P.release(m_q)
    s5_phase(c, uT)
    P.release(m_all)


def make_red(P, f32, nmax):
    I32 = mybir.dt.int32
    ki = P.sb("redki", [128, nmax], I32)
    kf = P.sb("redkf", [128, nmax], f32)

    def red(dst_t, dst, src_t, src, n, add=0.0):
        P.op("dve", lambda e: e.tensor_scalar(out=kf[:, 0:n], in0=src, scalar1=add, scalar2=1.0 / TWO_PI, op0=ALU.add, op1=ALU.mult),
             reads=[src_t.k()], writes=[kf.k()])
        P.op("dve", lambda e: e.tensor_copy(out=ki[:, 0:n], in_=kf[:, 0:n]), reads=[kf.k()], writes=[ki.k()])
        P.op("dve", lambda e: e.tensor_copy(out=kf[:, 0:n], in_=ki[:, 0:n]), reads=[ki.k()], writes=[kf.k()])
        P.op("dve", lambda e: e.tensor_scalar(out=kf[:, 0:n], in0=kf[:, 0:n], scalar1=-TWO_PI, scalar2=add, op0=ALU.mult, op1=ALU.add),
             reads=[kf.k()], writes=[kf.k()])
        P.op("dve", lambda e: e.tensor_tensor(out=dst, in0=kf[:, 0:n], in1=src, op=ALU.add), reads=[kf.k(), src_t.k()], writes=[dst_t.k()])
        P.op("dve", lambda e: e.tensor_scalar(out=kf[:, 0:n], in0=dst, scalar1=0.0, scalar2=TWO_PI, op0=ALU.is_lt, op1=ALU.mult),
             reads=[dst_t.k()], writes=[kf.k()])
        P.op("dve", lambda e: e.tensor_tensor(out=dst, in0=dst, in1=kf[:, 0:n], op=ALU.add), reads=[kf.k(), dst_t.k()], writes=[dst_t.k()])
    return red

def run_lanes_s5(items, make_gen, W):
    it = iter(items)
    free = list(range(W))
    active = []
    while True:
        while free:
            try:
                x_ = next(it)
            except StopIteration:
                break
            ln_ = free.pop(0)
            active.append((make_gen(x_, ln_), ln_))
        if not active:
            break
        for ga in list(active):
            try:
                next(ga[0])
            except StopIteration:
                active.remove(ga)
                free.append(ga[1])


def s5_phase(c, uT):
    P, B, bf, din = c.P, c.B, c.bf, c.din
    ident, identf, ones, cst, mixd = c.ident, c.identf, c.ones, c.cst, c.mixd
    m0 = P.mark()
    f32 = F32
    pr = P.sb("s5pr", [128, 3, 32], f32)
    P.dma("sp", lambda e: e.dma_start(out=pr[:, :, :], in_=din["s5p"].h[:, :, :]), reads=[din["s5p"].k()], writes=[pr.k()])
    w = P.sb("s5w", [128, 16, 32], f32)
    BT = P.sb("s5BT", [128, 32, 2, 128], BF16)
    CT = P.sb("s5CT", [128, 32, 2, 128], BF16)
    PA = P.sb("s5PA", [128, 32, 128], f32)
    PA2 = P.sb("s5PA2", [128, 32, 128], f32)
    PB = P.sb("s5PB", [128, 32, 32], f32)
    m_tmp = P.mark()
    red = make_red(P, f32, 32 * 128)
    LR, DT, ER, TH, SN, CS, ARE, AIM, NR, DEN, CRE, CIM, TH128, T0, T1, T2 = range(16)
    wk = [w.k()]

    def V1(fn):
        P.op("dve", fn, reads=wk + [pr.k(), cst.k()], writes=wk)

    def A1(fn):
        P.op("act", fn, reads=wk + [pr.k(), cst.k()], writes=wk)

    V1(lambda e: e.tensor_scalar(out=w[:, LR, :], in0=pr[:, 0, :], scalar1=-1e-4, scalar2=None, op0=ALU.min))
    A1(lambda e: e.activation(out=w[:, DT, :], in_=pr[:, 2, :], func=AF.Exp))
    V1(lambda e: e.tensor_tensor(out=w[:, T0, :], in0=w[:, LR, :], in1=w[:, DT, :], op=ALU.mult))
    A1(lambda e: e.activation(out=w[:, ER, :], in_=w[:, T0, :], func=AF.Exp))
    V1(lambda e: e.tensor_tensor(out=w[:, TH, :], in0=pr[:, 1, :], in1=w[:, DT, :], op=ALU.mult))
    red(w, w[:, T0, :], w, w[:, TH, :], 32, add=TWO_PI)
    A1(lambda e: e.activation(out=w[:, SN, :], in_=w[:, T0, :], func=AF.Sin, bias=cst[:, 1:2], scale=1.0))
    red(w, w[:, T0, :], w, w[:, TH, :], 32, add=TWO_PI + math.pi / 2)
    A1(lambda e: e.activation(out=w[:, CS, :], in_=w[:, T0, :], func=AF.Sin, bias=cst[:, 1:2], scale=1.0))
    V1(lambda e: e.scalar_tensor_tensor(out=w[:, ARE, :], in0=w[:, CS, :], scalar=-1.0, in1=w[:, ER, :], op0=ALU.mult, op1=ALU.mult))
    V1(lambda e: e.scalar_tensor_tensor(out=w[:, AIM, :], in0=w[:, SN, :], scalar=-1.0, in1=w[:, ER, :], op0=ALU.mult, op1=ALU.mult))
    V1(lambda e: e.tensor_scalar(out=w[:, NR, :], in0=w[:, ARE, :], scalar1=-1.0, scalar2=None, op0=ALU.add))
    V1(lambda e: e.tensor_tensor(out=w[:, T0, :], in0=w[:, LR, :], in1=w[:, LR, :], op=ALU.mult))
    V1(lambda e: e.tensor_tensor(out=w[:, T1, :], in0=pr[:, 1, :], in1=pr[:, 1, :], op=ALU.mult))
    V1(lambda e: e.tensor_tensor(out=w[:, DEN, :], in0=w[:, T0, :], in1=w[:, T1, :], op=ALU.add))
    V1(lambda e: e.reciprocal(out=w[:, DEN, :], in_=w[:, DEN, :]))
    V1(lambda e: e.tensor_tensor(out=w[:, T0, :], in0=w[:, NR, :], in1=w[:, LR, :], op=ALU.mult))
    V1(lambda e: e.tensor_tensor(out=w[:, T1, :], in0=w[:, AIM, :], in1=pr[:, 1, :], op=ALU.mult))
    V1(lambda e: e.tensor_tensor(out=w[:, T0, :], in0=w[:, T0, :], in1=w[:, T1, :], op=ALU.add))
    V1(lambda e: e.tensor_tensor(out=w[:, CRE, :], in0=w[:, T0, :], in1=w[:, DEN, :], op=ALU.mult))
    V1(lambda e: e.tensor_tensor(out=w[:, T0, :], in0=w[:, AIM, :], in1=w[:, LR, :], op=ALU.mult))
    V1(lambda e: e.tensor_tensor(out=w[:, T1, :], in0=w[:, NR, :], in1=pr[:, 1, :], op=ALU.mult))
    V1(lambda e: e.tensor_tensor(out=w[:, T0, :], in0=w[:, T0, :], in1=w[:, T1, :], op=ALU.subtract))
    V1(lambda e: e.tensor_tensor(out=w[:, CIM, :], in0=w[:, T0, :], in1=w[:, DEN, :], op=ALU.mult))
    V1(lambda e: e.tensor_scalar(out=w[:, T1, :], in0=w[:, TH, :], scalar1=128.0, scalar2=None, op0=ALU.mult))
    red(w, w[:, TH128, :], w, w[:, T1, :], 32, add=TWO_PI)
    tauj = P.sb("s5tauj", [128, 128], f32)
    P.dma("sp", lambda e: e.dma_start(out=tauj[:, :], in_=din["c_tau"].h[:, 0, 0:128]), reads=[din["c_tau"].k()], writes=[tauj.k()])
    P.op("dve", lambda e: e.tensor_tensor(out=PA[:, :, :], in0=tauj[:, :].unsqueeze(1).broadcast_to([128, 32, 128]),
                                          in1=w[:, TH, :].unsqueeze(2).broadcast_to([128, 32, 128]), op=ALU.mult), reads=wk + [tauj.k()], writes=[PA.k()])
    P.op("dve", lambda e: e.tensor_tensor(out=PB[:, :, :], in0=tauj[:, 0:32].unsqueeze(1).broadcast_to([128, 32, 32]),
                                          in1=w[:, TH128, :].unsqueeze(2).broadcast_to([128, 32, 32]), op=ALU.mult), reads=wk + [tauj.k()], writes=[PB.k()])
    red(PA2, PA2[:, :, :].rearrange("p a b -> p (a b)"), PA, PA[:, :, :].rearrange("p a b -> p (a b)"), 32 * 128, add=TWO_PI + 1.5 * math.pi)
    red(PA, PA[:, :, :].rearrange("p a b -> p (a b)"), PA, PA[:, :, :].rearrange("p a b -> p (a b)"), 32 * 128, add=TWO_PI)
    red(PB, PB[:, :, :].rearrange("p a b -> p (a b)"), PB, PB[:, :, :].rearrange("p a b -> p (a b)"), 32 * 32, add=TWO_PI)
    bb = P.sb("s5bb", [128, 2, 32, 16], f32)
    bbar = P.sb("s5bbar", [128, 2, 32, 16], f32)
    cc = P.sb("s5cc", [128, 2, 32, 16], f32)
    tt_ = P.sb("s5tt", [128, 32, 16], f32)
    P.dma("sp", lambda e: e.dma_start(out=bb[:, :, :, :], in_=din["s5b"].h[:, :, :, :]), reads=[din["s5b"].k()], writes=[bb.k()])
    P.dma("sp", lambda e: e.dma_start(out=cc[:, :, :, :], in_=din["s5c"].h[:, :, :, :]), reads=[din["s5c"].k()], writes=[cc.k()])
    cre_b = w[:, CRE, :].unsqueeze(2).broadcast_to([128, 32, 16])
    cim_b = w[:, CIM, :].unsqueeze(2).broadcast_to([128, 32, 16])
    rk = wk + [bb.k()]
    P.op("dve", lambda e: e.tensor_tensor(out=bbar[:, 0, :, :], in0=bb[:, 0, :, :], in1=cre_b, op=ALU.mult), reads=rk, writes=[bbar.k(0)])
    P.op("dve", lambda e: e.tensor_tensor(out=tt_[:, :, :], in0=bb[:, 1, :, :], in1=cim_b, op=ALU.mult), reads=rk, writes=[tt_.k()])
    P.op("dve", lambda e: e.tensor_tensor(out=bbar[:, 0, :, :], in0=bbar[:, 0, :, :], in1=tt_[:, :, :], op=ALU.subtract), reads=[bbar.k(0), tt_.k()], writes=[bbar.k(0)])
    P.op("dve", lambda e: e.tensor_tensor(out=bbar[:, 1, :, :], in0=bb[:, 1, :, :], in1=cre_b, op=ALU.mult), reads=rk + [bbar.k(0)], writes=[bbar.k(1)])
    P.op("dve", lambda e: e.tensor_tensor(out=tt_[:, :, :], in0=bb[:, 0, :, :], in1=cim_b, op=ALU.mult), reads=rk + [bbar.k(0)], writes=[tt_.k()])
    P.op("dve", lambda e: e.tensor_tensor(out=bbar[:, 1, :, :], in0=bbar[:, 1, :, :], in1=tt_[:, :, :], op=ALU.add), reads=[bbar.k(1), tt_.k()], writes=[bbar.k(1)])
    Mp = [P.sb("s5M", [128, 128], f32) for _ in range(2)]
    P.op("dve", lambda e: e.memset(CT[:, :, :, :], 0.0), writes=[CT.k()])
    n = 0
    for di in range(32):
        i = di % 16
        c0 = 16 * ((2 * i) % 8)
        for r in range(2):
            M = Mp[n % 2]
            tb = B[n % 2]
            n += 1
            P.op("dve", lambda e, M=M: e.memset(M[:, :], 0.0), writes=[M.k()])
            P.op("dve", lambda e, M=M, di=di, r=r, c0=c0: e.tensor_copy(out=M[0:64, c0:c0 + 16], in_=bbar[0:64, r, di, :]), reads=[bbar.k(r)], writes=[M.k()])
            P.op("dve", lambda e, M=M, di=di, r=r, c0=c0: e.tensor_copy(out=M[64:128, c0 + 16:c0 + 32], in_=bbar[64:128, r, di, :]), reads=[bbar.k(r)], writes=[M.k()])
            P.op("pe", lambda e, M=M, tb=tb: e.transpose(out=tb.h[:, 0:128], in_=M[:, :], identity=identf[:, :]), reads=[M.k(), identf.k()], writes=[tb.k()])
            P.op("act", lambda e, tb=tb, di=di, r=r: e.copy(out=BT[:, di, r, :], in_=tb.h[:, 0:128]), reads=[tb.k()], writes=[BT.k()])
            sgn = 1.0 if r == 0 else -1.0
            P.op("dve", lambda e, di=di, r=r, c0=c0, sgn=sgn: e.tensor_scalar(out=CT[0:64, di, r, c0:c0 + 16], in0=cc[0:64, r, di, :], scalar1=sgn, scalar2=None, op0=ALU.mult),
                 reads=[cc.k()], writes=[CT.k()])
            P.op("dve", lambda e, di=di, r=r, c0=c0, sgn=sgn: e.tensor_scalar(out=CT[64:128, di, r, c0 + 16:c0 + 32], in0=cc[64:128, r, di, :], scalar1=sgn, scalar2=None, op0=ALU.mult),
                 reads=[cc.k()], writes=[CT.k()])
    P.release(m_tmp)
    bufA, bufB, bufC, bufD, bufE, bufF, bufG, bufH = [P.sb("s5buf", [128, L], f32) for _ in range(8)]
    xre = P.sb("s5xre", [128, L], BF16)
    xim = P.sb("s5xim", [128, L], BF16)
    yg = P.sb("s5yg", [128, 4, L], BF16)
    dsk = P.sb("s5dsk", [128, 4], f32)
    glb = P.sb("s5glb", [128, 4], f32)
    glw = P.sb("s5glw", [128, 4, 512], BF16)
    P.dma("sp", lambda e: e.dma_start(out=dsk[:, :], in_=din["s5d"].h[:, :]), reads=[din["s5d"].k()], writes=[dsk.k()])
    P.dma("sp", lambda e: e.dma_start(out=glb[:, :], in_=din["s5gb"].h[:, :]), reads=[din["s5gb"].k()], writes=[glb.k()])
    P.dma("pool", lambda e: e.dma_start(out=glw[:, :, :], in_=din["s5gw"].h[:, :].rearrange("(kt p) c -> p kt c", p=128)), reads=[din["s5gw"].k()], writes=[glw.k()])
    TBL = [(j * 512, 512) for j in range(4)] + [(2048, 16)]
    nbk = {"n": 0}
    s5it = {"n": 0}
    ukeys = [uT.k(t) for t in range(17)]
    Y = [B[3 + b] for b in range(5)]
    for ct in range(4):
        first = True
        for d in range(2):
            for il in range(4):
                i = ct * 4 + il
                di = d * 16 + i
                HV = [(0, 1024, [(0, 512), (512, 512)]), (1024, L, [(1024, 512), (1536, 512), (2048, 16)])]
                sn, cs = bufC, bufB

                def half_gen(hf, lane, d=d, ct=ct, di=di):
                    a, b, blks = HV[hf]
                    kk = hf

                    def TTm(o, a_, b_, op, eng="dve"):
                        P.op(eng, lambda e: e.tensor_tensor(out=o[:, a:b], in0=a_[:, a:b], in1=b_[:, a:b], op=op), reads=[a_.k(kk), b_.k(kk)], writes=[o.k(kk)])
                    for (t0, n_) in blks:
                        if d == 0:
                            rhs = uT[:, ct, t0:t0 + n_]
                        else:
                            hi = L - 1 - t0
                            lo = hi - n_
                            rhs = uT[:, ct, hi:lo:-1] if lo >= 0 else uT[:, ct, hi::-1]
                        for r, dst in ((0, bufD), (1, bufE)):
                            bk = B[nbk["n"] % 3]
                            nbk["n"] += 1
                            P.op("pe", lambda e, bk=bk, n_=n_, r=r, rhs=rhs: e.matmul(bk.h[:, 0:n_], lhsT=BT[:, di, r, :], rhs=rhs, start=True, stop=True),
                                 reads=[BT.k()] + ukeys, writes=[bk.k()])
                            P.op("act", lambda e, bk=bk, n_=n_, dst=dst, t0=t0: e.copy(out=dst[:, t0:t0 + n_], in_=bk.h[:, 0:n_]), reads=[bk.k()], writes=[dst.k(kk)])
                            yield
                    c0 = 8 * hf
                    for (pa, dstb) in ((PA, bufB), (PA2, bufC)):
                        P.op("dve", lambda e, pa=pa, dstb=dstb: e.tensor_tensor(
                            out=dstb[:, c0 * 128:(c0 + 8) * 128].rearrange("p (c j) -> p c j", j=128),
                            in0=pa[:, di, :].unsqueeze(1).broadcast_to([128, 8, 128]),
                            in1=PB[:, di, c0:c0 + 8].unsqueeze(2).broadcast_to([128, 8, 128]), op=ALU.add),
                            reads=[pa.k(), PB.k()], writes=[dstb.k(kk)])
                        yield
                        if hf == 1:
                            P.op("dve", lambda e, pa=pa, dstb=dstb: e.tensor_scalar(out=dstb[:, 2048:L], in0=pa[:, di, 0:16], scalar1=PB[:, di, 16:17], scalar2=None, op0=ALU.add),
                                 reads=[pa.k(), PB.k()], writes=[dstb.k(kk)])
                            yield
                        P.op("act", lambda e, dstb=dstb: e.activation(out=dstb[:, a:b], in_=dstb[:, a:b], func=AF.Abs, bias=cst[:, 3:4], scale=1.0), reads=[dstb.k(kk), cst.k()], writes=[dstb.k(kk)])
                        yield
                        P.op("act", lambda e, dstb=dstb: e.activation(out=dstb[:, a:b], in_=dstb[:, a:b], func=AF.Abs, bias=cst[:, 1:2], scale=1.0), reads=[dstb.k(kk), cst.k()], writes=[dstb.k(kk)])
                        yield
                        P.op("act", lambda e, dstb=dstb: e.activation(out=dstb[:, a:b], in_=dstb[:, a:b], func=AF.Sin, bias=cst[:, 4:5], scale=-1.0), reads=[dstb.k(kk), cst.k()], writes=[dstb.k(kk)])
                        yield
                    br, bi = bufD, bufE
                    TTm(bufF, cs, br, ALU.mult)
                    yield
                    TTm(bufG, sn, bi, ALU.mult)
                    yield
                    TTm(bufF, bufF, bufG, ALU.add)
                    yield
                    TTm(bufG, cs, bi, ALU.mult)
                    yield
                    TTm(bufA, sn, br, ALU.mult)
                    yield
                    TTm(bufG, bufG, bufA, ALU.subtract)
                    yield
                    er = w[:, ER, di:di + 1].broadcast_to([128, b - a])
                    for (src_, dst_) in ((bufF, bufD), (bufG, bufE)):
                        if hf == 0:
                            P.op("dve", lambda e, src_=src_, dst_=dst_: e.tensor_tensor_scan(out=dst_[:, a:b], data0=er, data1=src_[:, a:b], initial=0.0, op0=ALU.mult, op1=ALU.add),
                                 reads=[src_.k(kk)] + wk, writes=[dst_.k(kk)])
                        else:
                            P.op("dve", lambda e, src_=src_, dst_=dst_: e.tensor_tensor_scan(out=dst_[:, a:b], data0=er, data1=src_[:, a:b], initial=dst_[:, a - 1:a], op0=ALU.mult, op1=ALU.add),
                                 reads=[src_.k(kk), dst_.k(0)] + wk, writes=[dst_.k(kk)])
                        yield
                    TTm(bufF, cs, bufD, ALU.mult)
                    yield
                    TTm(bufA, sn, bufE, ALU.mult)
                    yield
                    TTm(xre, bufF, bufA, ALU.subtract)
                    yield
                    TTm(bufG, sn, bufD, ALU.mult)
                    yield
                    TTm(bufH, cs, bufE, ALU.mult)
                    yield
                    TTm(xim, bufG, bufH, ALU.add)
                    yield

                run_lanes_s5([0, 1], half_gen, 2)
                if s5it["n"] < 16:
                    c.cast_pump(2, gate=[xim.k(1)])
                else:
                    c.pump(2)
                s5it["n"] += 1
                last = (d == 1 and il == 3)
                for b, (q0, nq, qtiles) in enumerate(TB):
                    for r, xs in ((0, xre), (1, xim)):
                        if d == 0:
                            rhs = xs[:, q0:q0 + nq]
                        else:
                            hi = L - 1 - q0
                            lo = hi - nq
                            rhs = xs[:, hi:lo:-1] if lo >= 0 else xs[:, hi::-1]
                        P.op("pe", lambda e, b=b, nq=nq, di=di, r=r, rhs=rhs, st_=(first and r == 0), sp_=(last and r == 1): e.matmul(
                            Y[b].h[:, 0:nq], lhsT=CT[:, di, r, :], rhs=rhs, start=st_, stop=sp_),
                            reads=[CT.k(), xs.k(0), xs.k(1)], writes=[Y[b].k()])
                first = False
        wholeA = [bufA.k(0), bufA.k(1), bufA.k()]
        wholeF = [bufF.k(0), bufF.k(1), bufF.k()]
        for b, (q0, nq, qtiles) in enumerate(TB):
            yv = bufA
            P.op("dve", lambda e, b=b, q0=q0, nq=nq, ct=ct: e.scalar_tensor_tensor(out=yv[:, q0:q0 + nq], in0=uT[:, ct, q0:q0 + nq], scalar=dsk[:, ct:ct + 1],
                                                                                     in1=Y[b].h[:, 0:nq], op0=ALU.mult, op1=ALU.add),
                 reads=ukeys + [dsk.k(), Y[b].k()], writes=wholeA)
        P.op("dve", lambda e: e.tensor_tensor(out=bufF[:, :], in0=bufA[:, :], in1=bufA[:, :], op=ALU.mult), reads=wholeA, writes=wholeF)
        P.op("dve", lambda e: e.tensor_scalar(out=bufF[:, :], in0=bufF[:, :], scalar1=0.044715, scalar2=1.0, op0=ALU.mult, op1=ALU.add), reads=wholeF, writes=wholeF)
        P.op("dve", lambda e: e.tensor_tensor(out=bufF[:, :], in0=bufF[:, :], in1=bufA[:, :], op=ALU.mult), reads=wholeF + wholeA, writes=wholeF)
        P.op("act", lambda e: e.activation(out=bufF[:, :], in_=bufF[:, :], func=AF.Sigmoid, scale=2.0 * math.sqrt(2.0 / math.pi)), reads=wholeF, writes=wholeF)
        P.op("dve", lambda e, ct=ct: e.tensor_tensor(out=yg[:, ct, :], in0=bufF[:, :], in1=bufA[:, :], op=ALU.mult), reads=wholeF + wholeA, writes=[yg.k(ct)])
    ygk = [yg.k(i) for i in range(4)]
    og = [P.sb("s5og", [128, 512], BF16) for _ in range(2)]
    sg = P.sb("s5sg", [128, 512], f32)
    n = 0
    for cto in range(4):
        for (q0, nq, qtiles) in TB:
            bk = B[n % 3]
            o = og[n % 2]
            n += 1
            for kt in range(4):
                P.op("pe", lambda e, bk=bk, nq=nq, kt=kt, cto=cto, q0=q0: e.matmul(bk.h[:, 0:nq], lhsT=glw[:, kt, cto * 128:(cto + 1) * 128], rhs=yg[:, kt, q0:q0 + nq],
                                                                                 start=(kt == 0), stop=(kt == 3)),
                     reads=ygk + [glw.k()], writes=[bk.k()])
            P.op("act", lambda e, bk=bk, nq=nq, cto=cto: e.activation(out=sg[:, 0:nq], in_=bk.h[:, 0:nq], func=AF.Sigmoid, bias=glb[:, cto:cto + 1], scale=1.0),
                 reads=[bk.k(), glb.k()], writes=[sg.k()])
            P.op("dve", lambda e, nq=nq, cto=cto, q0=q0, o=o: e.tensor_tensor(out=o[:, 0:nq], in0=yg[:, cto, q0:q0 + nq], in1=sg[:, 0:nq], op=ALU.mult),
                 reads=ygk + [sg.k()], writes=[o.k()])
            P.dma("sp", lambda e, o=o, cto=cto, q0=q0, nq=nq: e.dma_start(out=mixd.h[(12 + cto) * 128:(13 + cto) * 128, q0:q0 + nq], in_=o[:, 0:nq]),
                  reads=[o.k()], writes=[mixd.k(12 + cto)])
    P.release(m0)


NEGBIG = -1.0e30


def odd_mixer(c):
    P, B, bf, din, wb = c.P, c.B, c.bf, c.din, c.wb
    ident, identf, ones, cst, mixd = c.ident, c.identf, c.ones, c.cst, c.mixd
    nc = c.nc
    pd = c.pd
    m_all = P.mark()
    gat = P.sb("gat", [128, 17, 32], F32)
    m_proj = P.mark()
    hnT = P.sb("hnT", [128, 16, L], BF16)
    c.phase_norm_T(4, hnT, list(range(17)))
    wbufs = [P.sb("wp", [128, 16, 512], BF16) for _ in range(2)]
    rope = [P.sb("rope", [128, 4, 64], F32) for _ in range(2)]
    qf = [P.sb("qf", [128, 512], F32) for _ in range(2)]
    tA = P.sb("tA", [128, 256], F32)
    tB_ = P.sb("tB", [128, 256], F32)
    qr = [P.sb("qr", [128, 512], BF16) for _ in range(3)]
    cnt = {"n": 0}
    rope_loaded = {}

    def consumer(cb, tt, bk, cw):
        s0, p = TT[tt]
        n = cnt["n"]
        cnt["n"] += 1
        a = n % 2
        r = qr[n % 3]
        if cb == 12:
            P.op("act", lambda e: e.copy(out=gat[:p, tt, :], in_=bk.h[:p, 0:32]), reads=[bk.k()], writes=[gat.k(tt)])
            return
        if cb < 4:
            ti = 2 if cb < 2 else 0
            rp = rope[tt % 2]
            if rope_loaded.get(tt % 2) != tt:
                P.dma("sp", lambda e: e.dma_start(out=rp[:p, :, :], in_=din["c_rope_o"].h[0:p, tt, :, :]),
                      reads=[din["c_rope_o"].k()], writes=[rp.k()])
                rope_loaded[tt % 2] = tt
            f = qf[a]
            P.op("act", lambda e: e.copy(out=f[:p, :], in_=bk.h[:p, 0:512]), reads=[bk.k()], writes=[f.k()])
            f4 = f[:p, :].rearrange("q (h t f) -> q h t f", h=4, t=2, f=64)
            r4 = r[:p, :].rearrange("q (h t f) -> q h t f", h=4, t=2, f=64)
            x1 = f4[:, :, 0, :]
            x2 = f4[:, :, 1, :]
            cs_ = rp[:p, ti, :].unsqueeze(1).broadcast_to([p, 4, 64])
            sn_ = rp[:p, ti + 1, :].unsqueeze(1).broadcast_to([p, 4, 64])
            ta3 = tA[:p, :].rearrange("q (h f) -> q h f", h=4)
            tb3 = tB_[:p, :].rearrange("q (h f) -> q h f", h=4)
            rk = [f.k(), rp.k()]
            P.op("dve", lambda e: e.tensor_tensor(out=ta3, in0=x1, in1=cs_, op=ALU.mult), reads=rk, writes=[tA.k()])
            P.op("dve", lambda e: e.tensor_tensor(out=tb3, in0=x2, in1=sn_, op=ALU.mult), reads=rk, writes=[tB_.k()])
            P.op("dve", lambda e: e.tensor_tensor(out=r4[:, :, 0, :], in0=ta3, in1=tb3, op=ALU.subtract), reads=[tA.k(), tB_.k()], writes=[r.k(0)])
            P.op("dve", lambda e: e.tensor_tensor(out=ta3, in0=x1, in1=sn_, op=ALU.mult), reads=rk + [r.k(0)], writes=[tA.k()])
            P.op("dve", lambda e: e.tensor_tensor(out=tb3, in0=x2, in1=cs_, op=ALU.mult), reads=rk + [r.k(0)], writes=[tB_.k()])
            P.op("dve", lambda e: e.tensor_tensor(out=r4[:, :, 1, :], in0=ta3, in1=tb3, op=ALU.add), reads=[tA.k(), tB_.k()], writes=[r.k(1)])
        else:
            P.op("act", lambda e: e.copy(out=r[:p, :], in_=bk.h[:p, 0:512]), reads=[bk.k()], writes=[r.k(0), r.k(1)])
        P.dma("sp", lambda e: e.dma_start(out=pd.h[cb * 4:(cb + 1) * 4, 0:p, tt, :].rearrange("g q c -> q g c"), in_=r[:p, :].rearrange("q (g c) -> q g c", c=128)),
              reads=[r.k(0), r.k(1)], writes=[pd.k(cb)])

    blocks = [(i * 512, 512) for i in range(12)] + [(6144, 32)]
    c.need("owin")
    c.pump_until(50)
    c.proj_tok(wb["owin"], blocks, hnT, list(range(17)), consumer, banks=(2, 3, 4, 5), wbufs=wbufs, tiled=c.wt["owin"])
    P.release(m_proj)

    def load_tok(dst, col0, ncol=128, key=None):
        P.dma("sp", lambda e: e.dma_start(out=dst[:, :, :], in_=pd.h[col0 // 128, :, :, :]), reads=[key], writes=[dst.k()])

    tcount = {"n": 0}

    def transpose_tok(src, dstT, scale=None):
        for g0 in range(0, 17, 8):
            tiles = list(range(g0, min(17, g0 + 8)))
            tb = B[6 + tcount["n"] % 2]
            tcount["n"] += 1
            for j, tt in enumerate(tiles):
                s0, p = TT[tt]
                P.op("pe", lambda e, j=j, tt=tt, p=p, tb=tb: e.transpose(out=bf(tb)[:, j * 128:j * 128 + p], in_=src[:p, tt, 0:128], identity=ident[:p, :p]),
                     reads=[src.k(), ident.k()], writes=[tb.k()])
            for j, tt in enumerate(tiles):
                s0, p = TT[tt]
                P.op("act", lambda e, j=j, s0=s0, p=p, tb=tb: e.copy(out=dstT[:, s0:s0 + p], in_=bf(tb)[:, j * 128:j * 128 + p]), reads=[tb.k()], writes=[dstT.k()])

    NL = 3
    lnbs = []
    for _ in range(NL):
        b_ = K()
        b_.o = P.sb("ln_o", [128, 128], F32)
        b_.j = P.sb("ln_j", [128, 128], F32)
        b_.s = P.sb("ln_s", [128, 4], F32)
        lnbs.append(b_)

    def layer_norm(lane, src_t, src_ap, p, gain_ap, gain_t, out_t, out_ap):
        o, j, s = lnbs[lane].o, lnbs[lane].j, lnbs[lane].s
        P.op("act", lambda e: e.activation(out=o[:p, :], in_=src_ap, func=AF.Identity, accum_out=s[:p, 0:1]), reads=[src_t.k()], writes=[o.k(), s.k()])
        yield
        P.op("dve", lambda e: e.tensor_scalar(out=s[:p, 1:2], in0=s[:p, 0:1], scalar1=-1.0 / 128, scalar2=None, op0=ALU.mult), reads=[s.k()], writes=[s.k("m")])
        yield
        P.op("act", lambda e: e.activation(out=j[:p, :], in_=o[:p, :], func=AF.Square, bias=s[:p, 1:2], scale=1.0, accum_out=s[:p, 2:3]),
             reads=[o.k(), s.k("m")], writes=[j.k(), s.k("v")])
        yield
        P.op("act", lambda e: e.activation(out=s[:p, 3:4], in_=s[:p, 2:3], func=AF.Sqrt, bias=cst[:p, 0:1], scale=1.0 / 128), reads=[s.k("v"), cst.k()], writes=[s.k("r")])
        yield
        P.op("dve", lambda e: e.reciprocal(out=s[:p, 3:4], in_=s[:p, 3:4]), reads=[s.k("r")], writes=[s.k("r")])
        yield
        P.op("dve", lambda e: e.tensor_scalar(out=o[:p, :], in0=o[:p, :], scalar1=s[:p, 1:2], scalar2=s[:p, 3:4], op0=ALU.add, op1=ALU.mult),
             reads=[o.k(), s.k("m"), s.k("r")], writes=[o.k()])
        yield
        P.op("dve", lambda e: e.tensor_tensor(out=out_ap, in0=o[:p, :], in1=gain_ap, op=ALU.mult), reads=[o.k(), gain_t.k()], writes=[out_t.k()])
        yield

    def run_lanes(items, make_gen, W):
        it = iter(items)
        free = list(range(W))
        active = []
        while True:
            while free:
                try:
                    x_ = next(it)
                except StopIteration:
                    break
                ln_ = free.pop(0)
                active.append((make_gen(x_, ln_), ln_))
            if not active:
                break
            for ga in list(active):
                try:
                    next(ga[0])
                except StopIteration:
                    active.remove(ga)
                    free.append(ga[1])

    c.pump_until(50)
    m_ret = P.mark()
    cret = P.sb("cret", [128, 8, 128], F32)
    P.dma("sp", lambda e: e.dma_start(out=cret[:, :, :], in_=din["c_ret"].h[:, :, :]), reads=[din["c_ret"].k()], writes=[cret.k()])
    ccol = P.sb("ccol", [128, 4], F32)
    P.dma("sp", lambda e: e.dma_start(out=ccol[:, :], in_=din["c_col"].h[:, :]), reads=[din["c_col"].k()], writes=[ccol.k()])
    lg = P.sb("lg", [128, 16], F32)
    P.dma("sp", lambda e: e.dma_start(out=lg[:, :], in_=din["ret_ld"].h[0:1, :].partition_broadcast(128)), reads=[din["ret_ld"].k()], writes=[lg.k()])
    lg2 = P.sb("lg2", [128, 16], F32)
    P.op("dve", lambda e: e.tensor_scalar(out=lg2[:, :], in0=lg[:, :], scalar1=-1.0, scalar2=None, op0=ALU.mult), reads=[lg.k()], writes=[lg2.k()])
    P.op("dve", lambda e: e.tensor_tensor(out=lg[:, :], in0=lg[:, :], in1=lg2[:, :], op=ALU.min), reads=[lg.k(), lg2.k()], writes=[lg.k()])
    rnorm = P.sb("rnorm", [128, 1024], F32)
    P.dma("sp", lambda e: e.dma_start(out=rnorm[:, :], in_=din["ret_norm"].h[0:1, :].partition_broadcast(128)), reads=[din["ret_norm"].k()], writes=[rnorm.k()])
    rsets = [[P.sb("r_tok", [128, 17, 128], BF16) for _ in range(4)] for _ in range(2)]
    r_ofull = [P.sb("r_ofull", [128, L], BF16) for _ in range(2)]

    def r_loads(h):
        s_ = rsets[h % 2]
        load_tok(s_[0], h * 128, key=pd.k(h // 4))
        load_tok(s_[1], 1024 + h * 128, key=pd.k(2 + h // 4))
        load_tok(s_[2], 2048 + h * 128, key=pd.k(4 + h // 4))
        load_tok(s_[3], 3072 + h * 128, key=pd.k(6 + h // 4))
    qT = P.sb("r_qT", [128, L], BF16)
    kT = P.sb("r_kT", [128, L], BF16)
    DT = P.sb("r_DT", [128, 128], F32)
    tmpm = P.sb("r_tmpm", [128, 128], F32)
    xi = P.sb("r_xi", [128, 3, 128], F32)
    zc = P.sb("r_zc", [128, 4], F32)
    gdec = P.sb("r_g", [128, 2], F32)
    Rf = P.sb("r_Rf", [128, 128], F32)
    Rb = P.sb("r_Rb", [128, 128], F32)
    R16f = P.sb("r_R16f", [128, 17, 128], BF16)
    R16b = P.sb("r_R16b", [128, 17, 128], BF16)
    W_R = 3
    kz = [P.sb("r_kz", [128, 128], BF16) for _ in range(3)]
    kvs = [P.sb("r_kvs", [128, 17, 128], F32) for _ in range(2)]
    PTt = [P.sb("r_PT", [128, 128], BF16) for _ in range(W_R)]
    qx = [P.sb("r_qx", [128, 2, 128], BF16) for _ in range(W_R)]
    yb = [P.sb("r_yb", [128, 128], BF16) for _ in range(W_R)]
    yfl = [P.sb("r_yf", [128, 128], F32) for _ in range(W_R)]
    sgl = [P.sb("r_sg", [128, 128], F32) for _ in range(W_R)]
    oT = [P.sb("r_oT", [128, 128], BF16) for _ in range(W_R)]
    pdk = [pd.k(i) for i in range(12)]
    r_loads(0)
    for h in range(8):
        qtok, ktok, vtok, gtok = rsets[h % 2]
        if h + 1 < 8:
            r_loads(h + 1)
        transpose_tok(qtok, qT)
        transpose_tok(ktok, kT)
        lf_ = lg[:, h:h + 1]
        lb_ = lg[:, 8 + h:9 + h]
        P.op("act", lambda e, lf_=lf_: e.activation(out=DT[:, :], in_=cret[:, 0, :], func=AF.Exp, scale=lf_), reads=[cret.k(), lg.k()], writes=[DT.k()])
        P.op("dve", lambda e: e.tensor_tensor(out=DT[:, :], in0=DT[:, :], in1=cret[:, 2, :], op=ALU.mult), reads=[DT.k(), cret.k()], writes=[DT.k()])
        P.op("act", lambda e, lb_=lb_: e.activation(out=tmpm[:, :], in_=cret[:, 1, :], func=AF.Exp, scale=lb_), reads=[cret.k(), lg.k()], writes=[tmpm.k()])
        P.op("dve", lambda e: e.tensor_tensor(out=tmpm[:, :], in0=tmpm[:, :], in1=cret[:, 3, :], op=ALU.mult), reads=[tmpm.k(), cret.k()], writes=[tmpm.k()])
        P.op("dve", lambda e: e.tensor_tensor(out=DT[:, :], in0=DT[:, :], in1=tmpm[:, :], op=ALU.add), reads=[DT.k(), tmpm.k()], writes=[DT.k()])
        P.op("act", lambda e, lf_=lf_: e.activation(out=xi[:, 0, :], in_=cret[:, 4, :], func=AF.Exp, scale=lf_), reads=[cret.k(), lg.k()], writes=[xi.k()])
        P.op("act", lambda e, lb_=lb_: e.activation(out=xi[:, 1, :], in_=cret[:, 5, :], func=AF.Exp, scale=lb_), reads=[cret.k(), lg.k()], writes=[xi.k()])
        P.op("act", lambda e, lb_=lb_: e.activation(out=xi[:, 2, :], in_=cret[:, 6, :], func=AF.Exp, scale=lb_), reads=[cret.k(), lg.k()], writes=[xi.k()])
        P.op("act", lambda e, lf_=lf_: e.activation(out=zc[:, 0:1], in_=ccol[:, 0:1], func=AF.Exp, scale=lf_), reads=[ccol.k(), lg.k()], writes=[zc.k()])
        P.op("act", lambda e, lf_=lf_: e.activation(out=zc[:, 1:2], in_=ccol[:, 1:2], func=AF.Exp, scale=lf_), reads=[ccol.k(), lg.k()], writes=[zc.k()])
        P.op("act", lambda e, lb_=lb_: e.activation(out=zc[:, 2:3], in_=ccol[:, 2:3], func=AF.Exp, scale=lb_), reads=[ccol.k(), lg.k()], writes=[zc.k()])
        P.op("act", lambda e, lf_=lf_: e.activation(out=gdec[:, 0:1], in_=lf_, func=AF.Exp, scale=128.0), reads=[lg.k()], writes=[gdec.k()])
        P.op("act", lambda e, lb_=lb_: e.activation(out=gdec[:, 1:2], in_=lb_, func=AF.Exp, scale=128.0), reads=[lg.k()], writes=[gdec.k()])
        def kv_gen(item, lane, ktok=ktok, vtok=vtok):
            d, n = item
            s0, p = TT[n]
            z = kz[lane]
            bk = B[lane]
            if d == 0:
                zcol = zc[:p, 1:2] if n == 0 else zc[:p, 0:1]
            else:
                zcol = zc[:p, 2:3]
            P.op("dve", lambda e: e.tensor_scalar(out=z[:p, :], in0=ktok[:p, n, :], scalar1=zcol, scalar2=None, op0=ALU.mult),
                 reads=[ktok.k(), zc.k()], writes=[z.k()])
            yield
            P.op("pe", lambda e: e.matmul(bk.h[:, 0:128], lhsT=z[:p, :], rhs=vtok[:p, n, :], start=True, stop=True),
                 reads=[z.k(), vtok.k()], writes=[bk.k()])
            yield
            P.op("act", lambda e: e.copy(out=kvs[d][:, n, :], in_=bk.h[:, 0:128]), reads=[bk.k()], writes=[kvs[d].k(n)])
            yield

        run_lanes([(0, n) for n in range(16)] + [(1, n) for n in range(16, 0, -1)], kv_gen, 3)

        def rec_gen(d, lane):
            if d == 0:
                for n in range(16):
                    if n == 0:
                        P.op("dve", lambda e: e.tensor_copy(out=Rf[:, :], in_=kvs[0][:, 0, :]), reads=[kvs[0].k(0)], writes=[Rf.k()])
                    else:
                        P.op("dve", lambda e, n=n: e.scalar_tensor_tensor(out=Rf[:, :], in0=Rf[:, :], scalar=gdec[:, 0:1], in1=kvs[0][:, n, :], op0=ALU.mult, op1=ALU.add),
                             reads=[Rf.k(), gdec.k(), kvs[0].k(n)], writes=[Rf.k()])
                    yield
                    P.op("act", lambda e, n=n: e.copy(out=R16f[:, n + 1, :], in_=Rf[:, :]), reads=[Rf.k()], writes=[R16f.k(n + 1)])
                    yield
            else:
                for n in range(16, 0, -1):
                    if n == 16:
                        P.op("dve", lambda e: e.tensor_copy(out=Rb[:, :], in_=kvs[1][:, 16, :]), reads=[kvs[1].k(16)], writes=[Rb.k()])
                    else:
                        P.op("dve", lambda e, n=n: e.scalar_tensor_tensor(out=Rb[:, :], in0=Rb[:, :], scalar=gdec[:, 1:2], in1=kvs[1][:, n, :], op0=ALU.mult, op1=ALU.add),
                             reads=[Rb.k(), gdec.k(), kvs[1].k(n)], writes=[Rb.k()])
                    yield
                    P.op("act", lambda e, n=n: e.copy(out=R16b[:, n - 1, :], in_=Rb[:, :]), reads=[Rb.k()], writes=[R16b.k(n - 1)])
                    yield

        run_lanes([0, 1], rec_gen, 2)

        ofull = r_ofull[h % 2]

        def out_gen(n, lane, h=h, vtok=vtok, gtok=gtok, ofull=ofull):
            if n % 4 == 0:
                c.pump(1)
            s0, p = TT[n]
            Sb = B[2 + 2 * lane] if lane < 3 else None
            Ob = B[3 + 2 * lane]
            pt = PTt[lane]
            x_ = qx[lane]
            yf = yfl[lane]
            sg = sgl[lane]
            y = yb[lane]
            o_ = oT[lane]
            P.op("pe", lambda e: e.matmul(Sb.h[:p, 0:p], lhsT=kT[:, s0:s0 + p], rhs=qT[:, s0:s0 + p], start=True, stop=True),
                 reads=[kT.k(), qT.k()], writes=[Sb.k()])
            yield
            P.op("dve", lambda e: e.tensor_tensor(out=pt[:p, 0:p], in0=Sb.h[:p, 0:p], in1=DT[:p, 0:p], op=ALU.mult),
                 reads=[Sb.k(), DT.k()], writes=[pt.k()])
            yield
            if n >= 1:
                P.op("dve", lambda e: e.tensor_tensor(out=x_[:, 0, 0:p], in0=qT[:, s0:s0 + p], in1=xi[:, 0, 0:p], op=ALU.mult),
                     reads=[qT.k(), xi.k()], writes=[x_.k(0)])
                yield
            if n <= 15:
                xsel = 2 if n == 0 else 1
                P.op("dve", lambda e: e.tensor_tensor(out=x_[:, 1, 0:p], in0=qT[:, s0:s0 + p], in1=xi[:, xsel, 0:p], op=ALU.mult),
                     reads=[qT.k(), xi.k()], writes=[x_.k(1)])
                yield
            P.op("pe", lambda e: e.matmul(Ob.h[:p, 0:128], lhsT=pt[:p, 0:p], rhs=vtok[:p, n, :], start=True, stop=False),
                 reads=[pt.k(), vtok.k()], writes=[Ob.k()])
            if n >= 1:
                P.op("pe", lambda e: e.matmul(Ob.h[:p, 0:128], lhsT=x_[:, 0, 0:p], rhs=R16f[:, n, :], start=False, stop=(n == 16)),
                     reads=[x_.k(0), R16f.k(n)], writes=[Ob.k()])
            if n <= 15:
                P.op("pe", lambda e: e.matmul(Ob.h[:p, 0:128], lhsT=x_[:, 1, 0:p], rhs=R16b[:, n, :], start=False, stop=True),
                     reads=[x_.k(1), R16b.k(n)], writes=[Ob.k()])
            yield
            yield from layer_norm(lane, Ob, Ob.h[:p, 0:128], p, rnorm[:p, h * 128:(h + 1) * 128], rnorm, yf, yf[:p, :])
            P.op("act", lambda e: e.activation(out=sg[:p, :], in_=gtok[:p, n, :], func=AF.Sigmoid), reads=[gtok.k()], writes=[sg.k()])
            yield
            P.op("dve", lambda e: e.tensor_tensor(out=sg[:p, :], in0=sg[:p, :], in1=gtok[:p, n, :], op=ALU.mult), reads=[sg.k(), gtok.k()], writes=[sg.k()])
            yield
            P.op("dve", lambda e: e.tensor_tensor(out=y[:p, :], in0=yf[:p, :], in1=sg[:p, :], op=ALU.mult), reads=[yf.k(), sg.k()], writes=[y.k()])
            yield
            P.op("pe", lambda e: e.transpose(out=bf(Sb)[:, 0:p], in_=y[:p, :], identity=ident[:p, :p]), reads=[y.k(), ident.k()], writes=[Sb.k()])
            yield
            P.op("act", lambda e: e.copy(out=ofull[:, s0:s0 + p], in_=bf(Sb)[:, 0:p]), reads=[Sb.k()], writes=[ofull.k(n)])
            yield

        run_lanes(list(range(17)), out_gen, W_R)
        P.dma("sp", lambda e, h=h, ofull=ofull: e.dma_start(out=mixd.h[h * 128:(h + 1) * 128, :], in_=ofull[:, :]),
              reads=[ofull.k(n_) for n_ in range(17)], writes=[mixd.k(h)])
    P.release(m_ret)
    mlstm_phase(c, gat, load_tok, transpose_tok, layer_norm, run_lanes)
    P.release(m_all)


def mlstm_phase(c, gat, load_tok, transpose_tok, layer_norm, run_lanes):
    P, B, bf, din = c.P, c.B, c.bf, c.din
    ident, identf, ones, cst, mixd, pd = c.ident, c.identf, c.ones, c.cst, c.mixd, c.pd
    m0 = P.mark()
    f32 = F32
    scal = P.sb("m_scal", [128, 17, 2, 5, 8], f32)
    FP = [P.sb("m_FP", [128, 2, 8, 17], f32) for _ in range(2)]
    m_rows = P.mark()
    gb4 = P.sb("m_gb", [8, 4], f32)
    P.dma("sp", lambda e: e.dma_start(out=gb4[:, :], in_=din["ml_gb"].h[:, :]), reads=[din["ml_gb"].k()], writes=[gb4.k()])
    rowI = P.sb("m_rowI", [8, L], f32)
    rowF = P.sb("m_rowF", [8, L], f32)
    crow = P.sb("m_crow", [8, 4, L], f32)
    P.dma("sp", lambda e: e.dma_start(out=crow[:, 0, :], in_=din["c_rows"].h[:, 0, :]), reads=[din["c_rows"].k()], writes=[crow.k()])
    P.dma("sp", lambda e: e.dma_start(out=crow[:, 1, :], in_=din["c_rows"].h[:, 2, :]), reads=[din["c_rows"].k()], writes=[crow.k()])
    tA_, tB2, tC, tD, tE, tO = [P.sb("m_wk", [8, L], f32) for _ in range(6)]
    ch17 = P.sb("m_ch17", [8, 12, 17], f32)
    id8 = P.sb("m_id8", [8, 8], f32)
    P.op("dve", lambda e: e.tensor_copy(out=id8[:, :], in_=identf[0:8, 0:8]), reads=[identf.k()], writes=[id8.k()])
    onesf = P.sb("m_onesf", [8, 128], f32)
    P.op("dve", lambda e: e.memset(onesf[:, :], 1.0), writes=[onesf.k()])
    rhsD = P.sb("m_rhsD", [8, 8, 17], f32)

    def V(fn, reads, writes):
        P.op("dve", fn, reads=[t.k() for t in reads], writes=[t.k() for t in writes])

    def A(fn, reads, writes):
        P.op("act", fn, reads=[t.k() for t in reads], writes=[t.k() for t in writes])

    ntr = {"n": 0}
    for d in range(2):
        for tt in range(17):
            s0, p = TT[tt]
            for kk, dst in ((2 * d, rowI), (2 * d + 1, rowF)):
                tb = B[ntr["n"] % 2]
                ntr["n"] += 1
                P.op("pe", lambda e, tb=tb, tt=tt, p=p, kk=kk: e.transpose(out=tb.h[0:8, 0:p], in_=gat[:p, tt, kk * 8:(kk + 1) * 8], identity=identf[:p, :p]),
                     reads=[gat.k(tt), identf.k()], writes=[tb.k()])
                P.op("act", lambda e, tb=tb, s0=s0, p=p, kk=kk, dst=dst: e.activation(out=dst[:, s0:s0 + p], in_=tb.h[0:8, 0:p], func=AF.Identity,
                                                                                     bias=gb4[:, kk:kk + 1], scale=1.0),
                     reads=[tb.k(), gb4.k()], writes=[dst.k()])
        LI, LF, BT_, CS_, CM = tA_, tB2, tC, tD, tE
        if d == 0:
            V(lambda e: e.tensor_copy(out=LI[:, :], in_=rowI[:, :]), [rowI], [LI])
            V(lambda e: e.tensor_copy(out=LF[:, :], in_=rowF[:, :]), [rowF], [LF])

            def cend(t):
                return [(t[:, 15:16], 0, 1), (t[:, 16:L].rearrange("h (c j) -> h c j", j=128)[:, :, 127], 1, 17)]

            def cview(t):
                return t[:, 16:L].rearrange("h (c j) -> h c j", j=128), t[:, 0:16], 1, 0
        else:
            V(lambda e: e.tensor_copy(out=LI[:, :], in_=rowI[:, ::-1]), [rowI], [LI])
            V(lambda e: e.tensor_copy(out=LF[:, :], in_=rowF[:, ::-1]), [rowF], [LF])

            def cend(t):
                return [(t[:, 0:2048].rearrange("h (c j) -> h c j", j=128)[:, :, 127], 0, 16), (t[:, L - 1:L], 16, 17)]

            def cview(t):
                return t[:, 0:2048].rearrange("h (c j) -> h c j", j=128), t[:, 2048:L], 0, 16
        rm = crow[:, d, :]
        A(lambda e: e.activation(out=LF[:, :], in_=LF[:, :], func=AF.Exp, scale=-1.0), [LF], [LF])
        A(lambda e: e.activation(out=LF[:, :], in_=LF[:, :], func=AF.Ln, bias=1.0, scale=1.0), [LF], [LF])
        V(lambda e: e.tensor_scalar(out=LF[:, :], in0=LF[:, :], scalar1=-1.0, scalar2=None, op0=ALU.mult), [LF], [LF])
        V(lambda e, rm=rm: e.tensor_tensor_scan(out=BT_[:, :], data0=rm, data1=LF[:, :], initial=0.0, op0=ALU.mult, op1=ALU.add), [crow, LF], [BT_])
        V(lambda e: e.tensor_tensor(out=CS_[:, :], in0=LI[:, :], in1=BT_[:, :], op=ALU.subtract), [LI, BT_], [CS_])
        PEN = LF
        V(lambda e, rm=rm: e.tensor_scalar(out=PEN[:, :], in0=rm, scalar1=-1.0, scalar2=1.0e30, op0=ALU.add, op1=ALU.mult), [crow, BT_], [PEN])
        V(lambda e: e.tensor_tensor_scan(out=CM[:, :], data0=PEN[:, :], data1=CS_[:, :], initial=NEGBIG, op0=ALU.add, op1=ALU.max), [PEN, CS_], [CM])
        for (src, c0, c1) in cend(BT_):
            V(lambda e, src=src, c0=c0, c1=c1: e.tensor_copy(out=ch17[:, 0, c0:c1], in_=src), [BT_], [ch17])
        for (src, c0, c1) in cend(CM):
            V(lambda e, src=src, c0=c0, c1=c1: e.tensor_copy(out=ch17[:, 1, c0:c1], in_=src), [CM], [ch17])
        BL, CML, MLOC, MN, MP_, FPV, FLC, T17 = [ch17[:, i, :] for i in range(8)]
        V(lambda e: e.tensor_tensor(out=MLOC, in0=BL, in1=CML, op=ALU.add), [ch17], [ch17])
        V(lambda e: e.tensor_tensor_scan(out=MN, data0=BL, data1=MLOC, initial=0.0, op0=ALU.add, op1=ALU.max), [ch17], [ch17])
        V(lambda e: e.memset(ch17[:, 4, 0:1], 0.0), [ch17], [ch17])
        V(lambda e: e.tensor_copy(out=ch17[:, 4, 1:17], in_=ch17[:, 3, 0:16]), [ch17], [ch17])
        V(lambda e: e.tensor_tensor(out=T17, in0=BL, in1=MP_, op=ALU.add), [ch17], [ch17])
        V(lambda e: e.tensor_tensor(out=T17, in0=T17, in1=MN, op=ALU.subtract), [ch17], [ch17])
        A(lambda e: e.activation(out=FPV, in_=T17, func=AF.Exp), [ch17], [ch17])
        V(lambda e: e.tensor_tensor(out=T17, in0=MLOC, in1=MN, op=ALU.subtract), [ch17], [ch17])
        A(lambda e: e.activation(out=FLC, in_=T17, func=AF.Exp), [ch17], [ch17])
        for wi, srcv in ((0, FPV), (1, FLC)):
            V(lambda e, srcv=srcv: e.tensor_tensor(out=rhsD[:, :, :], in0=srcv.unsqueeze(1).broadcast_to([8, 8, 17]),
                                                   in1=id8[:, :].unsqueeze(2).broadcast_to([8, 8, 17]), op=ALU.mult), [ch17, id8], [rhsD])
            bk = B[2 + wi]
            P.op("pe", lambda e, bk=bk: e.matmul(bk.h[:, 0:136], lhsT=onesf[:, :], rhs=rhsD[:, :, :].rearrange("k h n -> k (h n)"), start=True, stop=True),
                 reads=[onesf.k(), rhsD.k()], writes=[bk.k()])
            P.op("act", lambda e, bk=bk, wi=wi, d=d: e.copy(out=FP[d][:, wi, :, :].rearrange("p h n -> p (h n)"), in_=bk.h[:, 0:136]), reads=[bk.k()], writes=[FP[d].k()])
        MT = CM
        real, metav, cr0, cm0 = cview(MT)
        V(lambda e, real=real, cr0=cr0: e.tensor_tensor(out=real, in0=real, in1=ch17[:, 4, cr0:cr0 + 16].unsqueeze(2).broadcast_to([8, 16, 128]), op=ALU.max), [CM, ch17], [MT])
        V(lambda e, metav=metav, cm0=cm0: e.tensor_scalar(out=metav, in0=metav, scalar1=ch17[:, 4, cm0:cm0 + 1], scalar2=None, op0=ALU.max), [CM, ch17], [MT])
        tmp5 = tA_

        def fin(kd, compute, d=d, tmp5=tmp5):
            if d == 0:
                compute(tO)
            else:
                compute(tmp5)
                V(lambda e: e.tensor_copy(out=tO[:, :], in_=tmp5[:, ::-1]), [tmp5], [tO])
            tb = B[4 + kd % 2]
            for tt in range(17):
                s0, p = TT[tt]
                P.op("pe", lambda e, tb=tb, tt=tt, s0=s0, p=p: e.transpose(out=tb.h[:p, tt * 8:(tt + 1) * 8], in_=tO[:, s0:s0 + p], identity=id8[:, :]),
                     reads=[tO.k(), id8.k()], writes=[tb.k()])
            P.op("act", lambda e, tb=tb, d=d, kd=kd: e.copy(out=scal[:, :, d, kd, :], in_=tb.h[:, 0:136].rearrange("p (t h) -> p t h", h=8)), reads=[tb.k()], writes=[scal.k()])

        fin(0, lambda t: A(lambda e: e.activation(out=t[:, :], in_=CS_[:, :], func=AF.Exp), [CS_], [t]))
        fin(1, lambda t: A(lambda e: e.activation(out=t[:, :], in_=MT[:, :], func=AF.Exp, scale=-1.0), [MT], [t]))

        def winter(t, cview=cview, MT=MT):
            r_, m_, cr0_, cm0_ = cview(t)
            rmt, mmt, _, _ = cview(MT)
            V(lambda e: e.tensor_tensor(out=r_, in0=ch17[:, 4, cr0_:cr0_ + 16].unsqueeze(2).broadcast_to([8, 16, 128]), in1=rmt, op=ALU.subtract), [MT, ch17], [t])
            V(lambda e: e.tensor_scalar(out=m_, in0=mmt, scalar1=-1.0, scalar2=ch17[:, 4, cm0_:cm0_ + 1], op0=ALU.mult, op1=ALU.add), [MT, ch17], [t])
            A(lambda e: e.activation(out=t[:, :], in_=t[:, :], func=AF.Exp), [t], [t])
        fin(2, winter)

        def enegm(t, MT=MT):
            V(lambda e: e.tensor_tensor(out=t[:, :], in0=BT_[:, :], in1=MT[:, :], op=ALU.add), [BT_, MT], [t])
            A(lambda e: e.activation(out=t[:, :], in_=t[:, :], func=AF.Exp, scale=-1.0), [t], [t])
        fin(3, enegm)

        def wsum(t, cview=cview):
            r_, m_, cr0_, cm0_ = cview(t)
            rcs, mcs, _, _ = cview(CS_)
            V(lambda e: e.tensor_tensor(out=r_, in0=rcs, in1=ch17[:, 1, cr0_:cr0_ + 16].unsqueeze(2).broadcast_to([8, 16, 128]), op=ALU.subtract), [CS_, ch17], [t])
            V(lambda e: e.tensor_scalar(out=m_, in0=mcs, scalar1=ch17[:, 1, cm0_:cm0_ + 1], scalar2=None, op0=ALU.subtract), [CS_, ch17], [t])
            A(lambda e: e.activation(out=t[:, :], in_=t[:, :], func=AF.Exp), [t], [t])
        fin(4, wsum)
    P.release(m_rows)

    cw_ = P.sb("m_cw", [128, 8, 5], f32)
    cb_ = P.sb("m_cb", [128, 8], f32)
    P.dma("sp", lambda e: e.dma_start(out=cw_[:, :, :], in_=din["ml_cw"].h[:, :, :]), reads=[din["ml_cw"].k()], writes=[cw_.k()])
    P.dma("sp", lambda e: e.dma_start(out=cb_[:, :], in_=din["ml_cb"].h[:, :]), reads=[din["ml_cb"].k()], writes=[cb_.k()])
    wq = P.sb("m_wq", [128, 8, 128], BF16)
    wkk = P.sb("m_wk2", [128, 8, 128], BF16)
    wv = P.sb("m_wv", [128, 8, 128], BF16)
    for nm_, t_ in (("ml_wq", wq), ("ml_wk", wkk), ("ml_wv", wv)):
        P.dma("pool", lambda e, nm_=nm_, t_=t_: e.dma_start(out=t_[:, :, :], in_=din[nm_].h[:, :, :].rearrange("h d e -> d h e")), reads=[din[nm_].k()], writes=[t_.k()])
    mnorm = P.sb("m_norm", [128, 1024], f32)
    P.dma("sp", lambda e: e.dma_start(out=mnorm[:, :], in_=din["ml_norm"].h[0:1, :].partition_broadcast(128)), reads=[din["ml_norm"].k()], writes=[mnorm.k()])
    cmask = P.sb("m_mask", [128, 2, 128], f32)
    P.dma("sp", lambda e: e.dma_start(out=cmask[:, :, :], in_=din["c_mmask"].h[:, :, :]), reads=[din["c_mmask"].k()], writes=[cmask.k()])
    msets = [[P.sb("m_tok", [128, 17, 128], BF16) for _ in range(2)] for _ in range(2)]
    m_ofull = [P.sb("m_ofull", [128, L], BF16) for _ in range(2)]

    def m_loads(h):
        s_ = msets[h % 2]
        load_tok(s_[0], 4096 + h * 128, key=pd.k(8 + h // 4))
        load_tok(s_[1], 5120 + h * 128, key=pd.k(10 + h // 4))
    muT = P.sb("m_muT", [128, L], BF16)
    acc = P.sb("m_acc", [128, L], f32)
    sgm = P.sb("m_sgm", [128, L], f32)
    ucT = P.sb("m_ucT", [128, L], BF16)
    mqT = P.sb("m_mqT", [128, L], BF16)
    mkT = P.sb("m_mkT", [128, L], BF16)
    ktok = P.sb("m_ktok", [128, 17, 128], BF16)
    vaug = P.sb("m_vaug", [128, 17, 129], BF16)
    Cst = [P.sb("m_Cst", [128, 129], f32) for _ in range(2)]
    C16 = [P.sb("m_C16", [128, 17, 129], BF16) for _ in range(2)]
    cls = [P.sb("m_cls", [128, 17, 129], f32) for _ in range(2)]
    kw = [P.sb("m_kw", [128, 128], BF16) for _ in range(2)]
    W_M = 2
    lanes = []
    for _ in range(W_M):
        b_ = K()
        b_.va = [P.sb("m_va", [128, 129], BF16) for _ in range(2)]
        b_.pt = [P.sb("m_PT", [128, 128], BF16) for _ in range(2)]
        b_.t129 = [P.sb("m_t129", [128, 129], f32) for _ in range(2)]
        b_.num = [P.sb("m_num", [128, 129], f32) for _ in range(2)]
        b_.dn = [P.sb("m_dn", [128, 2], f32) for _ in range(2)]
        b_.hsum = P.sb("m_hsum", [128, 128], f32)
        b_.yf = P.sb("m_yf", [128, 128], f32)
        b_.sg = P.sb("m_sg", [128, 128], f32)
        b_.yb = P.sb("m_yb", [128, 128], BF16)
        b_.oT = P.sb("m_oT", [128, 128], BF16)
        lanes.append(b_)
    TBL = [(0, 16)] + [(16 + 512 * j, 512) for j in range(4)]
    m_loads(0)
    for h in range(8):
        mutok, motok = msets[h % 2]
        if h + 1 < 8:
            m_loads(h + 1)
        transpose_tok(mutok, muT)
        P.op("dve", lambda e, h=h: e.tensor_scalar(out=acc[:, :], in0=muT[:, :], scalar1=cw_[:, h, 2:3], scalar2=cb_[:, h:h + 1], op0=ALU.mult, op1=ALU.add),
             reads=[muT.k(), cw_.k(), cb_.k()], writes=[acc.k()])
        for k in (0, 1, 3, 4):
            sh = k - 2
            lo = max(0, -sh)
            hi = min(L, L - sh)
            P.op("dve", lambda e, h=h, k=k, lo=lo, hi=hi, sh=sh: e.scalar_tensor_tensor(out=acc[:, lo:hi], in0=muT[:, lo + sh:hi + sh], scalar=cw_[:, h, k:k + 1],
                                                                                       in1=acc[:, lo:hi], op0=ALU.mult, op1=ALU.add),
                 reads=[muT.k(), cw_.k(), acc.k()], writes=[acc.k()])
        P.op("act", lambda e: e.activation(out=sgm[:, :], in_=acc[:, :], func=AF.Sigmoid), reads=[acc.k()], writes=[sgm.k()])
        P.op("dve", lambda e: e.tensor_tensor(out=ucT[:, :], in0=acc[:, :], in1=sgm[:, :], op=ALU.mult), reads=[acc.k(), sgm.k()], writes=[ucT.k()])
        nb = 0
        for (t0, n_) in TBL:
            for (wsel, dst, sc) in ((wq, mqT, 1.0), (wkk, mkT, 128.0 ** -0.5)):
                bk = B[nb % 2]
                nb += 1
                P.op("pe", lambda e, bk=bk, wsel=wsel, h=h, t0=t0, n_=n_: e.matmul(bk.h[:, 0:n_], lhsT=wsel[:, h, :], rhs=ucT[:, t0:t0 + n_], start=True, stop=True),
                     reads=[wsel.k(), ucT.k()], writes=[bk.k()])
                P.op("act", lambda e, bk=bk, dst=dst, sc=sc, t0=t0, n_=n_: e.activation(out=dst[:, t0:t0 + n_], in_=bk.h[:, 0:n_], func=AF.Copy, scale=sc),
                     reads=[bk.k()], writes=[dst.k()])
        P.op("dve", lambda e: e.memset(vaug[:, :, 128:129], 1.0), writes=[vaug.k()])
        for tt in range(17):
            s0, p = TT[tt]
            bk = B[nb % 2]
            nb += 1
            P.op("pe", lambda e, bk=bk, s0=s0, p=p, h=h: e.matmul(bk.h[:p, 0:128], lhsT=ucT[:, s0:s0 + p], rhs=wkk[:, h, :], start=True, stop=True),
                 reads=[ucT.k(), wkk.k()], writes=[bk.k()])
            P.op("act", lambda e, bk=bk, tt=tt, p=p: e.activation(out=ktok[:p, tt, :], in_=bk.h[:p, 0:128], func=AF.Copy, scale=128.0 ** -0.5), reads=[bk.k()], writes=[ktok.k()])
            bk = B[nb % 2]
            nb += 1
            P.op("pe", lambda e, bk=bk, s0=s0, p=p, h=h: e.matmul(bk.h[:p, 0:128], lhsT=muT[:, s0:s0 + p], rhs=wv[:, h, :], start=True, stop=True),
                 reads=[muT.k(), wv.k()], writes=[bk.k()])
            P.op("act", lambda e, bk=bk, tt=tt, p=p: e.copy(out=vaug[:p, tt, 0:128], in_=bk.h[:p, 0:128]), reads=[bk.k()], writes=[vaug.k()])
        orders = [list(range(17)), [16 - c_ for c_ in range(16)] + [0]]

        def cl_gen(item, lane, h=h):
            d, tt = item
            s0, p = TT[tt]
            kw_ = kw[lane]
            bk = B[lane]
            P.op("dve", lambda e: e.tensor_scalar(out=kw_[:p, :], in0=ktok[:p, tt, :], scalar1=scal[:p, tt, d, 4, h:h + 1], scalar2=None, op0=ALU.mult),
                 reads=[ktok.k(), scal.k()], writes=[kw_.k()])
            yield
            P.op("pe", lambda e: e.matmul(bk.h[:, 0:129], lhsT=kw_[:p, :], rhs=vaug[:p, tt, :], start=True, stop=True),
                 reads=[kw_.k(), vaug.k()], writes=[bk.k()])
            yield
            P.op("act", lambda e: e.copy(out=cls[d][:, tt, :], in_=bk.h[:, 0:129]), reads=[bk.k()], writes=[cls[d].k(tt)])
            yield

        run_lanes([(d, tt) for d in range(2) for tt in orders[d][:16]], cl_gen, 2)

        def st_gen(d, lane, h=h):
            P.op("dve", lambda e: e.memset(Cst[d][:, :], 0.0), writes=[Cst[d].k()])
            yield
            for ci, tt in enumerate(orders[d]):
                P.op("act", lambda e, tt=tt: e.copy(out=C16[d][:, tt, :], in_=Cst[d][:, :]), reads=[Cst[d].k()], writes=[C16[d].k(tt)])
                yield
                if ci == 16:
                    break
                P.op("dve", lambda e, ci=ci: e.tensor_scalar(out=Cst[d][:, :], in0=Cst[d][:, :], scalar1=FP[d][:, 0, h, ci:ci + 1], scalar2=None, op0=ALU.mult),
                     reads=[Cst[d].k(), FP[d].k()], writes=[Cst[d].k()])
                yield
                P.op("dve", lambda e, ci=ci, tt=tt: e.scalar_tensor_tensor(out=Cst[d][:, :], in0=cls[d][:, tt, :], scalar=FP[d][:, 1, h, ci:ci + 1], in1=Cst[d][:, :],
                                                                          op0=ALU.mult, op1=ALU.add),
                     reads=[Cst[d].k(), FP[d].k(), cls[d].k(tt)], writes=[Cst[d].k()])
                yield

        run_lanes([0, 1], st_gen, 2)

        ofull = m_ofull[h % 2]

        def ob_gen(tt, lane, h=h, motok=motok, ofull=ofull):
            s0, p = TT[tt]
            L_ = lanes[lane]
            Sb = B[2 + 3 * lane]
            Ib = B[3 + 3 * lane]
            Eb = B[4 + 3 * lane]
            P.op("pe", lambda e: e.matmul(Sb.h[:p, 0:p], lhsT=mkT[:, s0:s0 + p], rhs=mqT[:, s0:s0 + p], start=True, stop=True),
                 reads=[mkT.k(), mqT.k()], writes=[Sb.k()])
            yield
            for d in range(2):
                pt = L_.pt[d]
                va_ = L_.va[d]
                P.op("dve", lambda e, pt=pt, d=d: e.tensor_tensor(out=pt[:p, 0:p], in0=Sb.h[:p, 0:p], in1=cmask[:p, d, 0:p], op=ALU.mult),
                     reads=[Sb.k(), cmask.k()], writes=[pt.k()])
                yield
                P.op("dve", lambda e, va_=va_, d=d: e.tensor_scalar(out=va_[:p, :], in0=vaug[:p, tt, :], scalar1=scal[:p, tt, d, 0, h:h + 1], scalar2=None, op0=ALU.mult),
                     reads=[vaug.k(), scal.k()], writes=[va_.k()])
                yield
            for d in range(2):
                pt = L_.pt[d]
                va_ = L_.va[d]
                c0 = d * 256
                P.op("pe", lambda e, pt=pt, va_=va_, c0=c0: e.matmul(Ib.h[:p, c0:c0 + 129], lhsT=pt[:p, 0:p], rhs=va_[:p, :], start=True, stop=True),
                     reads=[pt.k(), va_.k()], writes=[Ib.k()])
                P.op("pe", lambda e, d=d, c0=c0: e.matmul(Eb.h[:p, c0:c0 + 129], lhsT=mqT[:, s0:s0 + p], rhs=C16[d][:, tt, :], start=True, stop=True),
                     reads=[mqT.k(), C16[d].k(tt)], writes=[Eb.k()])
            yield
            for d in range(2):
                c0 = d * 256
                t129, num, dn = L_.t129[d], L_.num[d], L_.dn[d]
                P.op("act", lambda e, d=d, c0=c0, t129=t129: e.activation(out=t129[:p, :], in_=Ib.h[:p, c0:c0 + 129], func=AF.Copy, scale=scal[:p, tt, d, 1, h:h + 1]),
                     reads=[Ib.k(), scal.k()], writes=[t129.k()])
                yield
                P.op("dve", lambda e, d=d, c0=c0, t129=t129, num=num: e.scalar_tensor_tensor(out=num[:p, :], in0=Eb.h[:p, c0:c0 + 129], scalar=scal[:p, tt, d, 2, h:h + 1], in1=t129[:p, :],
                                                                                            op0=ALU.mult, op1=ALU.add),
                     reads=[Eb.k(), scal.k(), t129.k()], writes=[num.k()])
                yield
                P.op("dve", lambda e, num=num, dn=dn: e.tensor_scalar(out=dn[:p, 0:1], in0=num[:p, 128:129], scalar1=-1.0, scalar2=None, op0=ALU.mult),
                     reads=[num.k()], writes=[dn.k()])
                yield
                P.op("dve", lambda e, num=num, dn=dn: e.tensor_tensor(out=dn[:p, 0:1], in0=dn[:p, 0:1], in1=num[:p, 128:129], op=ALU.max),
                     reads=[num.k(), dn.k()], writes=[dn.k()])
                yield
                P.op("dve", lambda e, d=d, dn=dn: e.tensor_scalar(out=dn[:p, 0:1], in0=dn[:p, 0:1], scalar1=scal[:p, tt, d, 3, h:h + 1], scalar2=None, op0=ALU.max),
                     reads=[dn.k(), scal.k()], writes=[dn.k()])
                yield
                P.op("dve", lambda e, dn=dn: e.reciprocal(out=dn[:p, 1:2], in_=dn[:p, 0:1]), reads=[dn.k()], writes=[dn.k("r")])
                yield
            hsum = L_.hsum
            P.op("dve", lambda e: e.tensor_scalar(out=hsum[:p, :], in0=L_.num[0][:p, 0:128], scalar1=L_.dn[0][:p, 1:2], scalar2=None, op0=ALU.mult),
                 reads=[L_.num[0].k(), L_.dn[0].k("r")], writes=[hsum.k()])
            yield
            P.op("dve", lambda e: e.scalar_tensor_tensor(out=hsum[:p, :], in0=L_.num[1][:p, 0:128], scalar=L_.dn[1][:p, 1:2], in1=hsum[:p, :], op0=ALU.mult, op1=ALU.add),
                 reads=[L_.num[1].k(), L_.dn[1].k("r"), hsum.k()], writes=[hsum.k()])
            yield
            yf, sg, y, o_ = L_.yf, L_.sg, L_.yb, L_.oT
            yield from layer_norm(lane, hsum, hsum[:p, :], p, mnorm[:p, h * 128:(h + 1) * 128], mnorm, yf, yf[:p, :])
            P.op("act", lambda e: e.activation(out=sg[:p, :], in_=motok[:p, tt, :], func=AF.Sigmoid), reads=[motok.k()], writes=[sg.k()])
            yield
            P.op("dve", lambda e: e.tensor_tensor(out=y[:p, :], in0=yf[:p, :], in1=sg[:p, :], op=ALU.mult), reads=[yf.k(), sg.k()], writes=[y.k()])
            yield
            P.op("pe", lambda e: e.transpose(out=bf(Sb)[:, 0:p], in_=y[:p, :], identity=ident[:p, :p]), reads=[y.k(), ident.k()], writes=[Sb.k()])
            yield
            P.op("act", lambda e: e.copy(out=ofull[:, s0:s0 + p], in_=bf(Sb)[:, 0:p]), reads=[Sb.k()], writes=[ofull.k(tt)])
            yield

        run_lanes(list(range(17)), ob_gen, W_M)
        P.dma("sp", lambda e, h=h, ofull=ofull: e.dma_start(out=mixd.h[(8 + h) * 128:(9 + h) * 128, :], in_=ofull[:, :]),
              reads=[ofull.k(n_) for n_ in range(17)], writes=[mixd.k(8 + h)])
    P.release(m0)


def consts():
    c = {}
    c["c_ident"] = np.eye(128, dtype=np.float32)
    n_tok = 2048
    rows = n_tok // 64
    row = np.concatenate([-np.ones(16, np.float32), np.repeat(np.arange(rows, dtype=np.float32), 64)])
    col = np.concatenate([np.arange(16, dtype=np.float32), np.tile(np.arange(64, dtype=np.float32), rows)])
    f_axis = (10000.0 ** (-np.arange(32, dtype=np.float32) / 32)).astype(np.float32)
    ang_row = row[:, None] * f_axis[None, :]
    ang_col = col[:, None] * f_axis[None, :]
    ang = np.concatenate([ang_row, ang_col], axis=1)
    rope = np.zeros((128, 17, 2, 64), np.float32)
    for tt, (s0, p) in enumerate(TT):
        rope[:p, tt, 0, :] = np.cos(ang[s0:s0 + p])
        rope[:p, tt, 1, :] = np.sin(ang[s0:s0 + p])
    c["c_rope_e"] = rope
    tau = np.arange(L)
    c["c_tau"] = np.ascontiguousarray(np.broadcast_to(np.stack([tau % 128, tau // 128]).astype(np.float32)[None], (128, 2, L)))
    fl = (10000.0 ** (-np.arange(64, dtype=np.float32) / 64)).astype(np.float32)
    angl = np.arange(L, dtype=np.float32)[:, None] * fl[None, :]
    ro = np.zeros((128, 17, 4, 64), np.float32)
    sc = np.float32(128.0 ** -0.5)
    for tt, (s0, p) in enumerate(TT):
        ro[:p, tt, 0, :] = np.cos(angl[s0:s0 + p]); ro[:p, tt, 1, :] = np.sin(angl[s0:s0 + p])
        ro[:p, tt, 2, :] = np.cos(angl[s0:s0 + p]) * sc; ro[:p, tt, 3, :] = np.sin(angl[s0:s0 + p]) * sc
    c["c_rope_o"] = ro
    s_ = np.arange(128, dtype=np.float32)[:, None]; q_ = np.arange(128, dtype=np.float32)[None, :]
    cr = np.zeros((128, 8, 128), np.float32)
    cr[:, 0] = np.maximum(q_ - s_, 0); cr[:, 1] = np.maximum(s_ - q_, 0)
    cr[:, 2] = (q_ >= s_); cr[:, 3] = (q_ < s_)
    cr[:, 4] = q_ + 1 + 0 * s_; cr[:, 5] = 128 - q_ + 0 * s_; cr[:, 6] = np.maximum(16 - q_, 0) + 0 * s_
    c["c_ret"] = cr
    cc = np.zeros((128, 4), np.float32)
    cc[:, 0] = 127 - s_[:, 0]; cc[:, 1] = np.maximum(15 - s_[:, 0], 0); cc[:, 2] = s_[:, 0]
    c["c_col"] = cc
    rows = np.zeros((8, 4, L), np.float32)
    st_f = np.zeros(L, bool); st_f[0] = True; st_f[16::128] = True
    st_b = np.zeros(L, bool); st_b[0:2048:128] = True; st_b[2048] = True
    rows[:, 0] = np.where(st_f, 0.0, 1.0); rows[:, 1] = np.where(st_f, -1e30, 0.0)
    rows[:, 2] = np.where(st_b, 0.0, 1.0); rows[:, 3] = np.where(st_b, -1e30, 0.0)
    c["c_rows"] = rows
    mm = np.zeros((128, 2, 128), np.float32)
    mm[:, 0] = (s_ <= q_); mm[:, 1] = (s_ >= q_)
    c["c_mmask"] = mm
    return c

def s5_layout(lam_re, lam_im, log_dt, b_re, b_im, c_re, c_im, d, glu_w, glu_b):
    o = {}
    def sp(a):
        return a.reshape(2, 16, 2, 64).transpose(2, 3, 0, 1).reshape(128, 32)
    dt = np.broadcast_to(log_dt.reshape(2, 16, 2, 1), (2, 16, 2, 64))
    dt = dt.transpose(2, 3, 0, 1).reshape(128, 32)
    o["s5p"] = np.ascontiguousarray(np.stack([sp(lam_re), sp(lam_im), dt], axis=1).astype(np.float32))
    def bp(a):
        return a.reshape(2, 16, 2, 64, 16).transpose(2, 3, 0, 1, 4).reshape(128, 32, 16)
    o["s5b"] = np.ascontiguousarray(np.stack([bp(b_re), bp(b_im)], axis=1).astype(np.float32))
    def cp(a):
        return a.reshape(2, 16, 2, 16, 64).transpose(2, 4, 0, 1, 3).reshape(128, 32, 16)
    o["s5c"] = np.ascontiguousarray(np.stack([cp(c_re), cp(c_im)], axis=1).astype(np.float32))
    o["s5d"] = np.ascontiguousarray(d.reshape(4, 128).T.astype(np.float32))
    o["s5gb"] = np.ascontiguousarray(glu_b.reshape(4, 128).T.astype(np.float32))
    o["s5gw"] = np.ascontiguousarray(glu_w.astype(np.float32))
    return o

def make_in_maps(inp, cores=range(8)):
    c = consts()
    shared = dict(c)
    shared["meta"] = np.ascontiguousarray(inp["meta_tokens"])
    shared["gains"] = np.ascontiguousarray(inp["norm_gains"].reshape(8, 2048))
    for i in range(2):
        shared["w1_%d" % i] = np.ascontiguousarray(inp["mlp_w1"][i])
        shared["w2_%d" % i] = np.ascontiguousarray(inp["mlp_w2"][i])
    shared["ewin"] = np.ascontiguousarray(inp["even_w_in"][0])
    shared["ewout"] = np.ascontiguousarray(inp["even_w_out"][0])
    shared["owin"] = np.ascontiguousarray(inp["odd_w_in"][0])
    shared["owout"] = np.ascontiguousarray(inp["odd_w_out"][0])
    shared["qkg"] = np.ascontiguousarray(np.stack([inp["att_q_norm"][0], inp["att_k_norm"][0]]))
    shared.update(s5_layout(inp["s5_lam_re"][0], inp["s5_lam_im"][0], inp["s5_log_dt"][0], inp["s5_b_re"][0], inp["s5_b_im"][0],
                            inp["s5_c_re"][0], inp["s5_c_im"][0], inp["s5_d"][0], inp["s5_glu_w"][0], inp["s5_glu_b"][0]))
    shared["ret_ld"] = np.ascontiguousarray(inp["ret_log_decay"][0].reshape(1, 16))
    shared["ret_norm"] = np.ascontiguousarray(inp["ret_norm"][0].reshape(1, 1024))
    shared["ml_gb"] = np.ascontiguousarray(inp["ml_gate_b"][0].T)
    shared["ml_cw"] = np.ascontiguousarray(inp["ml_conv_w"][0].reshape(5, 8, 128).transpose(2, 1, 0))
    shared["ml_cb"] = np.ascontiguousarray(inp["ml_conv_b"][0].reshape(8, 128).T)
    shared["ml_wq"] = np.ascontiguousarray(inp["ml_wq"][0]); shared["ml_wk"] = np.ascontiguousarray(inp["ml_wk"][0]); shared["ml_wv"] = np.ascontiguousarray(inp["ml_wv"][0])
    shared["ml_norm"] = np.ascontiguousarray(inp["ml_norm"][0].reshape(1, 1024))
    maps = []
    for b in cores:
        m = dict(shared)
        m["x"] = np.ascontiguousarray(inp["x"][b])
        maps.append(m)
    return maps


def kernel(**inputs):
    inp = {k: np.asarray(v) for k, v in inputs.items()}
    nc, st = build(stop_after=6)
    maps = make_in_maps(inp, cores=range(8))
    res = run_bass_kernel_spmd(nc, maps, core_ids=list(range(8)))
    outs = [np.asarray(r["out"]).astype(np.float32) for r in res.results]
    return np.stack(outs, axis=0)
```

```python
import numpy as np
import concourse.bass as bass
import concourse.mybir as mybir

F32 = mybir.dt.float32
BF16 = mybir.dt.bfloat16
ALU = mybir.AluOpType
AF = mybir.ActivationFunctionType
AX = mybir.AxisListType

EPOCH = 30000
NSLOT = 8


class _Op(object):
    __slots__ = ("fn", "waits", "signal", "stream", "idx", "is_dma")


class Prog(object):
    ENGS = ("pe", "act", "dve", "pool", "sp")

    def __init__(self, nc):
        self.nc = nc
        self.ops = {e: [] for e in self.ENGS}
        self.known = {e: {} for e in self.ENGS}
        self.state = {}
        self.tkeys = {}
        self.inherit = {}
        self.cnt = {}
        self.streamops = {}
        self.dma_rr = {"sp": 0, "pool": 0, "act": 0}
        self.regions = []
        self.top = 17408
        self.n_sb = 0
        self.dram_out_streams = set()

    def sb(self, name, shape, dtype, keep=False):
        esz = 2 if dtype == BF16 else 4
        nbytes = int(np.prod(shape[1:])) * esz
        nbytes = (nbytes + 63) // 64 * 64
        lo = self.top
        hi = lo + nbytes
        assert hi <= 229312, ("SBUF overflow", name, hi)
        self.top = hi
        self.n_sb += 1
        uname = "%s_%d" % (name, self.n_sb)
        t = self.nc.alloc_sbuf_tensor_at(uname, list(shape), dtype, offset=lo)
        inh = {}
        for (l2, h2, n2) in self.regions:
            if l2 < hi and lo < h2:
                for k in self.tkeys.get(n2, ()):
                    w, r = self.state[k]
                    if w is not None:
                        if inh.get(w[0], -1) < w[1]:
                            inh[w[0]] = w[1]
                    for s, i in r.items():
                        if inh.get(s, -1) < i:
                            inh[s] = i
                for s, i in self.inherit.get(n2, {}).items():
                    if inh.get(s, -1) < i:
                        inh[s] = i
        self.regions = [(l2, h2, n2) for (l2, h2, n2) in self.regions
                        if not (l2 >= lo and h2 <= hi)]
        self.regions.append((lo, hi, uname))
        self.inherit[uname] = inh
        return T(t, uname)

    def mark(self):
        return self.top

    def release(self, m):
        self.top = m

    def _deps(self, stream, reads, writes, same_raw):
        deps = {}

        def add(s, i):
            if deps.get(s, -1) < i:
                deps[s] = i

        for k in reads:
            st = self._st(k)
            w = st[0]
            if w is not None:
                if w[0] != stream or same_raw:
                    add(w[0], w[1])
            else:
                for s, i in st[1].items():
                    pass
        for k in writes:
            st = self._st(k)
            w = st[0]
            if w is not None and w[0] != stream:
                add(w[0], w[1])
            for s, i in st[1].items():
                if s != stream:
                    add(s, i)
        return deps

    def _st(self, k):
        st = self.state.get(k)
        if st is None:
            tn = k[0]
            st = [None, dict(self.inherit.get(tn, {}))]
            self.state[k] = st
            self.tkeys.setdefault(tn, set()).add(k)
        return st

    def _commit(self, stream, idx, reads, writes):
        for k in reads:
            st = self._st(k)
            if st[1].get(stream, -1) < idx:
                st[1][stream] = idx
        for k in writes:
            st = self._st(k)
            st[0] = (stream, idx)
            st[1] = {}

    def op(self, eng, fn, reads=(), writes=()):
        stream = eng
        idx = self.cnt.get(stream, 0)
        self.cnt[stream] = idx + 1
        deps = self._deps(stream, reads, writes, same_raw=(eng != "pe"))
        o = _Op()
        o.fn = fn
        o.stream = stream
        o.idx = idx
        o.signal = False
        o.is_dma = False
        o.waits = self._prune(eng, deps)
        self.ops[eng].append(o)
        self.streamops.setdefault(stream, []).append(o)
        self._commit(stream, idx, reads, writes)
        return o

    def dma(self, q, fn, reads=(), writes=(), is_out=False, grp="m"):
        nsl = NSLOT if grp == "m" else 4
        rk = q + grp
        slot = self.dma_rr.get(rk, 0) % nsl
        self.dma_rr[rk] = self.dma_rr.get(rk, 0) + 1
        stream = "dq:%s:%s%d" % (q, grp, slot)
        idx = self.cnt.get(stream, 0)
        self.cnt[stream] = idx + 1
        deps = self._deps(stream, reads, writes, same_raw=True)
        if idx > 0:
            if deps.get(stream, -1) < idx - 1:
                deps[stream] = idx - 1
        o = _Op()
        o.fn = fn
        o.stream = stream
        o.idx = idx
        o.signal = True
        o.is_dma = True
        o.waits = self._prune(q, deps)
        self.ops[q].append(o)
        self.streamops.setdefault(stream, []).append(o)
        self._commit(stream, idx, reads, writes)
        if is_out:
            self.dram_out_streams.add(stream)
        return o

    def _prune(self, issuer, deps):
        kn = self.known[issuer]
        out = []
        for s, i in deps.items():
            if kn.get(s, -1) < i:
                kn[s] = i
                out.append((s, i))
        return out

    def finish(self):
        deps = {}
        for s, n in self.cnt.items():
            if s.startswith("dq:"):
                deps[s] = n - 1
        o = _Op()
        o.fn = None
        o.stream = None
        o.idx = -1
        o.signal = False
        o.is_dma = False
        o.waits = self._prune("sp", deps)
        self.ops["sp"].append(o)

    def emit(self, stack):
        nc = self.nc
        for e in self.ENGS:
            for o in self.ops[e]:
                for (s, i) in o.waits:
                    self.streamops[s][i].signal = True
        semval = {}
        sems = {}
        for s, lst in self.streamops.items():
            if s.startswith("dq:"):
                sem = stack.enter_context(nc.semaphore("s_" + s.replace(":", "_")))
                sems[s] = [sem]
                for o in lst:
                    semval[(s, o.idx)] = (sem, 16 * (o.idx + 1))
            else:
                c = 0
                sems[s] = []
                for o in lst:
                    if o.signal:
                        ep = c // EPOCH
                        if ep >= len(sems[s]):
                            sems[s].append(stack.enter_context(
                                nc.semaphore("s_%s_%d" % (s, ep))))
                        semval[(s, o.idx)] = (sems[s][ep], c % EPOCH + 1)
                        c += 1
        block = stack.enter_context(nc.Block())
        regs = {"pe": block.tensor, "act": block.scalar, "dve": block.vector,
                "pool": block.gpsimd, "sp": block.sync}
        for e in self.ENGS:
            lst = self.ops[e]
            if not lst:
                continue

            def body(eng, lst=lst):
                for o in lst:
                    for (s, i) in o.waits:
                        sem, val = semval[(s, i)]
                        eng.wait_ge(sem, val)
                    if o.fn is None:
                        continue
                    inst = o.fn(eng)
                    if o.signal:
                        sem, val = semval[(o.stream, o.idx)]
                        inst.then_inc(sem, 16 if o.is_dma else 1)
            regs[e](body)

    def stats(self):
        return {e: len(self.ops[e]) for e in self.ENGS}


class T(object):
    def __init__(self, h, name):
        self.h = h
        self.name = name

    def k(self, *sub):
        return (self.name,) + tuple(sub)

    def __getitem__(self, idx):
        return self.h[idx]

import math
import numpy as np
from contextlib import ExitStack
import concourse.bass as bass
import concourse.mybir as mybir
from concourse.bass_utils import run_bass_kernel_spmd

D = 2048
L = 2064
NM = 16
DFF = 8192
EPS = 1e-6
TT = [(0, 16)] + [(16 + 128 * i, 128) for i in range(16)]
TB = [(0, 16, [0])] + [(16 + 512 * j, 512, [1 + 4 * j + m for m in range(4)]) for j in range(4)]
EVEN_IN = 3072
ODD_IN = 6176
TWO_PI = 2.0 * math.pi


class K(object):
    pass


def build(stop_after=99, dbg=False):
    nc = bass.Bass("TRN2", target_bir_lowering=False)
    st = ExitStack()
    P = Prog(nc)
    din = {}

    def inp(name, shape, dt=F32):
        h = nc.dram_tensor(name, list(shape), dt, kind="ExternalInput").ap()
        din[name] = T(h, name)
        return din[name]

    def scratch(name, shape, dt):
        h = nc.dram_tensor(name, list(shape), dt).ap()
        return T(h, name)

    x = inp("x", [2048, D])
    meta = inp("meta", [NM, D])
    gains = inp("gains", [8, D])
    w1 = [inp("w1_%d" % i, [D, DFF]) for i in range(2)]
    w2 = [inp("w2_%d" % i, [DFF, D]) for i in range(2)]
    ewin = inp("ewin", [D, EVEN_IN])
    ewout = inp("ewout", [D, D])
    owin = inp("owin", [D, ODD_IN])
    owout = inp("owout", [D, D])
    c_ident = inp("c_ident", [128, 128])
    c_rope_e = inp("c_rope_e", [128, 17, 2, 64])
    qkg = inp("qkg", [2, 128])
    out = nc.dram_tensor("out", [2048, D], F32, kind="ExternalOutput").ap()
    OUT = T(out, "out")
    s5p = inp("s5p", [128, 3, 32])
    s5b = inp("s5b", [128, 2, 32, 16])
    s5c = inp("s5c", [128, 2, 32, 16])
    s5d = inp("s5d", [128, 4])
    s5gw = inp("s5gw", [512, 512])
    s5gb = inp("s5gb", [128, 4])
    c_tau = inp("c_tau", [128, 2, L])

    c_rope_o = inp("c_rope_o", [128, 17, 4, 64])
    c_ret = inp("c_ret", [128, 8, 128])
    c_col = inp("c_col", [128, 4])
    ret_ld = inp("ret_ld", [1, 16])
    ret_norm = inp("ret_norm", [1, 1024])
    ml_gb = inp("ml_gb", [8, 4])
    c_rows = inp("c_rows", [8, 4, L])
    ml_cw = inp("ml_cw", [128, 8, 5])
    ml_cb = inp("ml_cb", [128, 8])
    ml_wq = inp("ml_wq", [8, 128, 128])
    ml_wk = inp("ml_wk", [8, 128, 128])
    ml_wv = inp("ml_wv", [8, 128, 128])
    ml_norm = inp("ml_norm", [1, 1024])
    c_mmask = inp("c_mmask", [128, 2, 128])
    pd = scratch("pd", [48, 128, 17, 128], BF16)
    wtd = {}
    wtd["w1_0"] = scratch("wt_w1_0", [16, 128, 16 * 512], BF16)
    wtd["w2_0"] = scratch("wt_w2_0", [16, 128, 16 * 512], BF16)
    wtd["w1_1"] = scratch("wt_w1_1", [16, 128, 16 * 512], BF16)
    wtd["w2_1"] = scratch("wt_w2_1", [16, 128, 16 * 512], BF16)
    wtd["owin"] = scratch("wt_owin", [12, 128, 16 * 512], BF16)
    wtd["ewin"] = scratch("wt_ewin", [6, 128, 16 * 512], BF16)
    bgq = []
    hres = scratch("hres", [L, D], F32)
    mixd = scratch("mixd", [D, L], BF16)
    wb = {}
    for nm, shp in (("ewin", [D, EVEN_IN]), ("ewout", [D, D]), ("w1_0", [D, DFF]), ("w2_0", [DFF, D]),
                    ("owin", [D, ODD_IN]), ("owout", [D, D]), ("w1_1", [D, DFF]), ("w2_1", [DFF, D])):
        wb[nm] = scratch("wb_" + nm, shp, BF16)

    B = []
    for b in range(8):
        h = st.enter_context(nc.psum_tensor("bank%d" % b, [128, 512], F32))
        B.append(T(h, "bank%d" % b))

    def bf(bank):
        return bank.h[:, :].bitcast(BF16)

    castq = []
    cast_pending = {}

    def convert(nm):
        src = din[nm]
        dst = wb[nm]
        rows = src.h.shape[0]
        cols = src.h.shape[1]
        rstep = max(128, (1 << 20) // cols // 128 * 128)
        cast_pending[nm] = 0
        for r0 in range(0, rows, rstep):
            r1 = min(rows, r0 + rstep)

            def f(gate, r0=r0, r1=r1, nm=nm):
                P.dma("pool", lambda e: e.dma_start(out=dst.h[r0:r1, :], in_=src.h[r0:r1, :]),
                      reads=[src.k()] + list(gate), writes=[dst.k(r0)])
                cast_pending[nm] -= 1
            castq.append((nm, f))
            cast_pending[nm] += 1
        dst.rkeys = [dst.k(r0) for r0 in range(0, rows, rstep)]
        dst.rstep = rstep

    def cast_pump(n=1, gate=()):
        for _ in range(n):
            if castq:
                castq.pop(0)[1](gate)

    def need(nm):
        while cast_pending.get(nm, 0) > 0:
            cast_pump(1)

    def retile(nm):
        src = wb[nm]
        dst = wtd[nm]
        if nm.startswith("w2"):
            lst = [(cb * 4 + fq, fq * 2048, 2048, cb * 512) for cb in range(4) for fq in range(4)]
        elif nm.startswith("w1"):
            lst = [(fc, 0, 2048, fc * 512) for fc in range(16)]
        else:
            lst = [(cbk, 0, 2048, cbk * 512) for cbk in range(6 if nm == "ewin" else 12)]
        for (ci, r0, nr, c0) in lst:
            def f(ci=ci, r0=r0, nr=nr, c0=c0):
                need(nm)
                rk_ = [src.k(rr) for rr in range(r0 // src.rstep * src.rstep, r0 + nr, src.rstep)]
                P.dma("sp", lambda e: e.dma_start(out=dst.h[ci].rearrange("p (kt c) -> p kt c", c=512),
                                                  in_=src.h[r0:r0 + nr, c0:c0 + 512].rearrange("(kt p) c -> p kt c", p=128)),
                      reads=rk_, writes=[dst.k(ci)], grp="bg")
            bgq.append(f)

    bgc = {"n": 0}

    def pump(n=1):
        for _ in range(n):
            if bgq:
                bgq.pop(0)()
                bgc["n"] += 1

    def pump_until(kk):
        while bgc["n"] < kk and bgq:
            pump(1)

    ident = P.sb("ident", [128, 128], BF16)
    identf = P.sb("identf", [128, 128], F32)
    ones = P.sb("ones", [128, 128], BF16)
    cst = P.sb("cst", [128, 8], F32)
    P.dma("sp", lambda e: e.dma_start(out=identf[:, :], in_=c_ident[:, :]), reads=[c_ident.k()], writes=[identf.k()])
    P.op("dve", lambda e: e.tensor_copy(out=ident[:, :], in_=identf[:, :]), reads=[identf.k()], writes=[ident.k()])
    P.op("dve", lambda e: e.memset(ones[:, :], 1.0), writes=[ones.k()])
    P.op("dve", lambda e: e.memset(cst[:, 0:1], EPS), writes=[cst.k()])
    P.op("dve", lambda e: e.memset(cst[:, 1:2], -math.pi), writes=[cst.k()])
    P.op("dve", lambda e: e.memset(cst[:, 2:3], 0.0), writes=[cst.k()])
    base_mark = P.mark()

    state = K()
    state.layer0 = True

    def h_src(tt):
        s0, p = TT[tt]
        if state.layer0:
            if tt == 0:
                return meta.h[0:16, :], meta.k()
            return x.h[s0 - 16:s0 - 16 + p, :], x.k()
        return hres.h[s0:s0 + p, :], hres.k(tt)

    rr = {"bank": 0}

    def rstd_from_ss(ss_t, ss_ap, rs_t, rs_ap, n):
        P.op("act", lambda e: e.activation(out=rs_ap, in_=ss_ap, func=AF.Sqrt, bias=cst[:ss_ap.shape[0], 0:1], scale=1.0 / n),
             reads=[ss_t.k(), cst.k()], writes=[rs_t.k()])
        P.op("dve", lambda e: e.reciprocal(out=rs_ap, in_=rs_ap), reads=[rs_t.k()], writes=[rs_t.k()])

    def phase_norm_T(gi, dstT, tiles, tok_off=0, banks=(0, 1), bufs=None, koff=0):
        m = P.mark()
        if bufs is None:
            bufs = K()
            bufs.gb = P.sb("gb", [128, D], F32)
            bufs.ht = [P.sb("ht", [128, D], F32) for _ in range(2)]
            bufs.hb = [P.sb("hb", [128, D], BF16) for _ in range(2)]
            bufs.junk = P.sb("junk", [128, D], BF16)
            bufs.ss = [P.sb("ss", [128, 2], F32) for _ in range(2)]
        gb, ht, hb, junk, ss = bufs.gb, bufs.ht, bufs.hb, bufs.junk, bufs.ss
        P.dma("sp", lambda e: e.dma_start(out=gb[:, :], in_=gains.h[gi:gi + 1, :].partition_broadcast(128)),
              reads=[gains.k()], writes=[gb.k()])
        for n, tt in enumerate(tiles):
            s0, p = TT[tt]
            a = n % len(ht)
            src, skey = h_src(tt)
            P.dma("sp", lambda e, a=a, p=p, src=src: e.dma_start(out=ht[a][:p, :], in_=src), reads=[skey], writes=[ht[a].k()])
            P.op("act", lambda e, a=a, p=p: e.activation(out=junk[:p, :], in_=ht[a][:p, :], func=AF.Square, accum_out=ss[a][:p, 0:1]),
                 reads=[ht[a].k()], writes=[junk.k(), ss[a].k()])
            P.op("act", lambda e, a=a, p=p: e.activation(out=ss[a][:p, 1:2], in_=ss[a][:p, 0:1], func=AF.Sqrt, bias=cst[:p, 0:1], scale=1.0 / D),
                 reads=[ss[a].k(), cst.k()], writes=[ss[a].k("r")])
            P.op("dve", lambda e, a=a, p=p: e.reciprocal(out=ss[a][:p, 1:2], in_=ss[a][:p, 1:2]), reads=[ss[a].k("r")], writes=[ss[a].k("r")])
            P.op("dve", lambda e, a=a, p=p: e.scalar_tensor_tensor(out=hb[a][:p, :], in0=ht[a][:p, :], scalar=ss[a][:p, 1:2], in1=gb[:p, :],
                                                                  op0=ALU.mult, op1=ALU.mult),
                 reads=[ht[a].k(), ss[a].k("r"), gb.k()], writes=[hb[a].k()])
            for half in range(2):
                bk = B[banks[half]]
                for j in range(8):
                    kt = half * 8 + j
                    P.op("pe", lambda e, a=a, p=p, j=j, kt=kt, bk=bk: e.transpose(out=bf(bk)[:, j * 128:j * 128 + p], in_=hb[a][:p, kt * 128:(kt + 1) * 128],
                                                                                   identity=ident[:p, :p]),
                         reads=[hb[a].k(), ident.k()], writes=[bk.k()])
                P.op("act", lambda e, p=p, half=half, bk=bk, s0=s0: e.copy(
                    out=dstT[:, half * 8:(half + 1) * 8, s0 - tok_off:s0 - tok_off + p],
                    in_=bf(bk).rearrange("q (j c) -> q j c", c=128)[:, 0:8, 0:p]),
                    reads=[bk.k()], writes=[dstT.k(tt - koff)])
        P.release(m)

    def proj_tok(wt_dram, col_blocks, srcT, tiles, consumer, tok_off=0, banks=(2, 3, 4, 5), wbufs=None, tiled=None, koff=0):
        nb = 0
        fifo = []
        for cb, (c0, cw) in enumerate(col_blocks):
            wtb = wbufs[cb % 2]
            if tiled is not None and cw == 512:
                P.dma("sp", lambda e, wtb=wtb, cb=cb: e.dma_start(out=wtb[:, :, :], in_=tiled.h[cb].rearrange("p (kt c) -> p kt c", c=512)),
                      reads=[tiled.k(cb)], writes=[wtb.k()])
            else:
                P.dma("sp", lambda e, wtb=wtb, c0=c0, cw=cw: e.dma_start(
                    out=wtb[:, :, 0:cw], in_=wt_dram.h[:, c0:c0 + cw].rearrange("(kt p) c -> p kt c", p=128)),
                    reads=wt_dram.rkeys, writes=[wtb.k()])
            for tt in tiles:
                s0, p = TT[tt]
                bk = B[banks[nb % len(banks)]]
                nb += 1
                for kt in range(16):
                    P.op("pe", lambda e, bk=bk, p=p, kt=kt, s0=s0, wtb=wtb, cw=cw: e.matmul(
                        bk.h[:p, 0:cw], lhsT=srcT[:, kt, s0 - tok_off:s0 - tok_off + p], rhs=wtb[:, kt, 0:cw], start=(kt == 0), stop=(kt == 15)),
                        reads=[srcT.k(tt - koff), wtb.k()], writes=[bk.k()])
                dfr = consumer(cb, tt, bk, cw)
                if dfr is not None:
                    fifo.append(dfr)
                    if len(fifo) > 2:
                        fifo.pop(0)()
        while fifo:
            fifo.pop(0)()

    for nm in ("ewin", "ewout", "w1_0", "w2_0", "owin", "owout", "w1_1", "w2_1"):
        convert(nm)
    cast_pump(12)
    for nm in ("ewin", "w1_0", "w2_0", "owin", "w1_1", "w2_1"):
        retile(nm)

    def mlp_and_finish(li, final):
        m0 = P.mark()
        groups = [[0, 1, 2, 3, 4], [5, 6, 7, 8], [9, 10, 11, 12], [13, 14, 15, 16]]
        hTs = [P.sb("hn2T", [128, 16, 528], BF16) for _ in range(2)]
        hid = P.sb("hid", [128, 64, 528], BF16)
        wA = [P.sb("wA", [128, 16, 512], BF16) for _ in range(2)]
        wBf = wA
        ot = [P.sb("ot", [128, D], F32) for _ in range(5)]
        rl = [P.sb("rl", [128, 512], BF16) for _ in range(2)]
        nb_ = K()
        nb_.gb = P.sb("gb", [128, D], F32)
        nb_.ht = [P.sb("ht", [128, D], F32)]
        nb_.hb = [P.sb("hb", [128, D], BF16)]
        nb_.junk = P.sb("junk", [128, D], BF16)
        nb_.ss = [P.sb("ss", [128, 2], F32)]
        g4 = nb_.gb
        hres_t = nb_.ht
        junk2 = nb_.junk
        ssm = P.sb("ssm", [128, 8], F32)
        pump_until(38 if li == 0 else 82)
        mstep = {"n": 0}
        W1 = wtd["w1_%d" % li]
        W2 = wtd["w2_%d" % li]
        phase_norm_T(li * 4 + 2, hTs[0], groups[0], tok_off=0, banks=(0, 1), bufs=nb_)
        for gi, gtiles in enumerate(groups):
            tok0 = TT[gtiles[0]][0]
            ntok = sum(TT[t][1] for t in gtiles)
            hT = hTs[gi % 2]
            if gtiles[0] == 0:
                sub = [(0, 16, [0]), (16, 512, [1, 2, 3, 4])]
            else:
                sub = [(0, 512, gtiles)]
            nb = 0
            for fc in range(16):
                wt = wA[fc % 2]
                P.dma("sp", lambda e, wt=wt, fc=fc: e.dma_start(
                    out=wt[:, :, :], in_=W1.h[fc].rearrange("p (kt c) -> p kt c", c=512)),
                    reads=[W1.k(fc)], writes=[wt.k()])
                if li == 0:
                    mstep["n"] += 1
                    if castq:
                        if mstep["n"] % 2 == 0:
                            cast_pump(1, gate=[wt.k()])
                    else:
                        pump(1)
                for fl in range(4):
                    ft = fc * 4 + fl
                    for (o0, n, stiles) in sub:
                        bk = B[nb % 3]
                        nb += 1
                        for kt in range(16):
                            P.op("pe", lambda e, bk=bk, n=n, kt=kt, o0=o0, wt=wt, fl=fl, hT=hT: e.matmul(
                                bk.h[:, 0:n], lhsT=wt[:, kt, fl * 128:(fl + 1) * 128], rhs=hT[:, kt, o0:o0 + n], start=(kt == 0), stop=(kt == 15)),
                                reads=[hT.k(t) for t in stiles] + [wt.k()], writes=[bk.k()])
                        r = rl[nb % 2]
                        P.op("act", lambda e, bk=bk, n=n, r=r: e.activation(out=r[:, 0:n], in_=bk.h[:, 0:n], func=AF.Relu),
                             reads=[bk.k()], writes=[r.k()])
                        P.op("dve", lambda e, n=n, r=r, ft=ft, o0=o0: e.tensor_tensor(out=hid[:, ft, o0:o0 + n], in0=r[:, 0:n], in1=r[:, 0:n], op=ALU.mult),
                             reads=[r.k()], writes=[hid.k(ft)])
            for cb in range(4):
                if cb == 2 and gi + 1 < len(groups):
                    phase_norm_T(li * 4 + 2, hTs[(gi + 1) % 2], groups[gi + 1], tok_off=TT[groups[gi + 1][0]][0], banks=(0, 1), bufs=nb_)
                for fq in range(4):
                    wt = wBf[(cb * 4 + fq) % 2]
                    P.dma("sp", lambda e, wt=wt, cb=cb, fq=fq: e.dma_start(
                        out=wt[:, :, :], in_=W2.h[cb * 4 + fq].rearrange("p (kt c) -> p kt c", c=512)),
                        reads=[W2.k(cb * 4 + fq)], writes=[wt.k()])
                    if li == 0:
                        mstep["n"] += 1
                        if castq:
                            if mstep["n"] % 2 == 0:
                                cast_pump(1, gate=[wt.k()])
                        else:
                            pump(1)
                    for fl in range(16):
                        ft = fq * 16 + fl
                        for n_, tt in enumerate(gtiles):
                            s0, p = TT[tt]
                            bk = B[3 + n_]
                            P.op("pe", lambda e, bk=bk, p=p, ft=ft, fl=fl, s0=s0, wt=wt, tok0=tok0: e.matmul(
                                bk.h[:p, :], lhsT=hid[:, ft, s0 - tok0:s0 - tok0 + p], rhs=wt[:, fl, :], start=(ft == 0), stop=(ft == 63)),
                                reads=[hid.k(ft), wt.k()], writes=[bk.k()])
                for n_, tt in enumerate(gtiles):
                    s0, p = TT[tt]
                    bk = B[3 + n_]
                    P.op("act", lambda e, bk=bk, p=p, n_=n_, cb=cb: e.copy(out=ot[n_][:p, cb * 512:(cb + 1) * 512], in_=bk.h[:p, :]),
                         reads=[bk.k()], writes=[ot[n_].k(cb)])
            P.dma("sp", lambda e: e.dma_start(out=g4[:, :], in_=gains.h[li * 4 + 3:li * 4 + 4, :].partition_broadcast(128)),
                  reads=[gains.k()], writes=[g4.k()])
            for n_, tt in enumerate(gtiles):
                s0, p = TT[tt]
                a = 0
                src, skey = h_src(tt)
                P.dma("sp", lambda e, a=a, p=p, src=src: e.dma_start(out=hres_t[a][:p, :], in_=src), reads=[skey], writes=[hres_t[a].k()])
                okeys = [ot[n_].k(c) for c in range(4)]
                P.op("act", lambda e, p=p, n_=n_: e.activation(out=junk2[:p, :], in_=ot[n_][:p, :], func=AF.Square, accum_out=ssm[:p, 0:1]),
                     reads=okeys, writes=[junk2.k(), ssm.k()])
                P.op("act", lambda e, p=p: e.activation(out=ssm[:p, 1:2], in_=ssm[:p, 0:1], func=AF.Sqrt, bias=cst[:p, 0:1], scale=1.0 / D),
                     reads=[ssm.k(), cst.k()], writes=[ssm.k("r")])
                P.op("dve", lambda e, p=p: e.reciprocal(out=ssm[:p, 1:2], in_=ssm[:p, 1:2]), reads=[ssm.k("r")], writes=[ssm.k("r")])
                P.op("dve", lambda e, p=p, n_=n_: e.scalar_tensor_tensor(out=ot[n_][:p, :], in0=ot[n_][:p, :], scalar=ssm[:p, 1:2], in1=g4[:p, :],
                                                                         op0=ALU.mult, op1=ALU.mult),
                     reads=okeys + [ssm.k("r"), g4.k()], writes=okeys)
                P.op("dve", lambda e, p=p, n_=n_, a=a: e.tensor_tensor(out=ot[n_][:p, :], in0=ot[n_][:p, :], in1=hres_t[a][:p, :], op=ALU.add),
                     reads=okeys + [hres_t[a].k()], writes=okeys)
                if final:
                    if tt > 0:
                        P.dma("sp", lambda e, p=p, n_=n_, s0=s0: e.dma_start(out=out[s0 - 16:s0 - 16 + p, :], in_=ot[n_][:p, :]),
                              reads=okeys, writes=[OUT.k(tt)], is_out=True)
                else:
                    P.dma("sp", lambda e, p=p, n_=n_, s0=s0: e.dma_start(out=hres.h[s0:s0 + p, :], in_=ot[n_][:p, :]),
                          reads=okeys, writes=[hres.k(tt)])
        P.release(m0)

    def wout_phase(li, wname):
        m0 = P.mark()
        mixT = P.sb("mixT", [128, 16, L], BF16)
        for kt in range(16):
            P.dma("sp", lambda e, kt=kt: e.dma_start(out=mixT[:, kt, :], in_=mixd.h[kt * 128:(kt + 1) * 128, :]),
                  reads=[mixd.k(kt)], writes=[mixT.k(kt)])
        need(wname)
        W = wb[wname]
        wo = P.sb("wo", [128, 16, D], BF16)
        for kt in range(16):
            P.dma("sp", lambda e, kt=kt: e.dma_start(out=wo[:, kt, :], in_=W.h[kt * 128:(kt + 1) * 128, :]),
                  reads=W.rkeys, writes=[wo.k(kt)])
        g1 = P.sb("g1", [128, D], F32)
        P.dma("sp", lambda e: e.dma_start(out=g1[:, :], in_=gains.h[li * 4 + 1:li * 4 + 2, :].partition_broadcast(128)),
              reads=[gains.k()], writes=[g1.k()])
        ot = [P.sb("wot", [128, D], F32) for _ in range(2)]
        hrt = [P.sb("whr", [128, D], F32) for _ in range(2)]
        ssm = P.sb("wss", [128, 8], F32)
        junk2 = P.sb("wjunk", [128, D], BF16)
        for tt in range(17):
            s0, p = TT[tt]
            a = tt % 2
            src, skey = h_src(tt)
            P.dma("sp", lambda e, a=a, p=p, src=src: e.dma_start(out=hrt[a][:p, :], in_=src), reads=[skey], writes=[hrt[a].k()])
            for cb in range(4):
                bk = B[(tt * 4 + cb) % 8]
                for kt in range(16):
                    P.op("pe", lambda e, bk=bk, p=p, kt=kt, s0=s0, cb=cb: e.matmul(
                        bk.h[:p, :], lhsT=mixT[:, kt, s0:s0 + p], rhs=wo[:, kt, cb * 512:(cb + 1) * 512], start=(kt == 0), stop=(kt == 15)),
                        reads=[mixT.k(kt), wo.k(kt)], writes=[bk.k()])
                P.op("act", lambda e, bk=bk, p=p, a=a, cb=cb: e.copy(out=ot[a][:p, cb * 512:(cb + 1) * 512], in_=bk.h[:p, :]),
                     reads=[bk.k()], writes=[ot[a].k(cb)])
            okeys = [ot[a].k(c) for c in range(4)]
            P.op("act", lambda e, p=p, a=a: e.activation(out=junk2[:p, :], in_=ot[a][:p, :], func=AF.Square, accum_out=ssm[:p, 0:1]),
                 reads=okeys, writes=[junk2.k(), ssm.k()])
            P.op("act", lambda e, p=p: e.activation(out=ssm[:p, 1:2], in_=ssm[:p, 0:1], func=AF.Sqrt, bias=cst[:p, 0:1], scale=1.0 / D),
                 reads=[ssm.k(), cst.k()], writes=[ssm.k("r")])
            P.op("dve", lambda e, p=p: e.reciprocal(out=ssm[:p, 1:2], in_=ssm[:p, 1:2]), reads=[ssm.k("r")], writes=[ssm.k("r")])
            P.op("dve", lambda e, p=p, a=a: e.scalar_tensor_tensor(out=ot[a][:p, :], in0=ot[a][:p, :], scalar=ssm[:p, 1:2], in1=g1[:p, :],
                                                                   op0=ALU.mult, op1=ALU.mult),
                 reads=okeys + [ssm.k("r"), g1.k()], writes=okeys)
            P.op("dve", lambda e, p=p, a=a: e.tensor_tensor(out=ot[a][:p, :], in0=ot[a][:p, :], in1=hrt[a][:p, :], op=ALU.add),
                 reads=okeys + [hrt[a].k()], writes=okeys)
            P.dma("sp", lambda e, p=p, a=a, s0=s0: e.dma_start(out=hres.h[s0:s0 + p, :], in_=ot[a][:p, :]),
                  reads=okeys, writes=[hres.k(tt)])
        P.release(m0)
        state.layer0 = False

    ctx = K()
    ctx.nc = nc; ctx.P = P; ctx.B = B; ctx.bf = bf; ctx.din = din; ctx.wb = wb; ctx.ident = ident; ctx.identf = identf
    ctx.ones = ones; ctx.cst = cst; ctx.mixd = mixd; ctx.hres = hres; ctx.state = state; ctx.h_src = h_src
    ctx.pd = pd; ctx.wt = wtd; ctx.retile = retile; ctx.pump = pump; ctx.cast_pump = cast_pump; ctx.need = need; ctx.castq = castq; ctx.pump_until = pump_until; ctx.bgq = bgq; ctx.phase_norm_T = phase_norm_T; ctx.proj_tok = proj_tok; ctx.convert = convert; ctx.gains = gains

    if stop_after == 104:
        need("owin")
        pump_until(50)
        odd_mixer(ctx)
        stop_after = 4
    else:
        if stop_after >= 1:
            even_mixer(ctx)
        if stop_after >= 2:
            wout_phase(0, "ewout")
        if stop_after >= 3:
            mlp_and_finish(0, final=False)
        if stop_after >= 4:
            odd_mixer(ctx)
        if stop_after >= 5:
            wout_phase(1, "owout")
        if stop_after >= 6:
            mlp_and_finish(1, final=True)
    if stop_after < 6:
        m0 = P.mark()
        if stop_after in (1, 4):
            tmpb = P.sb("dbg_b", [128, L], BF16)
            tmpf = P.sb("dbg_f", [128, L], F32)
            for kt in range(16):
                P.dma("sp", lambda e, kt=kt: e.dma_start(out=tmpb[:, :], in_=mixd.h[kt * 128:(kt + 1) * 128, :]), reads=[mixd.k(kt)], writes=[tmpb.k()])
                P.op("dve", lambda e: e.tensor_copy(out=tmpf[:, :], in_=tmpb[:, :]), reads=[tmpb.k()], writes=[tmpf.k()])
                P.dma("sp", lambda e, kt=kt: e.dma_start(out=out[kt * 128:(kt + 1) * 128, :], in_=tmpf[:, 16:L]), reads=[tmpf.k()], writes=[OUT.k(kt)], is_out=True)
        else:
            tmpf = P.sb("dbg_f", [128, D], F32)
            for tt in range(1, 17):
                s0, p = TT[tt]
                P.dma("sp", lambda e, s0=s0: e.dma_start(out=tmpf[:, :], in_=hres.h[s0:s0 + 128, :]), reads=[hres.k(tt)], writes=[tmpf.k()])
                P.dma("sp", lambda e, s0=s0: e.dma_start(out=out[s0 - 16:s0 + 112, :], in_=tmpf[:, :]), reads=[tmpf.k()], writes=[OUT.k(tt)], is_out=True)
    P.finish()
    P.emit(st)
    print("ops:", P.stats())
    return nc, st


def even_mixer(c):
    P, B, bf, din, wb = c.P, c.B, c.bf, c.din, c.wb
    ident, identf, ones, cst, mixd = c.ident, c.identf, c.ones, c.cst, c.mixd
    m_all = P.mark()
    uT = P.sb("uT", [128, 4, L], BF16)
    m_q = P.mark()
    qT = P.sb("qT", [128, 12, L], BF16)
    kT = P.sb("kT", [128, 4, L], BF16)
    V = P.sb("V", [128, 17, 512], BF16)
    m_proj = P.mark()
    hnT = P.sb("hnT", [128, 16, 1040], BF16)
    nbf = K()
    nbf.gb = P.sb("gb", [128, D], F32)
    nbf.ht = [P.sb("ht", [128, D], F32)]
    nbf.hb = [P.sb("hb", [128, D], BF16)]
    nbf.junk = P.sb("junk", [128, D], BF16)
    nbf.ss = [P.sb("ss", [128, 2], F32)]
    wbufs = [P.sb("wp", [128, 16, 512], BF16) for _ in range(2)]
    gqk = P.sb("gqk", [128, 2, 128], F32)
    P.dma("sp", lambda e: e.dma_start(out=gqk[:, 0, :], in_=din["qkg"].h[0:1, :].partition_broadcast(128)), reads=[din["qkg"].k()], writes=[gqk.k()])
    P.dma("sp", lambda e: e.dma_start(out=gqk[:, 1, :], in_=din["qkg"].h[1:2, :].partition_broadcast(128)), reads=[din["qkg"].k()], writes=[gqk.k()])
    P.op("dve", lambda e: e.tensor_scalar(out=gqk[:, 0, :], in0=gqk[:, 0, :], scalar1=128.0 ** -0.5, scalar2=None, op0=ALU.mult),
         reads=[gqk.k()], writes=[gqk.k()])
    rope = [P.sb("rope", [128, 2, 64], F32) for _ in range(2)]
    qf = [P.sb("qf", [128, 512], F32) for _ in range(2)]
    sq = P.sb("sq", [128, 512], F32)
    ss4 = P.sb("ss4", [128, 8], F32)
    tA = P.sb("tA", [128, 256], F32)
    tB_ = P.sb("tB", [128, 256], F32)
    qr = [P.sb("qr", [128, 512], BF16) for _ in range(4)]
    cnt = {"n": 0}
    rope_loaded = {}

    def consumer(cb, tt, bk, cw):
        s0, p = TT[tt]
        n = cnt["n"]
        cnt["n"] += 1
        a = n % 2
        if cb < 4:
            isq = cb < 3
            gi = 0 if isq else 1
            rp = rope[tt % 2]
            if rope_loaded.get(tt % 2) != tt:
                P.dma("sp", lambda e, rp=rp, tt=tt, p=p: e.dma_start(out=rp[:p, :, :], in_=din["c_rope_e"].h[0:p, tt, :, :]),
                      reads=[din["c_rope_e"].k()], writes=[rp.k()])
                rope_loaded[tt % 2] = tt
            f = qf[a]
            P.op("act", lambda e, f=f, p=p, bk=bk: e.copy(out=f[:p, :], in_=bk.h[:p, 0:512]), reads=[bk.k()], writes=[f.k()])
            P.op("act", lambda e, f=f, p=p: e.activation(out=sq[:p, :], in_=f[:p, :], func=AF.Square), reads=[f.k()], writes=[sq.k()])
            P.op("dve", lambda e, p=p: e.reduce_sum(out=ss4[:p, 0:4], in_=sq[:p, :].rearrange("q (h d) -> q h d", d=128), axis=AX.X),
                 reads=[sq.k()], writes=[ss4.k()])
            P.op("act", lambda e, p=p: e.activation(out=ss4[:p, 4:8], in_=ss4[:p, 0:4], func=AF.Sqrt, bias=cst[:p, 0:1], scale=1.0 / 128),
                 reads=[ss4.k(), cst.k()], writes=[ss4.k("r")])
            P.op("dve", lambda e, p=p: e.reciprocal(out=ss4[:p, 4:8], in_=ss4[:p, 4:8]), reads=[ss4.k("r")], writes=[ss4.k("r")])
            f3 = f[:p, :].rearrange("q (h d) -> q h d", d=128)
            P.op("dve", lambda e, f3=f3, p=p: e.tensor_tensor(out=f3, in0=f3, in1=ss4[:p, 4:8].unsqueeze(2).broadcast_to([p, 4, 128]), op=ALU.mult),
                 reads=[f.k(), ss4.k("r")], writes=[f.k()])
            P.op("dve", lambda e, f3=f3, p=p, gi=gi: e.tensor_tensor(out=f3, in0=f3, in1=gqk[:p, gi, :].unsqueeze(1).broadcast_to([p, 4, 128]), op=ALU.mult),
                 reads=[f.k(), gqk.k()], writes=[f.k()])
            f5 = f[:p, :].rearrange("q (g t f) -> q g t f", g=8, t=2, f=32)
            r = qr[n % 4]
            r5 = r[:p, :].rearrange("q (g t f) -> q g t f", g=8, t=2, f=32)
            x1 = f5[:, :, 0, :]
            x2 = f5[:, :, 1, :]
            cs_ = rp[:p, 0, :].rearrange("q (a f) -> q a f", a=2).unsqueeze(1).broadcast_to([p, 4, 2, 32])
            sn_ = rp[:p, 1, :].rearrange("q (a f) -> q a f", a=2).unsqueeze(1).broadcast_to([p, 4, 2, 32])
            x1 = f[:p, :].rearrange("q (h a t f) -> q h a t f", h=4, a=2, t=2, f=32)[:, :, :, 0, :]
            x2 = f[:p, :].rearrange("q (h a t f) -> q h a t f", h=4, a=2, t=2, f=32)[:, :, :, 1, :]
            r5 = r[:p, :].rearrange("q (h a t f) -> q h a t f", h=4, a=2, t=2, f=32)
            ta4 = tA[:p, :].rearrange("q (h a f) -> q h a f", h=4, a=2)
            tb4 = tB_[:p, :].rearrange("q (h a f) -> q h a f", h=4, a=2)
            rk = [f.k(), rp.k()]
            P.op("dve", lambda e: e.tensor_tensor(out=ta4, in0=x1, in1=cs_, op=ALU.mult), reads=rk, writes=[tA.k()])
            P.op("dve", lambda e: e.tensor_tensor(out=tb4, in0=x2, in1=sn_, op=ALU.mult), reads=rk, writes=[tB_.k()])
            P.op("dve", lambda e: e.tensor_tensor(out=r5[:, :, :, 0, :], in0=ta4, in1=tb4, op=ALU.subtract), reads=[tA.k(), tB_.k()], writes=[r.k(0)])
            P.op("dve", lambda e: e.tensor_tensor(out=ta4, in0=x1, in1=sn_, op=ALU.mult), reads=rk + [r.k(0)], writes=[tA.k()])
            P.op("dve", lambda e: e.tensor_tensor(out=tb4, in0=x2, in1=cs_, op=ALU.mult), reads=rk + [r.k(0)], writes=[tB_.k()])
            P.op("dve", lambda e: e.tensor_tensor(out=r5[:, :, :, 1, :], in0=ta4, in1=tb4, op=ALU.add), reads=[tA.k(), tB_.k()], writes=[r.k(1)])
            tb = B[6 + n % 2]
            dst = qT if isq else kT
            h0 = cb * 4 if isq else 0

            def part2():
                for j in range(4):
                    P.op("pe", lambda e, j=j: e.transpose(out=bf(tb)[:, j * 128:j * 128 + p], in_=r[:p, j * 128:(j + 1) * 128], identity=ident[:p, :p]),
                         reads=[r.k(0), r.k(1), ident.k()], writes=[tb.k()])
                P.op("act", lambda e: e.copy(out=dst[:, h0:h0 + 4, s0:s0 + p], in_=bf(tb).rearrange("q (j c) -> q j c", c=128)[:, 0:4, 0:p]),
                     reads=[tb.k()], writes=[dst.k(tt)])
            return part2
        elif cb == 4:
            P.op("act", lambda e, bk=bk, p=p, tt=tt: e.copy(out=V[:p, tt, :], in_=bk.h[:p, 0:512]), reads=[bk.k()], writes=[V.k(tt)])
        else:
            r = qr[n % 4]
            P.op("act", lambda e, bk=bk, p=p, r=r: e.copy(out=r[:p, :], in_=bk.h[:p, 0:512]), reads=[bk.k()], writes=[r.k(0), r.k(1)])
            tb = B[6 + n % 2]

            def part2u():
                for j in range(4):
                    P.op("pe", lambda e, j=j: e.transpose(out=bf(tb)[:, j * 128:j * 128 + p], in_=r[:p, j * 128:(j + 1) * 128], identity=ident[:p, :p]),
                         reads=[r.k(0), r.k(1), ident.k()], writes=[tb.k()])
                P.op("act", lambda e: e.copy(out=uT[:, 0:4, s0:s0 + p], in_=bf(tb).rearrange("q (j c) -> q j c", c=128)[:, 0:4, 0:p]),
                     reads=[tb.k()], writes=[uT.k(tt)])
            return part2u

    blocks = [(i * 512, 512) for i in range(6)]
    for (tiles, toff, koff) in ((list(range(0, 9)), 0, 0), (list(range(9, 17)), 1024, 8)):
        c.phase_norm_T(0, hnT, tiles, tok_off=toff, bufs=nbf, koff=koff)
        if koff == 0:
            c.pump(6)
        c.proj_tok(wb["ewin"], blocks, hnT, tiles, consumer, tok_off=toff, banks=(2, 3, 4, 5), wbufs=wbufs, tiled=c.wt["ewin"], koff=koff)
    P.release(m_proj)

    m_att = P.mark()
    PT = [P.sb("PT", [128, 512], BF16) for _ in range(4)]
    rden = [P.sb("rden", [128, 512], F32) for _ in range(2)]
    ob = [P.sb("ob", [128, 512], BF16) for _ in range(4)]
    it = 0
    for g in range(4):
        for hq in range(3):
            head = g * 3 + hq
            for (q0, nq, qtiles) in TB:
                Ob = B[4 + it % 2]
                Db = B[6 + it % 2]
                qk = [qT.k(t) for t in qtiles]

                def S(s, q0=q0, nq=nq, head=head, g=g, qk=qk):
                    ss0, ps = TT[s]
                    bk = B[s % 4]
                    P.op("pe", lambda e: e.matmul(bk.h[:ps, 0:nq], lhsT=kT[:, g, ss0:ss0 + ps], rhs=qT[:, head, q0:q0 + nq], start=True, stop=True),
                         reads=[kT.k(s)] + qk, writes=[bk.k()])

                S(0)
                S(1)
                S(2)
                for s in range(17):
                    if s + 3 < 17:
                        S(s + 3)
                    ss0, ps = TT[s]
                    bk = B[s % 4]
                    pt = PT[s % 4]
                    P.op("act", lambda e, bk=bk, pt=pt, ps=ps, nq=nq: e.activation(out=pt[:ps, 0:nq], in_=bk.h[:ps, 0:nq], func=AF.Exp),
                         reads=[bk.k()], writes=[pt.k()])
                    P.op("pe", lambda e, pt=pt, ps=ps, nq=nq, s=s, g=g, Ob=Ob: e.matmul(Ob.h[:, 0:nq], lhsT=V[:ps, s, g * 128:(g + 1) * 128], rhs=pt[:ps, 0:nq],
                                                                                        start=(s == 0), stop=(s == 16)),
                         reads=[V.k(s), pt.k()], writes=[Ob.k()])
                    P.op("pe", lambda e, pt=pt, ps=ps, nq=nq, s=s, Db=Db: e.matmul(Db.h[:, 0:nq], lhsT=ones[:ps, :], rhs=pt[:ps, 0:nq],
                                                                                  start=(s == 0), stop=(s == 16)),
                         reads=[ones.k(), pt.k()], writes=[Db.k()])
                o = ob[it % 4]
                rd = rden[it % 2]
                P.op("dve", lambda e, Db=Db, nq=nq, rd=rd: e.reciprocal(out=rd[:, 0:nq], in_=Db.h[:, 0:nq]), reads=[Db.k()], writes=[rd.k()])
                P.op("dve", lambda e, Ob=Ob, nq=nq, o=o, rd=rd: e.tensor_tensor(out=o[:, 0:nq], in0=Ob.h[:, 0:nq], in1=rd[:, 0:nq], op=ALU.mult),
                     reads=[Ob.k(), rd.k()], writes=[o.k()])
                P.dma("sp", lambda e, o=o, head=head, q0=q0, nq=nq: e.dma_start(out=mixd.h[head * 128:(head + 1) * 128, q0:q0 + nq], in_=o[:, 0:nq]),
                      reads=[o.k()], writes=[mixd.k(head)])
                it += 1
    P.release(m_q)
    s5_phase(c, uT)
    P.release(m_all)


def make_red(P, f32, nmax):
    I32 = mybir.dt.int32
    ki = P.sb("redki", [128, nmax], I32)
    kf = P.sb("redkf", [128, nmax], f32)

    def red(dst_t, dst, src_t, src, n, add=0.0):
        P.op("dve", lambda e: e.tensor_scalar(out=kf[:, 0:n], in0=src, scalar1=add, scalar2=1.0 / TWO_PI, op0=ALU.add, op1=ALU.mult),
             reads=[src_t.k()], writes=[kf.k()])
        P.op("dve", lambda e: e.tensor_copy(out=ki[:, 0:n], in_=kf[:, 0:n]), reads=[kf.k()], writes=[ki.k()])
        P.op("dve", lambda e: e.tensor_copy(out=kf[:, 0:n], in_=ki[:, 0:n]), reads=[ki.k()], writes=[kf.k()])
        P.op("dve", lambda e: e.tensor_scalar(out=kf[:, 0:n], in0=kf[:, 0:n], scalar1=-TWO_PI, scalar2=add, op0=ALU.mult, op1=ALU.add),
             reads=[kf.k()], writes=[kf.k()])
        P.op("dve", lambda e: e.tensor_tensor(out=dst, in0=kf[:, 0:n], in1=src, op=ALU.add), reads=[kf.k(), src_t.k()], writes=[dst_t.k()])
        P.op("dve", lambda e: e.tensor_scalar(out=kf[:, 0:n], in0=dst, scalar1=0.0, scalar2=TWO_PI, op0=ALU.is_lt, op1=ALU.mult),
             reads=[dst_t.k()], writes=[kf.k()])
        P.op("dve", lambda e: e.tensor_tensor(out=dst, in0=dst, in1=kf[:, 0:n], op=ALU.add), reads=[kf.k(), dst_t.k()], writes=[dst_t.k()])
    return red

def run_lanes_s5(items, make_gen, W):
    it = iter(items)
    free = list(range(W))
    active = []
    while True:
        while free:
            try:
                x_ = next(it)
            except StopIteration:
                break
            ln_ = free.pop(0)
            active.append((make_gen(x_, ln_), ln_))
        if not active:
            break
        for ga in list(active):
            try:
                next(ga[0])
            except StopIteration:
                active.remove(ga)
                free.append(ga[1])


def s5_phase(c, uT):
    P, B, bf, din = c.P, c.B, c.bf, c.din
    ident, identf, ones, cst, mixd = c.ident, c.identf, c.ones, c.cst, c.mixd
    m0 = P.mark()
    f32 = F32
    pr = P.sb("s5pr", [128, 3, 32], f32)
    P.dma("sp", lambda e: e.dma_start(out=pr[:, :, :], in_=din["s5p"].h[:, :, :]), reads=[din["s5p"].k()], writes=[pr.k()])
    w = P.sb("s5w", [128, 16, 32], f32)
    BT = P.sb("s5BT", [128, 32, 2, 128], BF16)
    CT = P.sb("s5CT", [128, 32, 2, 128], BF16)
    PA = P.sb("s5PA", [128, 32, 128], f32)
    PA2 = P.sb("s5PA2", [128, 32, 128], f32)
    PB = P.sb("s5PB", [128, 32, 32], f32)
    m_tmp = P.mark()
    red = make_red(P, f32, 32 * 128)
    LR, DT, ER, TH, SN, CS, ARE, AIM, NR, DEN, CRE, CIM, TH128, T0, T1, T2 = range(16)
    wk = [w.k()]

    def V1(fn):
        P.op("dve", fn, reads=wk + [pr.k(), cst.k()], writes=wk)

    def A1(fn):
        P.op("act", fn, reads=wk + [pr.k(), cst.k()], writes=wk)

    V1(lambda e: e.tensor_scalar(out=w[:, LR, :], in0=pr[:, 0, :], scalar1=-1e-4, scalar2=None, op0=ALU.min))
    A1(lambda e: e.activation(out=w[:, DT, :], in_=pr[:, 2, :], func=AF.Exp))
    V1(lambda e: e.tensor_tensor(out=w[:, T0, :], in0=w[:, LR, :], in1=w[:, DT, :], op=ALU.mult))
    A1(lambda e: e.activation(out=w[:, ER, :], in_=w[:, T0, :], func=AF.Exp))
    V1(lambda e: e.tensor_tensor(out=w[:, TH, :], in0=pr[:, 1, :], in1=w[:, DT, :], op=ALU.mult))
    red(w, w[:, T0, :], w, w[:, TH, :], 32, add=TWO_PI)
    A1(lambda e: e.activation(out=w[:, SN, :], in_=w[:, T0, :], func=AF.Sin, bias=cst[:, 1:2], scale=1.0))
    red(w, w[:, T0, :], w, w[:, TH, :], 32, add=TWO_PI + math.pi / 2)
    A1(lambda e: e.activation(out=w[:, CS, :], in_=w[:, T0, :], func=AF.Sin, bias=cst[:, 1:2], scale=1.0))
    V1(lambda e: e.scalar_tensor_tensor(out=w[:, ARE, :], in0=w[:, CS, :], scalar=-1.0, in1=w[:, ER, :], op0=ALU.mult, op1=ALU.mult))
    V1(lambda e: e.scalar_tensor_tensor(out=w[:, AIM, :], in0=w[:, SN, :], scalar=-1.0, in1=w[:, ER, :], op0=ALU.mult, op1=ALU.mult))
    V1(lambda e: e.tensor_scalar(out=w[:, NR, :], in0=w[:, ARE, :], scalar1=-1.0, scalar2=None, op0=ALU.add))
    V1(lambda e: e.tensor_tensor(out=w[:, T0, :], in0=w[:, LR, :], in1=w[:, LR, :], op=ALU.mult))
    V1(lambda e: e.tensor_tensor(out=w[:, T1, :], in0=pr[:, 1, :], in1=pr[:, 1, :], op=ALU.mult))
    V1(lambda e: e.tensor_tensor(out=w[:, DEN, :], in0=w[:, T0, :], in1=w[:, T1, :], op=ALU.add))
    V1(lambda e: e.reciprocal(out=w[:, DEN, :], in_=w[:, DEN, :]))
    V1(lambda e: e.tensor_tensor(out=w[:, T0, :], in0=w[:, NR, :], in1=w[:, LR, :], op=ALU.mult))
    V1(lambda e: e.tensor_tensor(out=w[:, T1, :], in0=w[:, AIM, :], in1=pr[:, 1, :], op=ALU.mult))
    V1(lambda e: e.tensor_tensor(out=w[:, T0, :], in0=w[:, T0, :], in1=w[:, T1, :], op=ALU.add))
    V1(lambda e: e.tensor_tensor(out=w[:, CRE, :], in0=w[:, T0, :], in1=w[:, DEN, :], op=ALU.mult))
    V1(lambda e: e.tensor_tensor(out=w[:, T0, :], in0=w[:, AIM, :], in1=w[:, LR, :], op=ALU.mult))
    V1(lambda e: e.tensor_tensor(out=w[:, T1, :], in0=w[:, NR, :], in1=pr[:, 1, :], op=ALU.mult))
    V1(lambda e: e.tensor_tensor(out=w[:, T0, :], in0=w[:, T0, :], in1=w[:, T1, :], op=ALU.subtract))
    V1(lambda e: e.tensor_tensor(out=w[:, CIM, :], in0=w[:, T0, :], in1=w[:, DEN, :], op=ALU.mult))
    V1(lambda e: e.tensor_scalar(out=w[:, T1, :], in0=w[:, TH, :], scalar1=128.0, scalar2=None, op0=ALU.mult))
    red(w, w[:, TH128, :], w, w[:, T1, :], 32, add=TWO_PI)
    tauj = P.sb("s5tauj", [128, 128], f32)
    P.dma("sp", lambda e: e.dma_start(out=tauj[:, :], in_=din["c_tau"].h[:, 0, 0:128]), reads=[din["c_tau"].k()], writes=[tauj.k()])
    P.op("dve", lambda e: e.tensor_tensor(out=PA[:, :, :], in0=tauj[:, :].unsqueeze(1).broadcast_to([128, 32, 128]),
                                          in1=w[:, TH, :].unsqueeze(2).broadcast_to([128, 32, 128]), op=ALU.mult), reads=wk + [tauj.k()], writes=[PA.k()])
    P.op("dve", lambda e: e.tensor_tensor(out=PB[:, :, :], in0=tauj[:, 0:32].unsqueeze(1).broadcast_to([128, 32, 32]),
                                          in1=w[:, TH128, :].unsqueeze(2).broadcast_to([128, 32, 32]), op=ALU.mult), reads=wk + [tauj.k()], writes=[PB.k()])
    red(PA2, PA2[:, :, :].rearrange("p a b -> p (a b)"), PA, PA[:, :, :].rearrange("p a b -> p (a b)"), 32 * 128, add=TWO_PI + math.pi / 2)
    red(PA, PA[:, :, :].rearrange("p a b -> p (a b)"), PA, PA[:, :, :].rearrange("p a b -> p (a b)"), 32 * 128, add=TWO_PI)
    red(PB, PB[:, :, :].rearrange("p a b -> p (a b)"), PB, PB[:, :, :].rearrange("p a b -> p (a b)"), 32 * 32, add=TWO_PI)
    bb = P.sb("s5bb", [128, 2, 32, 16], f32)
    bbar = P.sb("s5bbar", [128, 2, 32, 16], f32)
    cc = P.sb("s5cc", [128, 2, 32, 16], f32)
    tt_ = P.sb("s5tt", [128, 32, 16], f32)
    P.dma("sp", lambda e: e.dma_start(out=bb[:, :, :, :], in_=din["s5b"].h[:, :, :, :]), reads=[din["s5b"].k()], writes=[bb.k()])
    P.dma("sp", lambda e: e.dma_start(out=cc[:, :, :, :], in_=din["s5c"].h[:, :, :, :]), reads=[din["s5c"].k()], writes=[cc.k()])
    cre_b = w[:, CRE, :].unsqueeze(2).broadcast_to([128, 32, 16])
    cim_b = w[:, CIM, :].unsqueeze(2).broadcast_to([128, 32, 16])
    rk = wk + [bb.k()]
    P.op("dve", lambda e: e.tensor_tensor(out=bbar[:, 0, :, :], in0=bb[:, 0, :, :], in1=cre_b, op=ALU.mult), reads=rk, writes=[bbar.k(0)])
    P.op("dve", lambda e: e.tensor_tensor(out=tt_[:, :, :], in0=bb[:, 1, :, :], in1=cim_b, op=ALU.mult), reads=rk, writes=[tt_.k()])
    P.op("dve", lambda e: e.tensor_tensor(out=bbar[:, 0, :, :], in0=bbar[:, 0, :, :], in1=tt_[:, :, :], op=ALU.subtract), reads=[bbar.k(0), tt_.k()], writes=[bbar.k(0)])
    P.op("dve", lambda e: e.tensor_tensor(out=bbar[:, 1, :, :], in0=bb[:, 1, :, :], in1=cre_b, op=ALU.mult), reads=rk + [bbar.k(0)], writes=[bbar.k(1)])
    P.op("dve", lambda e: e.tensor_tensor(out=tt_[:, :, :], in0=bb[:, 0, :, :], in1=cim_b, op=ALU.mult), reads=rk + [bbar.k(0)], writes=[tt_.k()])
    P.op("dve", lambda e: e.tensor_tensor(out=bbar[:, 1, :, :], in0=bbar[:, 1, :, :], in1=tt_[:, :, :], op=ALU.add), reads=[bbar.k(1), tt_.k()], writes=[bbar.k(1)])
    Mp = [P.sb("s5M", [128, 128], f32) for _ in range(2)]
    P.op("dve", lambda e: e.memset(CT[:, :, :, :], 0.0), writes=[CT.k()])
    n = 0
    for di in range(32):
        i = di % 16
        c0 = 16 * ((2 * i) % 8)
        for r in range(2):
            M = Mp[n % 2]
            tb = B[n % 2]
            n += 1
            P.op("dve", lambda e, M=M: e.memset(M[:, :], 0.0), writes=[M.k()])
            P.op("dve", lambda e, M=M, di=di, r=r, c0=c0: e.tensor_copy(out=M[0:64, c0:c0 + 16], in_=bbar[0:64, r, di, :]), reads=[bbar.k(r)], writes=[M.k()])
            P.op("dve", lambda e, M=M, di=di, r=r, c0=c0: e.tensor_copy(out=M[64:128, c0 + 16:c0 + 32], in_=bbar[64:128, r, di, :]), reads=[bbar.k(r)], writes=[M.k()])
            P.op("pe", lambda e, M=M, tb=tb: e.transpose(out=tb.h[:, 0:128], in_=M[:, :], identity=identf[:, :]), reads=[M.k(), identf.k()], writes=[tb.k()])
            P.op("act", lambda e, tb=tb, di=di, r=r: e.copy(out=BT[:, di, r, :], in_=tb.h[:, 0:128]), reads=[tb.k()], writes=[BT.k()])
            sgn = 1.0 if r == 0 else -1.0
            P.op("dve", lambda e, di=di, r=r, c0=c0, sgn=sgn: e.tensor_scalar(out=CT[0:64, di, r, c0:c0 + 16], in0=cc[0:64, r, di, :], scalar1=sgn, scalar2=None, op0=ALU.mult),
                 reads=[cc.k()], writes=[CT.k()])
            P.op("dve", lambda e, di=di, r=r, c0=c0, sgn=sgn: e.tensor_scalar(out=CT[64:128, di, r, c0 + 16:c0 + 32], in0=cc[64:128, r, di, :], scalar1=sgn, scalar2=None, op0=ALU.mult),
                 reads=[cc.k()], writes=[CT.k()])
    P.release(m_tmp)
    bufA, bufB, bufC, bufD, bufE, bufF, bufG, bufH = [P.sb("s5buf", [128, L], f32) for _ in range(8)]
    xre = P.sb("s5xre", [128, L], BF16)
    xim = P.sb("s5xim", [128, L], BF16)
    yg = P.sb("s5yg", [128, 4, L], BF16)
    dsk = P.sb("s5dsk", [128, 4], f32)
    glb = P.sb("s5glb", [128, 4], f32)
    glw = P.sb("s5glw", [128, 4, 512], BF16)
    P.dma("sp", lambda e: e.dma_start(out=dsk[:, :], in_=din["s5d"].h[:, :]), reads=[din["s5d"].k()], writes=[dsk.k()])
    P.dma("sp", lambda e: e.dma_start(out=glb[:, :], in_=din["s5gb"].h[:, :]), reads=[din["s5gb"].k()], writes=[glb.k()])
    P.dma("pool", lambda e: e.dma_start(out=glw[:, :, :], in_=din["s5gw"].h[:, :].rearrange("(kt p) c -> p kt c", p=128)), reads=[din["s5gw"].k()], writes=[glw.k()])
    TBL = [(j * 512, 512) for j in range(4)] + [(2048, 16)]
    nbk = {"n": 0}
    s5it = {"n": 0}
    ukeys = [uT.k(t) for t in range(17)]
    Y = [B[3 + b] for b in range(5)]
    for ct in range(4):
        first = True
        for d in range(2):
            for il in range(4):
                i = ct * 4 + il
                di = d * 16 + i
                HV = [(0, 1024, [(0, 512), (512, 512)]), (1024, L, [(1024, 512), (1536, 512), (2048, 16)])]
                sn, cs = bufB, bufC

                def half_gen(hf, lane, d=d, ct=ct, di=di):
                    a, b, blks = HV[hf]
                    kk = hf

                    def TTm(o, a_, b_, op, eng="dve"):
                        P.op(eng, lambda e: e.tensor_tensor(out=o[:, a:b], in0=a_[:, a:b], in1=b_[:, a:b], op=op), reads=[a_.k(kk), b_.k(kk)], writes=[o.k(kk)])
                    for (t0, n_) in blks:
                        if d == 0:
                            rhs = uT[:, ct, t0:t0 + n_]
                        else:
                            hi = L - 1 - t0
                            lo = hi - n_
                            rhs = uT[:, ct, hi:lo:-1] if lo >= 0 else uT[:, ct, hi::-1]
                        for r, dst in ((0, bufD), (1, bufE)):
                            bk = B[nbk["n"] % 3]
                            nbk["n"] += 1
                            P.op("pe", lambda e, bk=bk, n_=n_, r=r, rhs=rhs: e.matmul(bk.h[:, 0:n_], lhsT=BT[:, di, r, :], rhs=rhs, start=True, stop=True),
                                 reads=[BT.k()] + ukeys, writes=[bk.k()])
                            P.op("act", lambda e, bk=bk, n_=n_, dst=dst, t0=t0: e.copy(out=dst[:, t0:t0 + n_], in_=bk.h[:, 0:n_]), reads=[bk.k()], writes=[dst.k(kk)])
                            yield
                    c0 = 8 * hf
                    for (pa, dstb) in ((PA, bufB), (PA2, bufC)):
                        P.op("dve", lambda e, pa=pa, dstb=dstb: e.tensor_tensor(
                            out=dstb[:, c0 * 128:(c0 + 8) * 128].rearrange("p (c j) -> p c j", j=128),
                            in0=pa[:, di, :].unsqueeze(1).broadcast_to([128, 8, 128]),
                            in1=PB[:, di, c0:c0 + 8].unsqueeze(2).broadcast_to([128, 8, 128]), op=ALU.add),
                            reads=[pa.k(), PB.k()], writes=[dstb.k(kk)])
                        yield
                        if hf == 1:
                            P.op("dve", lambda e, pa=pa, dstb=dstb: e.tensor_scalar(out=dstb[:, 2048:L], in0=pa[:, di, 0:16], scalar1=PB[:, di, 16:17], scalar2=None, op0=ALU.add),
                                 reads=[pa.k(), PB.k()], writes=[dstb.k(kk)])
                            yield
                        P.op("dve", lambda e, dstb=dstb: e.tensor_scalar(out=bufA[:, a:b], in0=dstb[:, a:b], scalar1=TWO_PI, scalar2=-TWO_PI, op0=ALU.is_ge, op1=ALU.mult),
                             reads=[dstb.k(kk)], writes=[bufA.k(kk)])
                        yield
                        P.op("dve", lambda e, dstb=dstb: e.tensor_tensor(out=dstb[:, a:b], in0=dstb[:, a:b], in1=bufA[:, a:b], op=ALU.add), reads=[dstb.k(kk), bufA.k(kk)], writes=[dstb.k(kk)])
                        yield
                        P.op("act", lambda e, dstb=dstb: e.activation(out=dstb[:, a:b], in_=dstb[:, a:b], func=AF.Sin, bias=cst[:, 1:2], scale=1.0), reads=[dstb.k(kk), cst.k()], writes=[dstb.k(kk)])
                        yield
                    br, bi = bufD, bufE
                    TTm(bufF, cs, br, ALU.mult)
                    yield
                    TTm(bufG, sn, bi, ALU.mult)
                    yield
                    TTm(bufF, bufF, bufG, ALU.add)
                    yield
                    TTm(bufG, cs, bi, ALU.mult)
                    yield
                    TTm(bufA, sn, br, ALU.mult)
                    yield
                    TTm(bufG, bufG, bufA, ALU.subtract)
                    yield
                    er = w[:, ER, di:di + 1].broadcast_to([128, b - a])
                    for (src_, dst_) in ((bufF, bufD), (bufG, bufE)):
                        if hf == 0:
                            P.op("dve", lambda e, src_=src_, dst_=dst_: e.tensor_tensor_scan(out=dst_[:, a:b], data0=er, data1=src_[:, a:b], initial=0.0, op0=ALU.mult, op1=ALU.add),
                                 reads=[src_.k(kk)] + wk, writes=[dst_.k(kk)])
                        else:
                            P.op("dve", lambda e, src_=src_, dst_=dst_: e.tensor_tensor_scan(out=dst_[:, a:b], data0=er, data1=src_[:, a:b], initial=dst_[:, a - 1:a], op0=ALU.mult, op1=ALU.add),
                                 reads=[src_.k(kk), dst_.k(0)] + wk, writes=[dst_.k(kk)])
                        yield
                    TTm(bufF, cs, bufD, ALU.mult)
                    yield
                    TTm(bufA, sn, bufE, ALU.mult)
                    yield
                    TTm(xre, bufF, bufA, ALU.subtract)
                    yield
                    TTm(bufG, sn, bufD, ALU.mult)
                    yield
                    TTm(bufH, cs, bufE, ALU.mult)
                    yield
                    TTm(xim, bufG, bufH, ALU.add)
                    yield

                run_lanes_s5([0, 1], half_gen, 2)
                if s5it["n"] < 16:
                    c.cast_pump(2, gate=[xim.k(1)])
                else:
                    c.pump(2)
                s5it["n"] += 1
                last = (d == 1 and il == 3)
                for b, (q0, nq, qtiles) in enumerate(TB):
                    for r, xs in ((0, xre), (1, xim)):
                        if d == 0:
                            rhs = xs[:, q0:q0 + nq]
                        else:
                            hi = L - 1 - q0
                            lo = hi - nq
                            rhs = xs[:, hi:lo:-1] if lo >= 0 else xs[:, hi::-1]
                        P.op("pe", lambda e, b=b, nq=nq, di=di, r=r, rhs=rhs, st_=(first and r == 0), sp_=(last and r == 1): e.matmul(
                            Y[b].h[:, 0:nq], lhsT=CT[:, di, r, :], rhs=rhs, start=st_, stop=sp_),
                            reads=[CT.k(), xs.k(0), xs.k(1)], writes=[Y[b].k()])
                first = False
        wholeA = [bufA.k(0), bufA.k(1), bufA.k()]
        wholeF = [bufF.k(0), bufF.k(1), bufF.k()]
        for b, (q0, nq, qtiles) in enumerate(TB):
            yv = bufA
            P.op("dve", lambda e, b=b, q0=q0, nq=nq, ct=ct: e.scalar_tensor_tensor(out=yv[:, q0:q0 + nq], in0=uT[:, ct, q0:q0 + nq], scalar=dsk[:, ct:ct + 1],
                                                                                     in1=Y[b].h[:, 0:nq], op0=ALU.mult, op1=ALU.add),
                 reads=ukeys + [dsk.k(), Y[b].k()], writes=wholeA)
        P.op("dve", lambda e: e.tensor_tensor(out=bufF[:, :], in0=bufA[:, :], in1=bufA[:, :], op=ALU.mult), reads=wholeA, writes=wholeF)
        P.op("dve", lambda e: e.tensor_scalar(out=bufF[:, :], in0=bufF[:, :], scalar1=0.044715, scalar2=1.0, op0=ALU.mult, op1=ALU.add), reads=wholeF, writes=wholeF)
        P.op("dve", lambda e: e.tensor_tensor(out=bufF[:, :], in0=bufF[:, :], in1=bufA[:, :], op=ALU.mult), reads=wholeF + wholeA, writes=wholeF)
        P.op("act", lambda e: e.activation(out=bufF[:, :], in_=bufF[:, :], func=AF.Sigmoid, scale=2.0 * math.sqrt(2.0 / math.pi)), reads=wholeF, writes=wholeF)
        P.op("dve", lambda e, ct=ct: e.tensor_tensor(out=yg[:, ct, :], in0=bufF[:, :], in1=bufA[:, :], op=ALU.mult), reads=wholeF + wholeA, writes=[yg.k(ct)])
    ygk = [yg.k(i) for i in range(4)]
    og = [P.sb("s5og", [128, 512], BF16) for _ in range(2)]
    sg = P.sb("s5sg", [128, 512], f32)
    n = 0
    for cto in range(4):
        for (q0, nq, qtiles) in TB:
            bk = B[n % 3]
            o = og[n % 2]
            n += 1
            for kt in range(4):
                P.op("pe", lambda e, bk=bk, nq=nq, kt=kt, cto=cto, q0=q0: e.matmul(bk.h[:, 0:nq], lhsT=glw[:, kt, cto * 128:(cto + 1) * 128], rhs=yg[:, kt, q0:q0 + nq],
                                                                                 start=(kt == 0), stop=(kt == 3)),
                     reads=ygk + [glw.k()], writes=[bk.k()])
            P.op("act", lambda e, bk=bk, nq=nq, cto=cto: e.activation(out=sg[:, 0:nq], in_=bk.h[:, 0:nq], func=AF.Sigmoid, bias=glb[:, cto:cto + 1], scale=1.0),
                 reads=[bk.k(), glb.k()], writes=[sg.k()])
            P.op("dve", lambda e, nq=nq, cto=cto, q0=q0, o=o: e.tensor_tensor(out=o[:, 0:nq], in0=yg[:, cto, q0:q0 + nq], in1=sg[:, 0:nq], op=ALU.mult),
                 reads=ygk + [sg.k()], writes=[o.k()])
            P.dma("sp", lambda e, o=o, cto=cto, q0=q0, nq=nq: e.dma_start(out=mixd.h[(12 + cto) * 128:(13 + cto) * 128, q0:q0 + nq], in_=o[:, 0:nq]),
                  reads=[o.k()], writes=[mixd.k(12 + cto)])
    P.release(m0)


NEGBIG = -1.0e30


def odd_mixer(c):
    P, B, bf, din, wb = c.P, c.B, c.bf, c.din, c.wb
    ident, identf, ones, cst, mixd = c.ident, c.identf, c.ones, c.cst, c.mixd
    nc = c.nc
    pd = c.pd
    m_all = P.mark()
    gat = P.sb("gat", [128, 17, 32], F32)
    m_proj = P.mark()
    hnT = P.sb("hnT", [128, 16, L], BF16)
    c.phase_norm_T(4, hnT, list(range(17)))
    wbufs = [P.sb("wp", [128, 16, 512], BF16) for _ in range(2)]
    rope = [P.sb("rope", [128, 4, 64], F32) for _ in range(2)]
    qf = [P.sb("qf", [128, 512], F32) for _ in range(2)]
    tA = P.sb("tA", [128, 256], F32)
    tB_ = P.sb("tB", [128, 256], F32)
    qr = [P.sb("qr", [128, 512], BF16) for _ in range(3)]
    cnt = {"n": 0}
    rope_loaded = {}

    def consumer(cb, tt, bk, cw):
        s0, p = TT[tt]
        n = cnt["n"]
        cnt["n"] += 1
        a = n % 2
        r = qr[n % 3]
        if cb == 12:
            P.op("act", lambda e: e.copy(out=gat[:p, tt, :], in_=bk.h[:p, 0:32]), reads=[bk.k()], writes=[gat.k(tt)])
            return
        if cb < 4:
            ti = 2 if cb < 2 else 0
            rp = rope[tt % 2]
            if rope_loaded.get(tt % 2) != tt:
                P.dma("sp", lambda e: e.dma_start(out=rp[:p, :, :], in_=din["c_rope_o"].h[0:p, tt, :, :]),
                      reads=[din["c_rope_o"].k()], writes=[rp.k()])
                rope_loaded[tt % 2] = tt
            f = qf[a]
            P.op("act", lambda e: e.copy(out=f[:p, :], in_=bk.h[:p, 0:512]), reads=[bk.k()], writes=[f.k()])
            f4 = f[:p, :].rearrange("q (h t f) -> q h t f", h=4, t=2, f=64)
            r4 = r[:p, :].rearrange("q (h t f) -> q h t f", h=4, t=2, f=64)
            x1 = f4[:, :, 0, :]
            x2 = f4[:, :, 1, :]
            cs_ = rp[:p, ti, :].unsqueeze(1).broadcast_to([p, 4, 64])
            sn_ = rp[:p, ti + 1, :].unsqueeze(1).broadcast_to([p, 4, 64])
            ta3 = tA[:p, :].rearrange("q (h f) -> q h f", h=4)
            tb3 = tB_[:p, :].rearrange("q (h f) -> q h f", h=4)
            rk = [f.k(), rp.k()]
            P.op("dve", lambda e: e.tensor_tensor(out=ta3, in0=x1, in1=cs_, op=ALU.mult), reads=rk, writes=[tA.k()])
            P.op("dve", lambda e: e.tensor_tensor(out=tb3, in0=x2, in1=sn_, op=ALU.mult), reads=rk, writes=[tB_.k()])
            P.op("dve", lambda e: e.tensor_tensor(out=r4[:, :, 0, :], in0=ta3, in1=tb3, op=ALU.subtract), reads=[tA.k(), tB_.k()], writes=[r.k(0)])
            P.op("dve", lambda e: e.tensor_tensor(out=ta3, in0=x1, in1=sn_, op=ALU.mult), reads=rk + [r.k(0)], writes=[tA.k()])
            P.op("dve", lambda e: e.tensor_tensor(out=tb3, in0=x2, in1=cs_, op=ALU.mult), reads=rk + [r.k(0)], writes=[tB_.k()])
            P.op("dve", lambda e: e.tensor_tensor(out=r4[:, :, 1, :], in0=ta3, in1=tb3, op=ALU.add), reads=[tA.k(), tB_.k()], writes=[r.k(1)])
        else:
            P.op("act", lambda e: e.copy(out=r[:p, :], in_=bk.h[:p, 0:512]), reads=[bk.k()], writes=[r.k(0), r.k(1)])
        P.dma("sp", lambda e: e.dma_start(out=pd.h[cb * 4:(cb + 1) * 4, 0:p, tt, :].rearrange("g q c -> q g c"), in_=r[:p, :].rearrange("q (g c) -> q g c", c=128)),
              reads=[r.k(0), r.k(1)], writes=[pd.k(cb)])

    blocks = [(i * 512, 512) for i in range(12)] + [(6144, 32)]
    c.need("owin")
    c.pump_until(50)
    c.proj_tok(wb["owin"], blocks, hnT, list(range(17)), consumer, banks=(2, 3, 4, 5), wbufs=wbufs, tiled=c.wt["owin"])
    P.release(m_proj)

    def load_tok(dst, col0, ncol=128, key=None):
        P.dma("sp", lambda e: e.dma_start(out=dst[:, :, :], in_=pd.h[col0 // 128, :, :, :]), reads=[key], writes=[dst.k()])

    tcount = {"n": 0}

    def transpose_tok(src, dstT, scale=None):
        for g0 in range(0, 17, 8):
            tiles = list(range(g0, min(17, g0 + 8)))
            tb = B[6 + tcount["n"] % 2]
            tcount["n"] += 1
            for j, tt in enumerate(tiles):
                s0, p = TT[tt]
                P.op("pe", lambda e, j=j, tt=tt, p=p, tb=tb: e.transpose(out=bf(tb)[:, j * 128:j * 128 + p], in_=src[:p, tt, 0:128], identity=ident[:p, :p]),
                     reads=[src.k(), ident.k()], writes=[tb.k()])
            for j, tt in enumerate(tiles):
                s0, p = TT[tt]
                P.op("act", lambda e, j=j, s0=s0, p=p, tb=tb: e.copy(out=dstT[:, s0:s0 + p], in_=bf(tb)[:, j * 128:j * 128 + p]), reads=[tb.k()], writes=[dstT.k()])

    NL = 3
    lnbs = []
    for _ in range(NL):
        b_ = K()
        b_.o = P.sb("ln_o", [128, 128], F32)
        b_.j = P.sb("ln_j", [128, 128], F32)
        b_.s = P.sb("ln_s", [128, 4], F32)
        lnbs.append(b_)

    def layer_norm(lane, src_t, src_ap, p, gain_ap, gain_t, out_t, out_ap):
        o, j, s = lnbs[lane].o, lnbs[lane].j, lnbs[lane].s
        P.op("act", lambda e: e.activation(out=o[:p, :], in_=src_ap, func=AF.Identity, accum_out=s[:p, 0:1]), reads=[src_t.k()], writes=[o.k(), s.k()])
        yield
        P.op("dve", lambda e: e.tensor_scalar(out=s[:p, 1:2], in0=s[:p, 0:1], scalar1=-1.0 / 128, scalar2=None, op0=ALU.mult), reads=[s.k()], writes=[s.k("m")])
        yield
        P.op("act", lambda e: e.activation(out=j[:p, :], in_=o[:p, :], func=AF.Square, bias=s[:p, 1:2], scale=1.0, accum_out=s[:p, 2:3]),
             reads=[o.k(), s.k("m")], writes=[j.k(), s.k("v")])
        yield
        P.op("act", lambda e: e.activation(out=s[:p, 3:4], in_=s[:p, 2:3], func=AF.Sqrt, bias=cst[:p, 0:1], scale=1.0 / 128), reads=[s.k("v"), cst.k()], writes=[s.k("r")])
        yield
        P.op("dve", lambda e: e.reciprocal(out=s[:p, 3:4], in_=s[:p, 3:4]), reads=[s.k("r")], writes=[s.k("r")])
        yield
        P.op("dve", lambda e: e.tensor_scalar(out=o[:p, :], in0=o[:p, :], scalar1=s[:p, 1:2], scalar2=s[:p, 3:4], op0=ALU.add, op1=ALU.mult),
             reads=[o.k(), s.k("m"), s.k("r")], writes=[o.k()])
        yield
        P.op("dve", lambda e: e.tensor_tensor(out=out_ap, in0=o[:p, :], in1=gain_ap, op=ALU.mult), reads=[o.k(), gain_t.k()], writes=[out_t.k()])
        yield

    def run_lanes(items, make_gen, W):
        it = iter(items)
        free = list(range(W))
        active = []
        while True:
            while free:
                try:
                    x_ = next(it)
                except StopIteration:
                    break
                ln_ = free.pop(0)
                active.append((make_gen(x_, ln_), ln_))
            if not active:
                break
            for ga in list(active):
                try:
                    next(ga[0])
                except StopIteration:
                    active.remove(ga)
                    free.append(ga[1])

    c.pump_until(50)
    m_ret = P.mark()
    cret = P.sb("cret", [128, 8, 128], F32)
    P.dma("sp", lambda e: e.dma_start(out=cret[:, :, :], in_=din["c_ret"].h[:, :, :]), reads=[din["c_ret"].k()], writes=[cret.k()])
    ccol = P.sb("ccol", [128, 4], F32)
    P.dma("sp", lambda e: e.dma_start(out=ccol[:, :], in_=din["c_col"].h[:, :]), reads=[din["c_col"].k()], writes=[ccol.k()])
    lg = P.sb("lg", [128, 16], F32)
    P.dma("sp", lambda e: e.dma_start(out=lg[:, :], in_=din["ret_ld"].h[0:1, :].partition_broadcast(128)), reads=[din["ret_ld"].k()], writes=[lg.k()])
    lg2 = P.sb("lg2", [128, 16], F32)
    P.op("dve", lambda e: e.tensor_scalar(out=lg2[:, :], in0=lg[:, :], scalar1=-1.0, scalar2=None, op0=ALU.mult), reads=[lg.k()], writes=[lg2.k()])
    P.op("dve", lambda e: e.tensor_tensor(out=lg[:, :], in0=lg[:, :], in1=lg2[:, :], op=ALU.min), reads=[lg.k(), lg2.k()], writes=[lg.k()])
    rnorm = P.sb("rnorm", [128, 1024], F32)
    P.dma("sp", lambda e: e.dma_start(out=rnorm[:, :], in_=din["ret_norm"].h[0:1, :].partition_broadcast(128)), reads=[din["ret_norm"].k()], writes=[rnorm.k()])
    rsets = [[P.sb("r_tok", [128, 17, 128], BF16) for _ in range(4)] for _ in range(2)]
    r_ofull = [P.sb("r_ofull", [128, L], BF16) for _ in range(2)]

    def r_loads(h):
        s_ = rsets[h % 2]
        load_tok(s_[0], h * 128, key=pd.k(h // 4))
        load_tok(s_[1], 1024 + h * 128, key=pd.k(2 + h // 4))
        load_tok(s_[2], 2048 + h * 128, key=pd.k(4 + h // 4))
        load_tok(s_[3], 3072 + h * 128, key=pd.k(6 + h // 4))
    qT = P.sb("r_qT", [128, L], BF16)
    kT = P.sb("r_kT", [128, L], BF16)
    DT = P.sb("r_DT", [128, 128], F32)
    tmpm = P.sb("r_tmpm", [128, 128], F32)
    xi = P.sb("r_xi", [128, 3, 128], F32)
    zc = P.sb("r_zc", [128, 4], F32)
    gdec = P.sb("r_g", [128, 2], F32)
    Rf = P.sb("r_Rf", [128, 128], F32)
    Rb = P.sb("r_Rb", [128, 128], F32)
    R16f = P.sb("r_R16f", [128, 17, 128], BF16)
    R16b = P.sb("r_R16b", [128, 17, 128], BF16)
    W_R = 3
    kz = [P.sb("r_kz", [128, 128], BF16) for _ in range(3)]
    kvs = [P.sb("r_kvs", [128, 17, 128], F32) for _ in range(2)]
    PTt = [P.sb("r_PT", [128, 128], BF16) for _ in range(W_R)]
    qx = [P.sb("r_qx", [128, 2, 128], BF16) for _ in range(W_R)]
    yb = [P.sb("r_yb", [128, 128], BF16) for _ in range(W_R)]
    yfl = [P.sb("r_yf", [128, 128], F32) for _ in range(W_R)]
    sgl = [P.sb("r_sg", [128, 128], F32) for _ in range(W_R)]
    oT = [P.sb("r_oT", [128, 128], BF16) for _ in range(W_R)]
    pdk = [pd.k(i) for i in range(12)]
    r_loads(0)
    for h in range(8):
        qtok, ktok, vtok, gtok = rsets[h % 2]
        if h + 1 < 8:
            r_loads(h + 1)
        transpose_tok(qtok, qT)
        transpose_tok(ktok, kT)
        lf_ = lg[:, h:h + 1]
        lb_ = lg[:, 8 + h:9 + h]
        P.op("act", lambda e, lf_=lf_: e.activation(out=DT[:, :], in_=cret[:, 0, :], func=AF.Exp, scale=lf_), reads=[cret.k(), lg.k()], writes=[DT.k()])
        P.op("dve", lambda e: e.tensor_tensor(out=DT[:, :], in0=DT[:, :], in1=cret[:, 2, :], op=ALU.mult), reads=[DT.k(), cret.k()], writes=[DT.k()])
        P.op("act", lambda e, lb_=lb_: e.activation(out=tmpm[:, :], in_=cret[:, 1, :], func=AF.Exp, scale=lb_), reads=[cret.k(), lg.k()], writes=[tmpm.k()])
        P.op("dve", lambda e: e.tensor_tensor(out=tmpm[:, :], in0=tmpm[:, :], in1=cret[:, 3, :], op=ALU.mult), reads=[tmpm.k(), cret.k()], writes=[tmpm.k()])
        P.op("dve", lambda e: e.tensor_tensor(out=DT[:, :], in0=DT[:, :], in1=tmpm[:, :], op=ALU.add), reads=[DT.k(), tmpm.k()], writes=[DT.k()])
        P.op("act", lambda e, lf_=lf_: e.activation(out=xi[:, 0, :], in_=cret[:, 4, :], func=AF.Exp, scale=lf_), reads=[cret.k(), lg.k()], writes=[xi.k()])
        P.op("act", lambda e, lb_=lb_: e.activation(out=xi[:, 1, :], in_=cret[:, 5, :], func=AF.Exp, scale=lb_), reads=[cret.k(), lg.k()], writes=[xi.k()])
        P.op("act", lambda e, lb_=lb_: e.activation(out=xi[:, 2, :], in_=cret[:, 6, :], func=AF.Exp, scale=lb_), reads=[cret.k(), lg.k()], writes=[xi.k()])
        P.op("act", lambda e, lf_=lf_: e.activation(out=zc[:, 0:1], in_=ccol[:, 0:1], func=AF.Exp, scale=lf_), reads=[ccol.k(), lg.k()], writes=[zc.k()])
        P.op("act", lambda e, lf_=lf_: e.activation(out=zc[:, 1:2], in_=ccol[:, 1:2], func=AF.Exp, scale=lf_), reads=[ccol.k(), lg.k()], writes=[zc.k()])
        P.op("act", lambda e, lb_=lb_: e.activation(out=zc[:, 2:3], in_=ccol[:, 2:3], func=AF.Exp, scale=lb_), reads=[ccol.k(), lg.k()], writes=[zc.k()])
        P.op("act", lambda e, lf_=lf_: e.activation(out=gdec[:, 0:1], in_=lf_, func=AF.Exp, scale=128.0), reads=[lg.k()], writes=[gdec.k()])
        P.op("act", lambda e, lb_=lb_: e.activation(out=gdec[:, 1:2], in_=lb_, func=AF.Exp, scale=128.0), reads=[lg.k()], writes=[gdec.k()])
        def kv_gen(item, lane, ktok=ktok, vtok=vtok):
            d, n = item
            s0, p = TT[n]
            z = kz[lane]
            bk = B[lane]
            if d == 0:
                zcol = zc[:p, 1:2] if n == 0 else zc[:p, 0:1]
            else:
                zcol = zc[:p, 2:3]
            P.op("dve", lambda e: e.tensor_scalar(out=z[:p, :], in0=ktok[:p, n, :], scalar1=zcol, scalar2=None, op0=ALU.mult),
                 reads=[ktok.k(), zc.k()], writes=[z.k()])
            yield
            P.op("pe", lambda e: e.matmul(bk.h[:, 0:128], lhsT=z[:p, :], rhs=vtok[:p, n, :], start=True, stop=True),
                 reads=[z.k(), vtok.k()], writes=[bk.k()])
            yield
            P.op("act", lambda e: e.copy(out=kvs[d][:, n, :], in_=bk.h[:, 0:128]), reads=[bk.k()], writes=[kvs[d].k(n)])
            yield

        run_lanes([(0, n) for n in range(16)] + [(1, n) for n in range(16, 0, -1)], kv_gen, 3)

        def rec_gen(d, lane):
            if d == 0:
                for n in range(16):
                    if n == 0:
                        P.op("dve", lambda e: e.tensor_copy(out=Rf[:, :], in_=kvs[0][:, 0, :]), reads=[kvs[0].k(0)], writes=[Rf.k()])
                    else:
                        P.op("dve", lambda e, n=n: e.scalar_tensor_tensor(out=Rf[:, :], in0=Rf[:, :], scalar=gdec[:, 0:1], in1=kvs[0][:, n, :], op0=ALU.mult, op1=ALU.add),
                             reads=[Rf.k(), gdec.k(), kvs[0].k(n)], writes=[Rf.k()])
                    yield
                    P.op("pool", lambda e, n=n: e.tensor_copy(out=R16f[:, n + 1, :], in_=Rf[:, :]), reads=[Rf.k()], writes=[R16f.k(n + 1)])
                    yield
            else:
                for n in range(16, 0, -1):
                    if n == 16:
                        P.op("dve", lambda e: e.tensor_copy(out=Rb[:, :], in_=kvs[1][:, 16, :]), reads=[kvs[1].k(16)], writes=[Rb.k()])
                    else:
                        P.op("dve", lambda e, n=n: e.scalar_tensor_tensor(out=Rb[:, :], in0=Rb[:, :], scalar=gdec[:, 1:2], in1=kvs[1][:, n, :], op0=ALU.mult, op1=ALU.add),
                             reads=[Rb.k(), gdec.k(), kvs[1].k(n)], writes=[Rb.k()])
                    yield
                    P.op("pool", lambda e, n=n: e.tensor_copy(out=R16b[:, n - 1, :], in_=Rb[:, :]), reads=[Rb.k()], writes=[R16b.k(n - 1)])
                    yield

        run_lanes([0, 1], rec_gen, 2)

        ofull = r_ofull[h % 2]

        def out_gen(n, lane, h=h, vtok=vtok, gtok=gtok, ofull=ofull):
            if n % 4 == 0:
                c.pump(1)
            s0, p = TT[n]
            Sb = B[2 + 2 * lane] if lane < 3 else None
            Ob = B[3 + 2 * lane]
            pt = PTt[lane]
            x_ = qx[lane]
            yf = yfl[lane]
            sg = sgl[lane]
            y = yb[lane]
            o_ = oT[lane]
            P.op("pe", lambda e: e.matmul(Sb.h[:p, 0:p], lhsT=kT[:, s0:s0 + p], rhs=qT[:, s0:s0 + p], start=True, stop=True),
                 reads=[kT.k(), qT.k()], writes=[Sb.k()])
            yield
            P.op("dve", lambda e: e.tensor_tensor(out=pt[:p, 0:p], in0=Sb.h[:p, 0:p], in1=DT[:p, 0:p], op=ALU.mult),
                 reads=[Sb.k(), DT.k()], writes=[pt.k()])
            yield
            if n >= 1:
                P.op("dve", lambda e: e.tensor_tensor(out=x_[:, 0, 0:p], in0=qT[:, s0:s0 + p], in1=xi[:, 0, 0:p], op=ALU.mult),
                     reads=[qT.k(), xi.k()], writes=[x_.k(0)])
                yield
            if n <= 15:
                xsel = 2 if n == 0 else 1
                P.op("dve", lambda e: e.tensor_tensor(out=x_[:, 1, 0:p], in0=qT[:, s0:s0 + p], in1=xi[:, xsel, 0:p], op=ALU.mult),
                     reads=[qT.k(), xi.k()], writes=[x_.k(1)])
                yield
            P.op("pe", lambda e: e.matmul(Ob.h[:p, 0:128], lhsT=pt[:p, 0:p], rhs=vtok[:p, n, :], start=True, stop=False),
                 reads=[pt.k(), vtok.k()], writes=[Ob.k()])
            if n >= 1:
                P.op("pe", lambda e: e.matmul(Ob.h[:p, 0:128], lhsT=x_[:, 0, 0:p], rhs=R16f[:, n, :], start=False, stop=(n == 16)),
                     reads=[x_.k(0), R16f.k(n)], writes=[Ob.k()])
            if n <= 15:
                P.op("pe", lambda e: e.matmul(Ob.h[:p, 0:128], lhsT=x_[:, 1, 0:p], rhs=R16b[:, n, :], start=False, stop=True),
                     reads=[x_.k(1), R16b.k(n)], writes=[Ob.k()])
            yield
            yield from layer_norm(lane, Ob, Ob.h[:p, 0:128], p, rnorm[:p, h * 128:(h + 1) * 128], rnorm, yf, yf[:p, :])
            P.op("act", lambda e: e.activation(out=sg[:p, :], in_=gtok[:p, n, :], func=AF.Sigmoid), reads=[gtok.k()], writes=[sg.k()])
            yield
            P.op("dve", lambda e: e.tensor_tensor(out=sg[:p, :], in0=sg[:p, :], in1=gtok[:p, n, :], op=ALU.mult), reads=[sg.k(), gtok.k()], writes=[sg.k()])
            yield
            P.op("dve", lambda e: e.tensor_tensor(out=y[:p, :], in0=yf[:p, :], in1=sg[:p, :], op=ALU.mult), reads=[yf.k(), sg.k()], writes=[y.k()])
            yield
            P.op("pe", lambda e: e.transpose(out=bf(Sb)[:, 0:p], in_=y[:p, :], identity=ident[:p, :p]), reads=[y.k(), ident.k()], writes=[Sb.k()])
            yield
            P.op("act", lambda e: e.copy(out=ofull[:, s0:s0 + p], in_=bf(Sb)[:, 0:p]), reads=[Sb.k()], writes=[ofull.k(n)])
            yield

        run_lanes(list(range(17)), out_gen, W_R)
        P.dma("sp", lambda e, h=h, ofull=ofull: e.dma_start(out=mixd.h[h * 128:(h + 1) * 128, :], in_=ofull[:, :]),
              reads=[ofull.k(n_) for n_ in range(17)], writes=[mixd.k(h)])
    P.release(m_ret)
    mlstm_phase(c, gat, load_tok, transpose_tok, layer_norm, run_lanes)
    P.release(m_all)


def mlstm_phase(c, gat, load_tok, transpose_tok, layer_norm, run_lanes):
    P, B, bf, din = c.P, c.B, c.bf, c.din
    ident, identf, ones, cst, mixd, pd = c.ident, c.identf, c.ones, c.cst, c.mixd, c.pd
    m0 = P.mark()
    f32 = F32
    scal = P.sb("m_scal", [128, 17, 2, 5, 8], f32)
    FP = [P.sb("m_FP", [128, 2, 8, 17], f32) for _ in range(2)]
    m_rows = P.mark()
    gb4 = P.sb("m_gb", [8, 4], f32)
    P.dma("sp", lambda e: e.dma_start(out=gb4[:, :], in_=din["ml_gb"].h[:, :]), reads=[din["ml_gb"].k()], writes=[gb4.k()])
    rowI = P.sb("m_rowI", [8, L], f32)
    rowF = P.sb("m_rowF", [8, L], f32)
    crow = P.sb("m_crow", [8, 4, L], f32)
    P.dma("sp", lambda e: e.dma_start(out=crow[:, 0, :], in_=din["c_rows"].h[:, 0, :]), reads=[din["c_rows"].k()], writes=[crow.k()])
    P.dma("sp", lambda e: e.dma_start(out=crow[:, 1, :], in_=din["c_rows"].h[:, 2, :]), reads=[din["c_rows"].k()], writes=[crow.k()])
    tA_, tB2, tC, tD, tE, tO = [P.sb("m_wk", [8, L], f32) for _ in range(6)]
    ch17 = P.sb("m_ch17", [8, 12, 17], f32)
    id8 = P.sb("m_id8", [8, 8], f32)
    P.op("dve", lambda e: e.tensor_copy(out=id8[:, :], in_=identf[0:8, 0:8]), reads=[identf.k()], writes=[id8.k()])
    onesf = P.sb("m_onesf", [8, 128], f32)
    P.op("dve", lambda e: e.memset(onesf[:, :], 1.0), writes=[onesf.k()])
    rhsD = P.sb("m_rhsD", [8, 8, 17], f32)

    def V(fn, reads, writes):
        P.op("dve", fn, reads=[t.k() for t in reads], writes=[t.k() for t in writes])

    def A(fn, reads, writes):
        P.op("act", fn, reads=[t.k() for t in reads], writes=[t.k() for t in writes])

    ntr = {"n": 0}
    for d in range(2):
        for tt in range(17):
            s0, p = TT[tt]
            for kk, dst in ((2 * d, rowI), (2 * d + 1, rowF)):
                tb = B[ntr["n"] % 2]
                ntr["n"] += 1
                P.op("pe", lambda e, tb=tb, tt=tt, p=p, kk=kk: e.transpose(out=tb.h[0:8, 0:p], in_=gat[:p, tt, kk * 8:(kk + 1) * 8], identity=identf[:p, :p]),
                     reads=[gat.k(tt), identf.k()], writes=[tb.k()])
                P.op("act", lambda e, tb=tb, s0=s0, p=p, kk=kk, dst=dst: e.activation(out=dst[:, s0:s0 + p], in_=tb.h[0:8, 0:p], func=AF.Identity,
                                                                                     bias=gb4[:, kk:kk + 1], scale=1.0),
                     reads=[tb.k(), gb4.k()], writes=[dst.k()])
        LI, LF, BT_, CS_, CM = tA_, tB2, tC, tD, tE
        if d == 0:
            V(lambda e: e.tensor_copy(out=LI[:, :], in_=rowI[:, :]), [rowI], [LI])
            V(lambda e: e.tensor_copy(out=LF[:, :], in_=rowF[:, :]), [rowF], [LF])

            def cend(t):
                return [(t[:, 15:16], 0, 1), (t[:, 16:L].rearrange("h (c j) -> h c j", j=128)[:, :, 127], 1, 17)]

            def cview(t):
                return t[:, 16:L].rearrange("h (c j) -> h c j", j=128), t[:, 0:16], 1, 0
        else:
            V(lambda e: e.tensor_copy(out=LI[:, :], in_=rowI[:, ::-1]), [rowI], [LI])
            V(lambda e: e.tensor_copy(out=LF[:, :], in_=rowF[:, ::-1]), [rowF], [LF])

            def cend(t):
                return [(t[:, 0:2048].rearrange("h (c j) -> h c j", j=128)[:, :, 127], 0, 16), (t[:, L - 1:L], 16, 17)]

            def cview(t):
                return t[:, 0:2048].rearrange("h (c j) -> h c j", j=128), t[:, 2048:L], 0, 16
        rm = crow[:, d, :]
        A(lambda e: e.activation(out=LF[:, :], in_=LF[:, :], func=AF.Exp, scale=-1.0), [LF], [LF])
        A(lambda e: e.activation(out=LF[:, :], in_=LF[:, :], func=AF.Ln, bias=1.0, scale=1.0), [LF], [LF])
        V(lambda e: e.tensor_scalar(out=LF[:, :], in0=LF[:, :], scalar1=-1.0, scalar2=None, op0=ALU.mult), [LF], [LF])
        V(lambda e, rm=rm: e.tensor_tensor_scan(out=BT_[:, :], data0=rm, data1=LF[:, :], initial=0.0, op0=ALU.mult, op1=ALU.add), [crow, LF], [BT_])
        V(lambda e: e.tensor_tensor(out=CS_[:, :], in0=LI[:, :], in1=BT_[:, :], op=ALU.subtract), [LI, BT_], [CS_])
        PEN = LF
        V(lambda e, rm=rm: e.tensor_scalar(out=PEN[:, :], in0=rm, scalar1=-1.0, scalar2=1.0e30, op0=ALU.add, op1=ALU.mult), [crow, BT_], [PEN])
        V(lambda e: e.tensor_tensor_scan(out=CM[:, :], data0=PEN[:, :], data1=CS_[:, :], initial=NEGBIG, op0=ALU.add, op1=ALU.max), [PEN, CS_], [CM])
        for (src, c0, c1) in cend(BT_):
            V(lambda e, src=src, c0=c0, c1=c1: e.tensor_copy(out=ch17[:, 0, c0:c1], in_=src), [BT_], [ch17])
        for (src, c0, c1) in cend(CM):
            V(lambda e, src=src, c0=c0, c1=c1: e.tensor_copy(out=ch17[:, 1, c0:c1], in_=src), [CM], [ch17])
        BL, CML, MLOC, MN, MP_, FPV, FLC, T17 = [ch17[:, i, :] for i in range(8)]
        V(lambda e: e.tensor_tensor(out=MLOC, in0=BL, in1=CML, op=ALU.add), [ch17], [ch17])
        V(lambda e: e.tensor_tensor_scan(out=MN, data0=BL, data1=MLOC, initial=0.0, op0=ALU.add, op1=ALU.max), [ch17], [ch17])
        V(lambda e: e.memset(ch17[:, 4, 0:1], 0.0), [ch17], [ch17])
        V(lambda e: e.tensor_copy(out=ch17[:, 4, 1:17], in_=ch17[:, 3, 0:16]), [ch17], [ch17])
        V(lambda e: e.tensor_tensor(out=T17, in0=BL, in1=MP_, op=ALU.add), [ch17], [ch17])
        V(lambda e: e.tensor_tensor(out=T17, in0=T17, in1=MN, op=ALU.subtract), [ch17], [ch17])
        A(lambda e: e.activation(out=FPV, in_=T17, func=AF.Exp), [ch17], [ch17])
        V(lambda e: e.tensor_tensor(out=T17, in0=MLOC, in1=MN, op=ALU.subtract), [ch17], [ch17])
        A(lambda e: e.activation(out=FLC, in_=T17, func=AF.Exp), [ch17], [ch17])
        for wi, srcv in ((0, FPV), (1, FLC)):
            V(lambda e, srcv=srcv: e.tensor_tensor(out=rhsD[:, :, :], in0=srcv.unsqueeze(1).broadcast_to([8, 8, 17]),
                                                   in1=id8[:, :].unsqueeze(2).broadcast_to([8, 8, 17]), op=ALU.mult), [ch17, id8], [rhsD])
            bk = B[2 + wi]
            P.op("pe", lambda e, bk=bk: e.matmul(bk.h[:, 0:136], lhsT=onesf[:, :], rhs=rhsD[:, :, :].rearrange("k h n -> k (h n)"), start=True, stop=True),
                 reads=[onesf.k(), rhsD.k()], writes=[bk.k()])
            P.op("act", lambda e, bk=bk, wi=wi, d=d: e.copy(out=FP[d][:, wi, :, :].rearrange("p h n -> p (h n)"), in_=bk.h[:, 0:136]), reads=[bk.k()], writes=[FP[d].k()])
        MT = CM
        real, metav, cr0, cm0 = cview(MT)
        V(lambda e, real=real, cr0=cr0: e.tensor_tensor(out=real, in0=real, in1=ch17[:, 4, cr0:cr0 + 16].unsqueeze(2).broadcast_to([8, 16, 128]), op=ALU.max), [CM, ch17], [MT])
        V(lambda e, metav=metav, cm0=cm0: e.tensor_scalar(out=metav, in0=metav, scalar1=ch17[:, 4, cm0:cm0 + 1], scalar2=None, op0=ALU.max), [CM, ch17], [MT])
        tmp5 = tA_

        def fin(kd, compute, d=d, tmp5=tmp5):
            if d == 0:
                compute(tO)
            else:
                compute(tmp5)
                V(lambda e: e.tensor_copy(out=tO[:, :], in_=tmp5[:, ::-1]), [tmp5], [tO])
            tb = B[4 + kd % 2]
            for tt in range(17):
                s0, p = TT[tt]
                P.op("pe", lambda e, tb=tb, tt=tt, s0=s0, p=p: e.transpose(out=tb.h[:p, tt * 8:(tt + 1) * 8], in_=tO[:, s0:s0 + p], identity=id8[:, :]),
                     reads=[tO.k(), id8.k()], writes=[tb.k()])
            P.op("act", lambda e, tb=tb, d=d, kd=kd: e.copy(out=scal[:, :, d, kd, :], in_=tb.h[:, 0:136].rearrange("p (t h) -> p t h", h=8)), reads=[tb.k()], writes=[scal.k()])

        fin(0, lambda t: A(lambda e: e.activation(out=t[:, :], in_=CS_[:, :], func=AF.Exp), [CS_], [t]))
        fin(1, lambda t: A(lambda e: e.activation(out=t[:, :], in_=MT[:, :], func=AF.Exp, scale=-1.0), [MT], [t]))

        def winter(t, cview=cview, MT=MT):
            r_, m_, cr0_, cm0_ = cview(t)
            rmt, mmt, _, _ = cview(MT)
            V(lambda e: e.tensor_tensor(out=r_, in0=ch17[:, 4, cr0_:cr0_ + 16].unsqueeze(2).broadcast_to([8, 16, 128]), in1=rmt, op=ALU.subtract), [MT, ch17], [t])
            V(lambda e: e.tensor_scalar(out=m_, in0=mmt, scalar1=-1.0, scalar2=ch17[:, 4, cm0_:cm0_ + 1], op0=ALU.mult, op1=ALU.add), [MT, ch17], [t])
            A(lambda e: e.activation(out=t[:, :], in_=t[:, :], func=AF.Exp), [t], [t])
        fin(2, winter)

        def enegm(t, MT=MT):
            V(lambda e: e.tensor_tensor(out=t[:, :], in0=BT_[:, :], in1=MT[:, :], op=ALU.add), [BT_, MT], [t])
            A(lambda e: e.activation(out=t[:, :], in_=t[:, :], func=AF.Exp, scale=-1.0), [t], [t])
        fin(3, enegm)

        def wsum(t, cview=cview):
            r_, m_, cr0_, cm0_ = cview(t)
            rcs, mcs, _, _ = cview(CS_)
            V(lambda e: e.tensor_tensor(out=r_, in0=rcs, in1=ch17[:, 1, cr0_:cr0_ + 16].unsqueeze(2).broadcast_to([8, 16, 128]), op=ALU.subtract), [CS_, ch17], [t])
            V(lambda e: e.tensor_scalar(out=m_, in0=mcs, scalar1=ch17[:, 1, cm0_:cm0_ + 1], scalar2=None, op0=ALU.subtract), [CS_, ch17], [t])
            A(lambda e: e.activation(out=t[:, :], in_=t[:, :], func=AF.Exp), [t], [t])
        fin(4, wsum)
    P.release(m_rows)

    cw_ = P.sb("m_cw", [128, 8, 5], f32)
    cb_ = P.sb("m_cb", [128, 8], f32)
    P.dma("sp", lambda e: e.dma_start(out=cw_[:, :, :], in_=din["ml_cw"].h[:, :, :]), reads=[din["ml_cw"].k()], writes=[cw_.k()])
    P.dma("sp", lambda e: e.dma_start(out=cb_[:, :], in_=din["ml_cb"].h[:, :]), reads=[din["ml_cb"].k()], writes=[cb_.k()])
    wq = P.sb("m_wq", [128, 8, 128], BF16)
    wkk = P.sb("m_wk2", [128, 8, 128], BF16)
    wv = P.sb("m_wv", [128, 8, 128], BF16)
    for nm_, t_ in (("ml_wq", wq), ("ml_wk", wkk), ("ml_wv", wv)):
        P.dma("pool", lambda e, nm_=nm_, t_=t_: e.dma_start(out=t_[:, :, :], in_=din[nm_].h[:, :, :].rearrange("h d e -> d h e")), reads=[din[nm_].k()], writes=[t_.k()])
    mnorm = P.sb("m_norm", [128, 1024], f32)
    P.dma("sp", lambda e: e.dma_start(out=mnorm[:, :], in_=din["ml_norm"].h[0:1, :].partition_broadcast(128)), reads=[din["ml_norm"].k()], writes=[mnorm.k()])
    cmask = P.sb("m_mask", [128, 2, 128], f32)
    P.dma("sp", lambda e: e.dma_start(out=cmask[:, :, :], in_=din["c_mmask"].h[:, :, :]), reads=[din["c_mmask"].k()], writes=[cmask.k()])
    msets = [[P.sb("m_tok", [128, 17, 128], BF16) for _ in range(2)] for _ in range(2)]
    m_ofull = [P.sb("m_ofull", [128, L], BF16) for _ in range(2)]

    def m_loads(h):
        s_ = msets[h % 2]
        load_tok(s_[0], 4096 + h * 128, key=pd.k(8 + h // 4))
        load_tok(s_[1], 5120 + h * 128, key=pd.k(10 + h // 4))
    muT = P.sb("m_muT", [128, L], BF16)
    acc = P.sb("m_acc", [128, L], f32)
    sgm = P.sb("m_sgm", [128, L], f32)
    ucT = P.sb("m_ucT", [128, L], BF16)
    mqT = P.sb("m_mqT", [128, L], BF16)
    mkT = P.sb("m_mkT", [128, L], BF16)
    ktok = P.sb("m_ktok", [128, 17, 128], BF16)
    vaug = P.sb("m_vaug", [128, 17, 129], BF16)
    Cst = [P.sb("m_Cst", [128, 129], f32) for _ in range(2)]
    C16 = [P.sb("m_C16", [128, 17, 129], BF16) for _ in range(2)]
    cls = [P.sb("m_cls", [128, 17, 129], f32) for _ in range(2)]
    kw = [P.sb("m_kw", [128, 128], BF16) for _ in range(2)]
    W_M = 2
    lanes = []
    for _ in range(W_M):
        b_ = K()
        b_.va = [P.sb("m_va", [128, 129], BF16) for _ in range(2)]
        b_.pt = [P.sb("m_PT", [128, 128], BF16) for _ in range(2)]
        b_.t129 = [P.sb("m_t129", [128, 129], f32) for _ in range(2)]
        b_.num = [P.sb("m_num", [128, 129], f32) for _ in range(2)]
        b_.dn = [P.sb("m_dn", [128, 2], f32) for _ in range(2)]
        b_.hsum = P.sb("m_hsum", [128, 128], f32)
        b_.yf = P.sb("m_yf", [128, 128], f32)
        b_.sg = P.sb("m_sg", [128, 128], f32)
        b_.yb = P.sb("m_yb", [128, 128], BF16)
        b_.oT = P.sb("m_oT", [128, 128], BF16)
        lanes.append(b_)
    TBL = [(0, 16)] + [(16 + 512 * j, 512) for j in range(4)]
    m_loads(0)
    for h in range(8):
        mutok, motok = msets[h % 2]
        if h + 1 < 8:
            m_loads(h + 1)
        transpose_tok(mutok, muT)
        P.op("dve", lambda e, h=h: e.tensor_scalar(out=acc[:, :], in0=muT[:, :], scalar1=cw_[:, h, 2:3], scalar2=cb_[:, h:h + 1], op0=ALU.mult, op1=ALU.add),
             reads=[muT.k(), cw_.k(), cb_.k()], writes=[acc.k()])
        for k in (0, 1, 3, 4):
            sh = k - 2
            lo = max(0, -sh)
            hi = min(L, L - sh)
            P.op("dve", lambda e, h=h, k=k, lo=lo, hi=hi, sh=sh: e.scalar_tensor_tensor(out=acc[:, lo:hi], in0=muT[:, lo + sh:hi + sh], scalar=cw_[:, h, k:k + 1],
                                                                                       in1=acc[:, lo:hi], op0=ALU.mult, op1=ALU.add),
                 reads=[muT.k(), cw_.k(), acc.k()], writes=[acc.k()])
        P.op("act", lambda e: e.activation(out=sgm[:, :], in_=acc[:, :], func=AF.Sigmoid), reads=[acc.k()], writes=[sgm.k()])
        P.op("dve", lambda e: e.tensor_tensor(out=ucT[:, :], in0=acc[:, :], in1=sgm[:, :], op=ALU.mult), reads=[acc.k(), sgm.k()], writes=[ucT.k()])
        nb = 0
        for (t0, n_) in TBL:
            for (wsel, dst, sc) in ((wq, mqT, 1.0), (wkk, mkT, 128.0 ** -0.5)):
                bk = B[nb % 2]
                nb += 1
                P.op("pe", lambda e, bk=bk, wsel=wsel, h=h, t0=t0, n_=n_: e.matmul(bk.h[:, 0:n_], lhsT=wsel[:, h, :], rhs=ucT[:, t0:t0 + n_], start=True, stop=True),
                     reads=[wsel.k(), ucT.k()], writes=[bk.k()])
                P.op("act", lambda e, bk=bk, dst=dst, sc=sc, t0=t0, n_=n_: e.activation(out=dst[:, t0:t0 + n_], in_=bk.h[:, 0:n_], func=AF.Copy, scale=sc),
                     reads=[bk.k()], writes=[dst.k()])
        P.op("dve", lambda e: e.memset(vaug[:, :, 128:129], 1.0), writes=[vaug.k()])
        for tt in range(17):
            s0, p = TT[tt]
            bk = B[nb % 2]
            nb += 1
            P.op("pe", lambda e, bk=bk, s0=s0, p=p, h=h: e.matmul(bk.h[:p, 0:128], lhsT=ucT[:, s0:s0 + p], rhs=wkk[:, h, :], start=True, stop=True),
                 reads=[ucT.k(), wkk.k()], writes=[bk.k()])
            P.op("act", lambda e, bk=bk, tt=tt, p=p: e.activation(out=ktok[:p, tt, :], in_=bk.h[:p, 0:128], func=AF.Copy, scale=128.0 ** -0.5), reads=[bk.k()], writes=[ktok.k()])
            bk = B[nb % 2]
            nb += 1
            P.op("pe", lambda e, bk=bk, s0=s0, p=p, h=h: e.matmul(bk.h[:p, 0:128], lhsT=muT[:, s0:s0 + p], rhs=wv[:, h, :], start=True, stop=True),
                 reads=[muT.k(), wv.k()], writes=[bk.k()])
            P.op("act", lambda e, bk=bk, tt=tt, p=p: e.copy(out=vaug[:p, tt, 0:128], in_=bk.h[:p, 0:128]), reads=[bk.k()], writes=[vaug.k()])
        orders = [list(range(17)), [16 - c_ for c_ in range(16)] + [0]]

        def cl_gen(item, lane, h=h):
            d, tt = item
            s0, p = TT[tt]
            kw_ = kw[lane]
            bk = B[lane]
            P.op("dve", lambda e: e.tensor_scalar(out=kw_[:p, :], in0=ktok[:p, tt, :], scalar1=scal[:p, tt, d, 4, h:h + 1], scalar2=None, op0=ALU.mult),
                 reads=[ktok.k(), scal.k()], writes=[kw_.k()])
            yield
            P.op("pe", lambda e: e.matmul(bk.h[:, 0:129], lhsT=kw_[:p, :], rhs=vaug[:p, tt, :], start=True, stop=True),
                 reads=[kw_.k(), vaug.k()], writes=[bk.k()])
            yield
            P.op("act", lambda e: e.copy(out=cls[d][:, tt, :], in_=bk.h[:, 0:129]), reads=[bk.k()], writes=[cls[d].k(tt)])
            yield

        run_lanes([(d, tt) for d in range(2) for tt in orders[d][:16]], cl_gen, 2)

        def st_gen(d, lane, h=h):
            P.op("dve", lambda e: e.memset(Cst[d][:, :], 0.0), writes=[Cst[d].k()])
            yield
            for ci, tt in enumerate(orders[d]):
                P.op("pool", lambda e, tt=tt: e.tensor_copy(out=C16[d][:, tt, :], in_=Cst[d][:, :]), reads=[Cst[d].k()], writes=[C16[d].k(tt)])
                yield
                if ci == 16:
                    break
                P.op("dve", lambda e, ci=ci: e.tensor_scalar(out=Cst[d][:, :], in0=Cst[d][:, :], scalar1=FP[d][:, 0, h, ci:ci + 1], scalar2=None, op0=ALU.mult),
                     reads=[Cst[d].k(), FP[d].k()], writes=[Cst[d].k()])
                yield
                P.op("dve", lambda e, ci=ci, tt=tt: e.scalar_tensor_tensor(out=Cst[d][:, :], in0=cls[d][:, tt, :], scalar=FP[d][:, 1, h, ci:ci + 1], in1=Cst[d][:, :],
                                                                          op0=ALU.mult, op1=ALU.add),
                     reads=[Cst[d].k(), FP[d].k(), cls[d].k(tt)], writes=[Cst[d].k()])
                yield

        run_lanes([0, 1], st_gen, 2)

        ofull = m_ofull[h % 2]

        def ob_gen(tt, lane, h=h, motok=motok, ofull=ofull):
            s0, p = TT[tt]
            L_ = lanes[lane]
            Sb = B[2 + 3 * lane]
            Ib = B[3 + 3 * lane]
            Eb = B[4 + 3 * lane]
            P.op("pe", lambda e: e.matmul(Sb.h[:p, 0:p], lhsT=mkT[:, s0:s0 + p], rhs=mqT[:, s0:s0 + p], start=True, stop=True),
                 reads=[mkT.k(), mqT.k()], writes=[Sb.k()])
            yield
            for d in range(2):
                pt = L_.pt[d]
                va_ = L_.va[d]
                P.op("dve", lambda e, pt=pt, d=d: e.tensor_tensor(out=pt[:p, 0:p], in0=Sb.h[:p, 0:p], in1=cmask[:p, d, 0:p], op=ALU.mult),
                     reads=[Sb.k(), cmask.k()], writes=[pt.k()])
                yield
                P.op("dve", lambda e, va_=va_, d=d: e.tensor_scalar(out=va_[:p, :], in0=vaug[:p, tt, :], scalar1=scal[:p, tt, d, 0, h:h + 1], scalar2=None, op0=ALU.mult),
                     reads=[vaug.k(), scal.k()], writes=[va_.k()])
                yield
            for d in range(2):
                pt = L_.pt[d]
                va_ = L_.va[d]
                c0 = d * 256
                P.op("pe", lambda e, pt=pt, va_=va_, c0=c0: e.matmul(Ib.h[:p, c0:c0 + 129], lhsT=pt[:p, 0:p], rhs=va_[:p, :], start=True, stop=True),
                     reads=[pt.k(), va_.k()], writes=[Ib.k()])
                P.op("pe", lambda e, d=d, c0=c0: e.matmul(Eb.h[:p, c0:c0 + 129], lhsT=mqT[:, s0:s0 + p], rhs=C16[d][:, tt, :], start=True, stop=True),
                     reads=[mqT.k(), C16[d].k(tt)], writes=[Eb.k()])
            yield
            for d in range(2):
                c0 = d * 256
                t129, num, dn = L_.t129[d], L_.num[d], L_.dn[d]
                P.op("act", lambda e, d=d, c0=c0, t129=t129: e.activation(out=t129[:p, :], in_=Ib.h[:p, c0:c0 + 129], func=AF.Copy, scale=scal[:p, tt, d, 1, h:h + 1]),
                     reads=[Ib.k(), scal.k()], writes=[t129.k()])
                yield
                P.op("dve", lambda e, d=d, c0=c0, t129=t129, num=num: e.scalar_tensor_tensor(out=num[:p, :], in0=Eb.h[:p, c0:c0 + 129], scalar=scal[:p, tt, d, 2, h:h + 1], in1=t129[:p, :],
                                                                                            op0=ALU.mult, op1=ALU.add),
                     reads=[Eb.k(), scal.k(), t129.k()], writes=[num.k()])
                yield
                P.op("dve", lambda e, num=num, dn=dn: e.tensor_scalar(out=dn[:p, 0:1], in0=num[:p, 128:129], scalar1=-1.0, scalar2=None, op0=ALU.mult),
                     reads=[num.k()], writes=[dn.k()])
                yield
                P.op("dve", lambda e, num=num, dn=dn: e.tensor_tensor(out=dn[:p, 0:1], in0=dn[:p, 0:1], in1=num[:p, 128:129], op=ALU.max),
                     reads=[num.k(), dn.k()], writes=[dn.k()])
                yield
                P.op("dve", lambda e, d=d, dn=dn: e.tensor_scalar(out=dn[:p, 0:1], in0=dn[:p, 0:1], scalar1=scal[:p, tt, d, 3, h:h + 1], scalar2=None, op0=ALU.max),
                     reads=[dn.k(), scal.k()], writes=[dn.k()])
                yield
                P.op("dve", lambda e, dn=dn: e.reciprocal(out=dn[:p, 1:2], in_=dn[:p, 0:1]), reads=[dn.k()], writes=[dn.k("r")])
                yield
            hsum = L_.hsum
            P.op("dve", lambda e: e.tensor_scalar(out=hsum[:p, :], in0=L_.num[0][:p, 0:128], scalar1=L_.dn[0][:p, 1:2], scalar2=None, op0=ALU.mult),
                 reads=[L_.num[0].k(), L_.dn[0].k("r")], writes=[hsum.k()])
            yield
            P.op("dve", lambda e: e.scalar_tensor_tensor(out=hsum[:p, :], in0=L_.num[1][:p, 0:128], scalar=L_.dn[1][:p, 1:2], in1=hsum[:p, :], op0=ALU.mult, op1=ALU.add),
                 reads=[L_.num[1].k(), L_.dn[1].k("r"), hsum.k()], writes=[hsum.k()])
            yield
            yf, sg, y, o_ = L_.yf, L_.sg, L_.yb, L_.oT
            yield from layer_norm(lane, hsum, hsum[:p, :], p, mnorm[:p, h * 128:(h + 1) * 128], mnorm, yf, yf[:p, :])
            P.op("act", lambda e: e.activation(out=sg[:p, :], in_=motok[:p, tt, :], func=AF.Sigmoid), reads=[motok.k()], writes=[sg.k()])
            yield
            P.op("dve", lambda e: e.tensor_tensor(out=y[:p, :], in0=yf[:p, :], in1=sg[:p, :], op=ALU.mult), reads=[yf.k(), sg.k()], writes=[y.k()])
            yield
            P.op("pe", lambda e: e.transpose(out=bf(Sb)[:, 0:p], in_=y[:p, :], identity=ident[:p, :p]), reads=[y.k(), ident.k()], writes=[Sb.k()])
            yield
            P.op("act", lambda e: e.copy(out=ofull[:, s0:s0 + p], in_=bf(Sb)[:, 0:p]), reads=[Sb.k()], writes=[ofull.k(tt)])
            yield

        run_lanes(list(range(17)), ob_gen, W_M)
        P.dma("sp", lambda e, h=h, ofull=ofull: e.dma_start(out=mixd.h[(8 + h) * 128:(9 + h) * 128, :], in_=ofull[:, :]),
              reads=[ofull.k(n_) for n_ in range(17)], writes=[mixd.k(8 + h)])
    P.release(m0)


def consts():
    c = {}
    c["c_ident"] = np.eye(128, dtype=np.float32)
    n_tok = 2048
    rows = n_tok // 64
    row = np.concatenate([-np.ones(16, np.float32), np.repeat(np.arange(rows, dtype=np.float32), 64)])
    col = np.concatenate([np.arange(16, dtype=np.float32), np.tile(np.arange(64, dtype=np.float32), rows)])
    f_axis = (10000.0 ** (-np.arange(32, dtype=np.float32) / 32)).astype(np.float32)
    ang_row = row[:, None] * f_axis[None, :]
    ang_col = col[:, None] * f_axis[None, :]
    ang = np.concatenate([ang_row, ang_col], axis=1)
    rope = np.zeros((128, 17, 2, 64), np.float32)
    for tt, (s0, p) in enumerate(TT):
        rope[:p, tt, 0, :] = np.cos(ang[s0:s0 + p])
        rope[:p, tt, 1, :] = np.sin(ang[s0:s0 + p])
    c["c_rope_e"] = rope
    tau = np.arange(L)
    c["c_tau"] = np.ascontiguousarray(np.broadcast_to(np.stack([tau % 128, tau // 128]).astype(np.float32)[None], (128, 2, L)))
    fl = (10000.0 ** (-np.arange(64, dtype=np.float32) / 64)).astype(np.float32)
    angl = np.arange(L, dtype=np.float32)[:, None] * fl[None, :]
    ro = np.zeros((128, 17, 4, 64), np.float32)
    sc = np.float32(128.0 ** -0.5)
    for tt, (s0, p) in enumerate(TT):
        ro[:p, tt, 0, :] = np.cos(angl[s0:s0 + p]); ro[:p, tt, 1, :] = np.sin(angl[s0:s0 + p])
        ro[:p, tt, 2, :] = np.cos(angl[s0:s0 + p]) * sc; ro[:p, tt, 3, :] = np.sin(angl[s0:s0 + p]) * sc
    c["c_rope_o"] = ro
    s_ = np.arange(128, dtype=np.float32)[:, None]; q_ = np.arange(128, dtype=np.float32)[None, :]
    cr = np.zeros((128, 8, 128), np.float32)
    cr[:, 0] = np.maximum(q_ - s_, 0); cr[:, 1] = np.maximum(s_ - q_, 0)
    cr[:, 2] = (q_ >= s_); cr[:, 3] = (q_ < s_)
    cr[:, 4] = q_ + 1 + 0 * s_; cr[:, 5] = 128 - q_ + 0 * s_; cr[:, 6] = np.maximum(16 - q_, 0) + 0 * s_
    c["c_ret"] = cr
    cc = np.zeros((128, 4), np.float32)
    cc[:, 0] = 127 - s_[:, 0]; cc[:, 1] = np.maximum(15 - s_[:, 0], 0); cc[:, 2] = s_[:, 0]
    c["c_col"] = cc
    rows = np.zeros((8, 4, L), np.float32)
    st_f = np.zeros(L, bool); st_f[0] = True; st_f[16::128] = True
    st_b = np.zeros(L, bool); st_b[0:2048:128] = True; st_b[2048] = True
    rows[:, 0] = np.where(st_f, 0.0, 1.0); rows[:, 1] = np.where(st_f, -1e30, 0.0)
    rows[:, 2] = np.where(st_b, 0.0, 1.0); rows[:, 3] = np.where(st_b, -1e30, 0.0)
    c["c_rows"] = rows
    mm = np.zeros((128, 2, 128), np.float32)
    mm[:, 0] = (s_ <= q_); mm[:, 1] = (s_ >= q_)
    c["c_mmask"] = mm
    return c

def s5_layout(lam_re, lam_im, log_dt, b_re, b_im, c_re, c_im, d, glu_w, glu_b):
    o = {}
    def sp(a):
        return a.reshape(2, 16, 2, 64).transpose(2, 3, 0, 1).reshape(128, 32)
    dt = np.broadcast_to(log_dt.reshape(2, 16, 2, 1), (2, 16, 2, 64))
    dt = dt.transpose(2, 3, 0, 1).reshape(128, 32)
    o["s5p"] = np.ascontiguousarray(np.stack([sp(lam_re), sp(lam_im), dt], axis=1).astype(np.float32))
    def bp(a):
        return a.reshape(2, 16, 2, 64, 16).transpose(2, 3, 0, 1, 4).reshape(128, 32, 16)
    o["s5b"] = np.ascontiguousarray(np.stack([bp(b_re), bp(b_im)], axis=1).astype(np.float32))
    def cp(a):
        return a.reshape(2, 16, 2, 16, 64).transpose(2, 4, 0, 1, 3).reshape(128, 32, 16)
    o["s5c"] = np.ascontiguousarray(np.stack([cp(c_re), cp(c_im)], axis=1).astype(np.float32))
    o["s5d"] = np.ascontiguousarray(d.reshape(4, 128).T.astype(np.float32))
    o["s5gb"] = np.ascontiguousarray(glu_b.reshape(4, 128).T.astype(np.float32))
    o["s5gw"] = np.ascontiguousarray(glu_w.astype(np.float32))
    return o

def make_in_maps(inp, cores=range(8)):
    c = consts()
    shared = dict(c)
    shared["meta"] = np.ascontiguousarray(inp["meta_tokens"])
    shared["gains"] = np.ascontiguousarray(inp["norm_gains"].reshape(8, 2048))
    for i in range(2):
        shared["w1_%d" % i] = np.ascontiguousarray(inp["mlp_w1"][i])
        shared["w2_%d" % i] = np.ascontiguousarray(inp["mlp_w2"][i])
    shared["ewin"] = np.ascontiguousarray(inp["even_w_in"][0])
    shared["ewout"] = np.ascontiguousarray(inp["even_w_out"][0])
    shared["owin"] = np.ascontiguousarray(inp["odd_w_in"][0])
    shared["owout"] = np.ascontiguousarray(inp["odd_w_out"][0])
    shared["qkg"] = np.ascontiguousarray(np.stack([inp["att_q_norm"][0], inp["att_k_norm"][0]]))
    shared.update(s5_layout(inp["s5_lam_re"][0], inp["s5_lam_im"][0], inp["s5_log_dt"][0], inp["s5_b_re"][0], inp["s5_b_im"][0],
                            inp["s5_c_re"][0], inp["s5_c_im"][0], inp["s5_d"][0], inp["s5_glu_w"][0], inp["s5_glu_b"][0]))
    shared["ret_ld"] = np.ascontiguousarray(inp["ret_log_decay"][0].reshape(1, 16))
    shared["ret_norm"] = np.ascontiguousarray(inp["ret_norm"][0].reshape(1, 1024))
    shared["ml_gb"] = np.ascontiguousarray(inp["ml_gate_b"][0].T)
    shared["ml_cw"] = np.ascontiguousarray(inp["ml_conv_w"][0].reshape(5, 8, 128).transpose(2, 1, 0))
    shared["ml_cb"] = np.ascontiguousarray(inp["ml_conv_b"][0].reshape(8, 128).T)
    shared["ml_wq"] = np.ascontiguousarray(inp["ml_wq"][0]); shared["ml_wk"] = np.ascontiguousarray(inp["ml_wk"][0]); shared["ml_wv"] = np.ascontiguousarray(inp["ml_wv"][0])
    shared["ml_norm"] = np.ascontiguousarray(inp["ml_norm"][0].reshape(1, 1024))
    maps = []
    for b in cores:
        m = dict(shared)
        m["x"] = np.ascontiguousarray(inp["x"][b])
        maps.append(m)
    return maps


def kernel(**inputs):
    inp = {k: np.asarray(v) for k, v in inputs.items()}
    nc, st = build(stop_after=6)
    maps = make_in_maps(inp, cores=range(8))
    res = run_bass_kernel_spmd(nc, maps, core_ids=list(range(8)))
    outs = [np.asarray(r["out"]).astype(np.float32) for r in res.results]
    return np.stack(outs, axis=0)
```
